# Optimizing a Trainium2 kernel written in Bass

```python
import math
import jax, jax.numpy as jnp
from jax import lax
import numpy as np

D_MODEL = 1024
BATCH = 2
SEQ = 16384
DEPTH = 2

GRID_W = 64
CTX_LEN = 256
HEAD_DIM = 64
EPS = 1e-6
ROPE_BASE = 10000.0
ROPE_PAIRS = HEAD_DIM // 4
Q_BLOCK = 128
DA_HEADS = 4
DA_QK = 2 * HEAD_DIM
DA_V = 2 * HEAD_DIM
GLA_HEADS = 4
GLA_DK = 64
GLA_DV = 128
GLA_RANK = 16
GLA_TAU = 16.0
GLA_CHUNK = 64
NA_HEADS = 8
NA_DIM = 64
NA_KR = 8
NA_KC = 16
BR_W = 512
D_FF = 4 * D_MODEL
IN_SPLITS = (DA_HEADS * DA_QK, DA_HEADS * DA_QK, DA_HEADS * DA_V,
             GLA_HEADS * GLA_DK, GLA_HEADS * GLA_DK, GLA_HEADS * GLA_DV, GLA_HEADS * GLA_DV, 2 * GLA_RANK,
             NA_HEADS * NA_DIM, NA_HEADS * NA_DIM, NA_HEADS * NA_DIM,
             D_MODEL, D_MODEL, D_MODEL)
N_IN = sum(IN_SPLITS)

kernel_name = "hybrid_diffattn_gla_natten_prefix_block"


def rms_norm(x, g):
    xf = x.astype(jnp.float32)
    y = xf * lax.rsqrt(jnp.mean(xf * xf, axis=-1, keepdims=True) + EPS)
    return (y * g.astype(jnp.float32)).astype(x.dtype)


def to_heads(t, n_heads):
    b, n, w = t.shape
    return t.reshape(b, n, n_heads, w // n_heads).transpose(0, 2, 1, 3)


def from_heads(t):
    b, h, n, d = t.shape
    return t.transpose(0, 2, 1, 3).reshape(b, n, h * d)


def split_cols(z):
    idx = []
    acc = 0
    for s in IN_SPLITS[:-1]:
        acc += s
        idx.append(acc)
    return jnp.split(z, idx, axis=-1)


def rotate_pairs(x, ang):
    half = x.shape[-1] // 2
    x1, x2 = x[..., :half], x[..., half:]
    cos = jnp.cos(ang).astype(x.dtype)
    sin = jnp.sin(ang).astype(x.dtype)
    return jnp.concatenate([x1 * cos - x2 * sin, x1 * sin + x2 * cos], axis=-1)


def axial_rope(x, ang_row, ang_col):
    half = x.shape[-1] // 2
    return jnp.concatenate([rotate_pairs(x[..., :half], ang_row), rotate_pairs(x[..., half:], ang_col)], axis=-1)


def softmax_attend(q, k, v):
    s = jnp.einsum('bhqd,bhkd->bhqk', q, k).astype(jnp.float32)
    p = jax.nn.softmax(s, axis=-1)
    return jnp.einsum('bhqk,bhkd->bhqd', p.astype(v.dtype), v)


def da_heads(dq, dk, dv, qn_g, kn_g):
    scale = HEAD_DIM ** -0.5
    q = to_heads(dq, DA_HEADS)
    k = to_heads(dk, DA_HEADS)
    q1 = rms_norm(q[..., :HEAD_DIM], qn_g) * scale
    q2 = rms_norm(q[..., HEAD_DIM:], qn_g) * scale
    k1 = rms_norm(k[..., :HEAD_DIM], kn_g)
    k2 = rms_norm(k[..., HEAD_DIM:], kn_g)
    return q1, q2, k1, k2, to_heads(dv, DA_HEADS)


def diff_attention(q1, q2, k1, k2, v, lam, q_block):
    b, h, tq, d = q1.shape
    nb = tq // q_block

    def blocks(t):
        return t.reshape(b, h, nb, q_block, d).transpose(2, 0, 1, 3, 4)

    def one(args):
        a1, a2 = args
        p1 = jax.nn.softmax(jnp.einsum('bhqd,bhkd->bhqk', a1, k1).astype(jnp.float32), axis=-1)
        p2 = jax.nn.softmax(jnp.einsum('bhqd,bhkd->bhqk', a2, k2).astype(jnp.float32), axis=-1)
        return jnp.einsum('bhqk,bhkv->bhqv', (p1 - lam * p2).astype(v.dtype), v)

    o = lax.map(one, (blocks(q1), blocks(q2)))
    return o.transpose(1, 2, 0, 3, 4).reshape(b, h, tq, v.shape[-1])


def da_output(o, subln_g, lam_init):
    return from_heads(rms_norm(o, subln_g) * (1.0 - lam_init))


def gla_inputs(gq, gk, gv, ga, a2, ab):
    q = to_heads(gq, GLA_HEADS) * (GLA_DK ** -0.5)
    k = to_heads(gk, GLA_HEADS)
    v = to_heads(gv, GLA_HEADS)
    log_a = []
    for i in range(2):
        z = ga[..., i * GLA_RANK:(i + 1) * GLA_RANK] @ a2[i] + ab[i]
        log_a.append(to_heads(jax.nn.log_sigmoid(z.astype(jnp.float32)) / GLA_TAU, GLA_HEADS))
    return q, k, v, log_a[0], log_a[1]


def gla_chunked(q, k, v, log_a, s0):
    b, h, t, dk = q.shape
    dv = v.shape[-1]
    c = GLA_CHUNK
    n = t // c
    f32 = jnp.float32
    q = q.astype(f32).reshape(b, h, n, c, dk)
    k = k.astype(f32).reshape(b, h, n, c, dk)
    v = v.astype(f32).reshape(b, h, n, c, dv)
    cum = jnp.cumsum(log_a.reshape(b, h, n, c, dk), axis=3)
    cum_last = cum[:, :, :, -1:, :]
    qe = q * jnp.exp(cum)
    ke = k * jnp.exp(-cum)
    kd = k * jnp.exp(cum_last - cum)
    mask = jnp.tril(jnp.ones((c, c), dtype=bool))
    a_intra = jnp.where(mask, jnp.einsum('bhncd,bhnsd->bhncs', qe, ke), 0.0)
    o_intra = jnp.einsum('bhncs,bhnsv->bhncv', a_intra, v)
    d_state = jnp.einsum('bhncd,bhncv->nbhdv', kd, v)
    decay = jnp.exp(cum_last[:, :, :, 0, :]).transpose(2, 0, 1, 3)

    def step(s, inp):
        dec, ds = inp
        return dec[..., None] * s + ds, s

    s_final, s_prev = lax.scan(step, s0, (decay, d_state))
    o_inter = jnp.einsum('bhncd,nbhdv->bhncv', qe, s_prev)
    return (o_intra + o_inter).reshape(b, h, t, dv), s_final


def gla_bidirectional(lat, ctx):
    ql, kl, vl, lfl, lbl = lat
    qc, kc, vc, lfc, lbc = ctx
    b, h, _, dk = ql.shape
    s0 = jnp.zeros((b, h, dk, GLA_DV), jnp.float32)
    flip = lambda t: jnp.flip(t, axis=2)
    oc_f, sc_f = gla_chunked(qc, kc, vc, lfc, s0)
    oc_b, sc_b = gla_chunked(flip(qc), flip(kc), flip(vc), flip(lbc), s0)
    ol_f, _ = gla_chunked(ql, kl, vl, lfl, sc_f)
    ol_b, _ = gla_chunked(flip(ql), flip(kl), flip(vl), flip(lbl), sc_b)
    return ol_f + flip(ol_b), oc_f + flip(oc_b)


def gla_output(o, gg, gn_g):
    return from_heads(rms_norm(o, gn_g)).astype(gg.dtype) * jax.nn.silu(gg)


def na_heads(nq, nk, nv, qn_g, kn_g):
    q = rms_norm(to_heads(nq, NA_HEADS), qn_g) * (NA_DIM ** -0.5)
    k = rms_norm(to_heads(nk, NA_HEADS), kn_g)
    return q, k, to_heads(nv, NA_HEADS)


def neighborhood_attention(q, k, v, kc, vc, rpb):
    b, h, t, d = q.shape
    rows = t // GRID_W
    kr = min(NA_KR, rows)
    qg = q.reshape(b, h, rows, GRID_W, d)
    kg = k.reshape(b, h, rows, GRID_W, d)
    vg = v.reshape(b, h, rows, GRID_W, d)
    r_idx = jnp.arange(rows)
    row_start = jnp.clip(r_idx - kr // 2, 0, rows - kr)
    c_idx = jnp.arange(GRID_W)
    col_start = jnp.clip(c_idx - NA_KC // 2, 0, GRID_W - NA_KC)
    col_win = col_start[:, None] + jnp.arange(NA_KC)
    rel_col = col_win - c_idx[:, None] + (NA_KC - 1)
    rpb = rpb.astype(jnp.float32)
    n_loc = kr * NA_KC

    def row_fn(args):
        q_row, r, rs = args
        k_band = lax.dynamic_slice_in_dim(kg, rs, kr, axis=2)
        v_band = lax.dynamic_slice_in_dim(vg, rs, kr, axis=2)
        k_win = k_band[:, :, :, col_win, :]
        v_win = v_band[:, :, :, col_win, :]
        s_loc = jnp.einsum('bhwd,bhjwmd->bhwjm', q_row, k_win).astype(jnp.float32)
        rel_row = rs + jnp.arange(kr) - r + (NA_KR - 1)
        bias = rpb[:, rel_row[:, None, None], rel_col[None, :, :]]
        s_loc = s_loc + bias.transpose(0, 2, 1, 3)[None]
        s_ctx = jnp.einsum('bhwd,bhkd->bhwk', q_row, kc).astype(jnp.float32)
        s = jnp.concatenate([s_loc.reshape(b, h, GRID_W, n_loc), s_ctx], axis=-1)
        p = jax.nn.softmax(s, axis=-1).astype(v.dtype)
        p_loc = p[..., :n_loc].reshape(b, h, GRID_W, kr, NA_KC)
        return (jnp.einsum('bhwjm,bhjwmd->bhwd', p_loc, v_win)
                + jnp.einsum('bhwk,bhkd->bhwd', p[..., n_loc:], vc))

    o = lax.map(row_fn, (qg.transpose(2, 0, 1, 3, 4), r_idx, row_start))
    return o.transpose(1, 2, 0, 3, 4).reshape(b, h, t, d)


def branch_merge(y_da, y_gla, y_na, g_da, g_gla, g_na, w_br_da, w_br_gla, w_br_na, w_out):
    m = (jax.nn.sigmoid(g_da) * (y_da @ w_br_da)
         + jax.nn.sigmoid(g_gla) * (y_gla @ w_br_gla)
         + jax.nn.sigmoid(g_na) * (y_na @ w_br_na))
    return m @ w_out


def hybrid_mixer(h_lat, h_ctx, ang_row, ang_col, lam_init, need_ctx, w_in, da_qn_g, da_kn_g, da_lambda,
                 da_subln_g, gla_a2, gla_a_b, gla_gn_g, na_qn_g, na_kn_g, na_rpb, w_br_da, w_br_gla, w_br_na, w_out):
    zl = split_cols(h_lat @ w_in)
    zc = split_cols(h_ctx @ w_in)
    q1l, q2l, k1l, k2l, vl = da_heads(zl[0], zl[1], zl[2], da_qn_g, da_kn_g)
    q1l, q2l, k1l, k2l = [axial_rope(t, ang_row, ang_col) for t in (q1l, q2l, k1l, k2l)]
    q1c, q2c, k1c, k2c, vc = da_heads(zc[0], zc[1], zc[2], da_qn_g, da_kn_g)
    lp = da_lambda.astype(jnp.float32)
    lam = jnp.exp(jnp.sum(lp[0] * lp[1])) - jnp.exp(jnp.sum(lp[2] * lp[3])) + lam_init
    k1_all = jnp.concatenate([k1l, k1c], axis=2)
    k2_all = jnp.concatenate([k2l, k2c], axis=2)
    v_all = jnp.concatenate([vl, vc], axis=2)
    y_da_l = da_output(diff_attention(q1l, q2l, k1_all, k2_all, v_all, lam, Q_BLOCK), da_subln_g, lam_init)
    gl = gla_inputs(zl[3], zl[4], zl[5], zl[7], gla_a2, gla_a_b)
    gc = gla_inputs(zc[3], zc[4], zc[5], zc[7], gla_a2, gla_a_b)
    o_gla_l, o_gla_c = gla_bidirectional(gl, gc)
    y_gla_l = gla_output(o_gla_l, zl[6], gla_gn_g)
    nql, nkl, nvl = na_heads(zl[8], zl[9], zl[10], na_qn_g, na_kn_g)
    nqc, nkc, nvc = na_heads(zc[8], zc[9], zc[10], na_qn_g, na_kn_g)
    y_na_l = from_heads(neighborhood_attention(nql, nkl, nvl, nkc, nvc, na_rpb))
    out_l = branch_merge(y_da_l, y_gla_l, y_na_l, zl[11], zl[12], zl[13], w_br_da, w_br_gla, w_br_na, w_out)
    if not need_ctx:
        return out_l, None
    y_da_c = da_output(diff_attention(q1c, q2c, k1c, k2c, vc, lam, q1c.shape[2]), da_subln_g, lam_init)
    y_gla_c = gla_output(o_gla_c, zc[6], gla_gn_g)
    y_na_c = from_heads(softmax_attend(nqc, nkc, nvc))
    out_c = branch_merge(y_da_c, y_gla_c, y_na_c, zc[11], zc[12], zc[13], w_br_da, w_br_gla, w_br_na, w_out)
    return out_l, out_c


def sqrelu_mlp(h, w1, w2):
    a = jax.nn.relu(h @ w1)
    return (a * a) @ w2


def setup_inputs(seed: int = 0) -> dict:
    key = jax.random.key(seed)
    ks = jax.random.split(key, 32)
    f32 = jnp.float32
    nrm = lambda k, shape, s: jax.random.normal(k, shape, f32) * s
    d = D_MODEL
    return {
        'x': nrm(ks[0], (BATCH, SEQ, d), 1.0),
        'c': nrm(ks[1], (BATCH, d), 1.0),
        'ctx': nrm(ks[2], (BATCH, CTX_LEN, d), 1.0),
        'c_ctx': nrm(ks[3], (d,), 1.0),
        'w_mod': nrm(ks[4], (DEPTH, d, 6 * d), 0.5 * d ** -0.5),
        'b_mod': nrm(ks[5], (DEPTH, 6 * d), 0.02),
        'norm1_g': 1.0 + nrm(ks[6], (DEPTH, d), 0.02),
        'norm2_g': 1.0 + nrm(ks[7], (DEPTH, d), 0.02),
        'w_in': nrm(ks[8], (DEPTH, d, N_IN), d ** -0.5),
        'da_qn_g': 1.0 + nrm(ks[9], (DEPTH, HEAD_DIM), 0.02),
        'da_kn_g': 1.0 + nrm(ks[10], (DEPTH, HEAD_DIM), 0.02),
        'da_lambda': nrm(ks[11], (DEPTH, 4, HEAD_DIM), 0.1),
        'da_subln_g': 1.0 + nrm(ks[12], (DEPTH, DA_V), 0.02),
        'gla_a2': nrm(ks[13], (DEPTH, 2, GLA_RANK, GLA_HEADS * GLA_DK), GLA_RANK ** -0.5),
        'gla_a_b': nrm(ks[14], (DEPTH, 2, GLA_HEADS * GLA_DK), 0.1),
        'gla_gn_g': 1.0 + nrm(ks[15], (DEPTH, GLA_DV), 0.02),
        'na_qn_g': 1.0 + nrm(ks[16], (DEPTH, NA_DIM), 0.02),
        'na_kn_g': 1.0 + nrm(ks[17], (DEPTH, NA_DIM), 0.02),
        'na_rpb': nrm(ks[18], (DEPTH, NA_HEADS, 2 * NA_KR - 1, 2 * NA_KC - 1), 0.02),
        'w_br_da': nrm(ks[19], (DEPTH, BR_W, d), BR_W ** -0.5),
        'w_br_gla': nrm(ks[20], (DEPTH, BR_W, d), BR_W ** -0.5),
        'w_br_na': nrm(ks[21], (DEPTH, BR_W, d), BR_W ** -0.5),
        'w_out': nrm(ks[22], (DEPTH, d, d), d ** -0.5),
        'w_ff1': nrm(ks[23], (DEPTH, d, D_FF), d ** -0.5),
        'w_ff2': nrm(ks[24], (DEPTH, D_FF, d), D_FF ** -0.5),
    }


def reference(x, c, ctx, c_ctx, w_mod, b_mod, norm1_g, norm2_g, w_in, da_qn_g, da_kn_g, da_lambda, da_subln_g,
              gla_a2, gla_a_b, gla_gn_g, na_qn_g, na_kn_g, na_rpb, w_br_da, w_br_gla, w_br_na, w_out, w_ff1, w_ff2):
    t_len = x.shape[1]
    pos = jnp.arange(t_len)
    row = (pos // GRID_W).astype(jnp.float32)
    col = (pos % GRID_W).astype(jnp.float32)
    freqs = ROPE_BASE ** (-jnp.arange(ROPE_PAIRS, dtype=jnp.float32) / ROPE_PAIRS)
    ang_row = row[:, None] * freqs
    ang_col = col[:, None] * freqs
    for l in range(DEPTH):
        need_ctx = l < DEPTH - 1
        lam_init = 0.8 - 0.6 * math.exp(-0.3 * l)
        mod = jax.nn.silu(c) @ w_mod[l] + b_mod[l]
        mod_c = jax.nn.silu(c_ctx) @ w_mod[l] + b_mod[l]
        sh1, sc1, g1, sh2, sc2, g2 = [m[:, None, :] for m in jnp.split(mod, 6, axis=-1)]
        sh1c, sc1c, g1c, sh2c, sc2c, g2c = jnp.split(mod_c, 6, axis=-1)
        h_lat = rms_norm(x, norm1_g[l]) * (1.0 + sc1) + sh1
        h_ctx = rms_norm(ctx, norm1_g[l]) * (1.0 + sc1c) + sh1c
        m_lat, m_ctx = hybrid_mixer(h_lat, h_ctx, ang_row, ang_col, lam_init, need_ctx, w_in[l], da_qn_g[l],
                                    da_kn_g[l], da_lambda[l], da_subln_g[l], gla_a2[l], gla_a_b[l], gla_gn_g[l],
                                    na_qn_g[l], na_kn_g[l], na_rpb[l], w_br_da[l], w_br_gla[l], w_br_na[l], w_out[l])
        x = x + g1 * m_lat
        x = x + g2 * sqrelu_mlp(rms_norm(x, norm2_g[l]) * (1.0 + sc2) + sh2, w_ff1[l], w_ff2[l])
        if need_ctx:
            ctx = ctx + g1c * m_ctx
            ctx = ctx + g2c * sqrelu_mlp(rms_norm(ctx, norm2_g[l]) * (1.0 + sc2c) + sh2c, w_ff1[l], w_ff2[l])
    return x
```

```python
import contextlib
import math
import numpy as np
import ml_dtypes
import concourse.bass as bass
import concourse.mybir as mybir
from concourse.bass_utils import run_bass_kernel_spmd

F32 = mybir.dt.float32
BF16 = mybir.dt.bfloat16
AF = mybir.ActivationFunctionType
ALU = mybir.AluOpType
AX = mybir.AxisListType

NCORES = 8
D = 1024
KC = 8
SEQ = 16384
NLAT = 4096
NCTX = 256
NTOK = NLAT + NCTX
GRID_W = 64
EPS = 1e-6
N_IN = 7712
TT = [(i * 512, 512) for i in range(8)] + [(4096, 256)]


class Tile:
    def __init__(self, t, name):
        self.t = t
        self.name = name
        self.w = {}
        self.r = {}
        self.sem = None
        self.semval = 0
        self.track = True

    def __getitem__(self, k):
        return self.t[k]


def _merge(dst, src):
    for k, v in src.items():
        if dst.get(k, 0) < v:
            dst[k] = v


class Sched:
    def __init__(self, nc, es):
        self.nc = nc
        self.es = es
        self.eng = {"pe": nc.tensor, "act": nc.scalar, "dve": nc.vector, "pool": nc.gpsimd, "sp": nc.sync}
        self.sem = {}
        self.cnt = {}
        for k in ("pe", "act", "dve", "pool"):
            self.sem[k] = es.enter_context(nc.semaphore("s_" + k))
            self.cnt[k] = 0
        self.seen = {k: {} for k in self.eng}
        self.pending = {}
        self.nsem = 0
        self.pools = {}

    def sbuf(self, name, shape, dtype):
        self.uid = getattr(self, "uid", 0) + 1
        return Tile(self.es.enter_context(self.nc.sbuf_tensor("sb%d_%s" % (self.uid, name), list(shape), dtype)),
                    "%s_%d" % (name, self.uid))

    def psum(self, name, shape, dtype=F32):
        self.uid = getattr(self, "uid", 0) + 1
        return Tile(self.es.enter_context(self.nc.psum_tensor("pp%d_%s" % (self.uid, name), list(shape), dtype)),
                    "%s_%d" % (name, self.uid))

    def dram(self, name, shape, dtype, kind):
        t = self.nc.dram_tensor(name, list(shape), dtype, kind=kind)
        tl = Tile(t.ap(), name)
        tl.track = False
        return tl

    def pool(self, tag, n, shape, dtype, space="sbuf"):
        mk = self.sbuf if space == "sbuf" else self.psum
        self.pools[tag] = [[mk("%s%d" % (tag, i), shape, dtype) for i in range(n)], 0]

    def get(self, tag):
        p = self.pools[tag]
        t = p[0][p[1] % len(p[0])]
        p[1] += 1
        return t

    def _wait(self, en, deps):
        seen = self.seen[en]
        for k, v in deps.items():
            if k == "pe" and en == "pe":
                continue
            if seen.get(k, 0) >= v:
                continue
            self.eng[en].wait_ge(self.sem[k], v)
            seen[k] = v

    def op(self, en, fn, reads=(), writes=()):
        deps = {}
        for t in reads:
            _merge(deps, t.w)
        for t in writes:
            _merge(deps, t.w)
            _merge(deps, t.r)
        self._wait(en, deps)
        ins = fn(self.eng[en])
        self.cnt[en] += 1
        ev = {en: self.cnt[en]}
        ins.then_inc(self.sem[en], 1)
        for t in reads:
            _merge(t.r, ev)
        for t in writes:
            t.w = dict(ev)
            t.r = {}
        return ins

    def dma(self, dst, dst_ap, src, src_ap, owner):
        if owner.sem is None:
            self.nsem += 1
            owner.sem = "d%d_%s" % (self.nsem, owner.name)
            self.sem[owner.sem] = self.es.enter_context(self.nc.semaphore(owner.sem))
        deps = {}
        if src.track:
            _merge(deps, src.w)
        if dst.track:
            _merge(deps, dst.w)
            _merge(deps, dst.r)
        self._wait("sp", deps)
        owner.semval += 16
        ev = {owner.sem: owner.semval}
        self.nc.sync.dma_start(out=dst_ap, in_=src_ap).then_inc(self.sem[owner.sem], 16)
        if src.track:
            _merge(src.r, ev)
        if dst.track:
            dst.w = dict(ev)
            dst.r = {}
        _merge(self.pending, ev)

    def finish(self):
        self._wait("sp", self.pending)


def load(S, dst, dst_ap, src, src_ap):
    S.dma(dst, dst_ap, src, src_ap, owner=dst)


def store(S, dst, dst_ap, src, src_ap):
    S.dma(dst, dst_ap, src, src_ap, owner=src)


C_DAQ, C_DAK, C_DAV = 0, 512, 1024
C_GQ, C_GK, C_GV, C_GG, C_GA = 1536, 1792, 2048, 2560, 3072
C_NQ, C_NK, C_NV = 3104, 3616, 4128
C_GATE = 4640

LA_OUT = [("qT_da", [512, NTOK], BF16), ("kT_da", [512, NTOK], BF16), ("v_da", [NTOK, 512], BF16),
          ("qT_gl", [256, NTOK], BF16), ("kT_gl", [256, NTOK], BF16), ("k_gl", [NTOK, 256], BF16),
          ("v_gl", [NTOK, 512], BF16), ("ggT", [512, NTOK], BF16), ("la", [NTOK, 512], F32),
          ("qT_na", [512, NTOK], BF16), ("kT_na", [512, NTOK], BF16), ("v_na", [NTOK, 512], BF16),
          ("gates", [3072, NTOK], BF16), ("modT", [128, 96], F32)]


def build_la():
    nc = bass.Bass("TRN2", target_bir_lowering=False)
    es = contextlib.ExitStack()
    with es:
        S = Sched(nc, es)
        di = lambda n, s, d=F32: S.dram(n, s, d, "ExternalInput")
        xT = di("xT", [D, NTOK])
        cT = di("cT", [128, KC * 2])
        w_mod = di("w_mod", [D, 6 * D])
        b_modT = di("b_modT", [128, 48])
        n1g = di("n1g", [128, KC])
        w_in = di("w_in", [D, N_IN])
        gcols = di("gcols", [128, 4])
        ropeC = di("ropeC", [128, NTOK])
        ropeS = di("ropeS", [128, NTOK])
        cmats = di("cmats", [128, 3 * 128])
        a2aug = di("a2aug", [33, 512])
        outs = {n: S.dram(n, s, d, "ExternalOutput") for n, s, d in LA_OUT}

        hT = S.sbuf("hT", [128, KC, NTOK], BF16)
        S.pool("stage", 2, [128, KC, 512], F32)
        S.pool("wb", 2, [128, KC, 512], BF16)
        S.pool("sq8", 1, [128, KC, 512], BF16)
        S.pool("rope", 4, [128, 512], F32)
        S.pool("f32a", 4, [128, 512], F32)
        S.pool("f32b", 4, [128, 512], F32)
        S.pool("bfa", 3, [128, 512], BF16)
        S.pool("bfo", 4, [128, 512], BF16)
        S.pool("ps", 7, [128, 512], F32, space="psum")
        cm32 = S.sbuf("cm32", [128, 384], F32)
        cmb = S.sbuf("cmb", [128, 384], BF16)
        cTs = S.sbuf("cTs", [128, KC * 2], F32)
        sil = S.sbuf("sil", [128, KC * 2], F32)
        bmod = S.sbuf("bmod", [128, 48], F32)
        n1gs = S.sbuf("n1gs", [128, KC], F32)
        gcs = S.sbuf("gcs", [128, 4], F32)
        modT = S.sbuf("modT", [128, 96], F32)
        A1 = S.sbuf("A1", [128, KC * 2], F32)
        a2s = S.sbuf("a2s", [33, 512], F32)
        gaT = S.sbuf("gaT", [33, 512], F32)
        ps_small = S.psum("ps_small", [128, 2], F32)
        cst = S.sbuf("cst", [128, 2], F32)
        S.op("pool", lambda e: e.memset(cst[:, 0:1], EPS), [], [cst])
        S.op("pool", lambda e: e.memset(cst[:, 1:2], 1.0), [], [cst])

        def rstd_from(ps, n, pool_tag):
            sq_ = S.get(pool_tag)
            S.op("act", lambda e: e.activation(out=sq_[:, 0:n], in_=ps[:, 0:n], func=AF.Sqrt, bias=cst[:, 0:1], scale=1.0),
                 [ps, cst], [sq_])
            r_ = S.get(pool_tag)
            S.op("dve", lambda e: e.reciprocal(out=r_[:, 0:n], in_=sq_[:, 0:n]), [sq_], [r_])
            return r_

        load(S, cm32, cm32[:], cmats, cmats[:, :])
        load(S, cTs, cTs[:], cT, cT[:, :])
        load(S, bmod, bmod[:], b_modT, b_modT[:, :])
        load(S, n1gs, n1gs[:], n1g, n1g[:, :])
        load(S, gcs, gcs[:], gcols, gcols[:, :])
        load(S, a2s, a2s[:], a2aug, a2aug[:, :])
        S.op("dve", lambda e: e.tensor_copy(out=cmb[:], in_=cm32[:]), [cm32], [cmb])
        Pm = cmb[:, 0:128]
        Bones = cmb[:, 128:256]
        Ones = cmb[:, 256:384]
        S.op("act", lambda e: e.activation(out=sil[:], in_=cTs[:], func=AF.Silu), [cTs], [sil])
        S.op("pool", lambda e: e.memset(gaT[32:33, :], 1.0), [], [gaT])

        wm_v = w_mod.t.rearrange("(kc p) f -> p kc f", p=128)
        for sb in range(12):
            st = S.get("stage")
            load(S, st, st[:], w_mod, wm_v[:, :, sb * 512:(sb + 1) * 512])
            for j in range(4):
                fo = sb * 4 + j
                for kc in range(KC):
                    S.op("pe", lambda e, kc=kc, j=j, st=st: e.matmul(
                        ps_small[:], st[:, kc, j * 128:(j + 1) * 128], sil[:, kc * 2:kc * 2 + 2],
                        start=(kc == 0), stop=(kc == KC - 1)), [st, sil], [ps_small])
                S.op("dve", lambda e, fo=fo: e.tensor_scalar(
                    out=modT[:, fo * 2:fo * 2 + 2], in0=ps_small[:], scalar1=bmod[:, fo:fo + 1], scalar2=None,
                    op0=ALU.add), [ps_small, bmod], [modT])
        store(S, outs["modT"], outs["modT"][:, :], modT, modT[:])
        for kc in range(KC):
            S.op("dve", lambda e, kc=kc: e.tensor_scalar(
                out=A1[:, kc * 2:kc * 2 + 2], in0=modT[:, (8 + kc) * 2:(8 + kc) * 2 + 2], scalar1=1.0,
                scalar2=n1gs[:, kc:kc + 1], op0=ALU.add, op1=ALU.mult), [modT, n1gs], [A1])

        xv = xT.t.rearrange("(kc p) t -> p kc t", p=128)
        for ti, (t0, n) in enumerate(TT):
            col = 0 if ti < 8 else 1
            st = S.get("stage")
            load(S, st, st[:, :, 0:n], xT, xv[:, :, t0:t0 + n])
            sq = S.get("sq8")
            S.op("act", lambda e: e.activation(out=sq[:, :, 0:n], in_=st[:, :, 0:n], func=AF.Square), [st], [sq])
            ps = S.get("ps")
            for kc in range(KC):
                S.op("pe", lambda e, kc=kc: e.matmul(ps[:, 0:n], Ones, sq[:, kc, 0:n], start=(kc == 0),
                                                      stop=(kc == KC - 1)), [cmb, sq], [ps])
            rstd = rstd_from(ps, n, "f32a")
            for kc in range(KC):
                tmp = S.get("f32b")
                S.op("dve", lambda e, kc=kc: e.scalar_tensor_tensor(
                    out=tmp[:, 0:n], in0=st[:, kc, 0:n], scalar=A1[:, kc * 2 + col:kc * 2 + col + 1],
                    in1=rstd[:, 0:n], op0=ALU.mult, op1=ALU.mult), [st, A1, rstd], [tmp])
                S.op("act", lambda e, kc=kc: e.activation(
                    out=hT[:, kc, t0:t0 + n], in_=tmp[:, 0:n], func=AF.Identity,
                    bias=modT[:, kc * 2 + col:kc * 2 + col + 1], scale=1.0), [tmp, modT], [hT])

        wv = w_in.t.rearrange("(kc p) c -> p kc c", p=128)

        def load_w(c0, ncol):
            st = S.get("stage")
            load(S, st, st[:, :, 0:ncol], w_in, wv[:, :, c0:c0 + ncol])
            wb = S.get("wb")
            for kc in range(KC):
                en = "pool" if kc % 2 == 0 else "dve"
                S.op(en, lambda e, kc=kc: e.tensor_copy(out=wb[:, kc, 0:ncol], in_=st[:, kc, 0:ncol]), [st], [wb])
            return wb

        def proj_fm(wb, j, t0, n, m=128):
            ps = S.get("ps")
            for kc in range(KC):
                S.op("pe", lambda e, kc=kc: e.matmul(ps[0:m, 0:n], wb[:, kc, j * 128:j * 128 + m], hT[:, kc, t0:t0 + n],
                                                      start=(kc == 0), stop=(kc == KC - 1)), [wb, hT], [ps])
            return ps

        def headnorm(ps, n, gi):
            zs = S.get("f32a")
            S.op("act", lambda e: e.activation(out=zs[:, 0:n], in_=ps[:, 0:n], func=AF.Copy), [ps], [zs])
            sq = S.get("bfa")
            S.op("act", lambda e: e.activation(out=sq[:, 0:n], in_=ps[:, 0:n], func=AF.Square), [ps], [sq])
            ps2 = S.get("ps")
            S.op("pe", lambda e: e.matmul(ps2[:, 0:n], Bones, sq[:, 0:n], start=True, stop=True), [cmb, sq], [ps2])
            rstd = rstd_from(ps2, n, "f32b")
            qh = S.get("bfa")
            S.op("dve", lambda e: e.scalar_tensor_tensor(out=qh[:, 0:n], in0=zs[:, 0:n], scalar=gcs[:, gi:gi + 1],
                                                         in1=rstd[:, 0:n], op0=ALU.mult, op1=ALU.mult),
                 [zs, gcs, rstd], [qh])
            return qh

        def fm_store(name, j, t0, n, src, m=128):
            o = outs[name]
            store(S, o, o[j * 128:j * 128 + m, t0:t0 + n], src, src[0:m, 0:n])

        def qk_block(c0, name, gi, rope):
            wb = load_w(c0, 512)
            for j in range(4):
                for (t0, n) in TT:
                    ps = proj_fm(wb, j, t0, n)
                    qh = headnorm(ps, n, gi)
                    if rope:
                        rc = S.get("rope")
                        load(S, rc, rc[:, 0:n], ropeC, ropeC[:, t0:t0 + n])
                        rs = S.get("rope")
                        load(S, rs, rs[:, 0:n], ropeS, ropeS[:, t0:t0 + n])
                        ps3 = S.get("ps")
                        S.op("pe", lambda e: e.matmul(ps3[:, 0:n], Pm, qh[:, 0:n], start=True, stop=True), [cmb, qh], [ps3])
                        t1 = S.get("f32a")
                        S.op("pool", lambda e: e.tensor_tensor(out=t1[:, 0:n], in0=qh[:, 0:n], in1=rc[:, 0:n], op=ALU.mult),
                             [qh, rc], [t1])
                        t2 = S.get("f32b")
                        S.op("dve", lambda e: e.tensor_tensor(out=t2[:, 0:n], in0=ps3[:, 0:n], in1=rs[:, 0:n], op=ALU.mult),
                             [ps3, rs], [t2])
                        ob = S.get("bfo")
                        S.op("pool", lambda e: e.tensor_tensor(out=ob[:, 0:n], in0=t1[:, 0:n], in1=t2[:, 0:n], op=ALU.add),
                             [t1, t2], [ob])
                    else:
                        ob = qh
                    fm_store(name, j, t0, n, ob)

        def act_block(c0, ncol, name, func, scale=1.0, jbase=0):
            wb = load_w(c0, ncol)
            for j in range(ncol // 128):
                for (t0, n) in TT:
                    ps = proj_fm(wb, j, t0, n)
                    ob = S.get("bfo")
                    S.op("act", lambda e: e.activation(out=ob[:, 0:n], in_=ps[:, 0:n], func=func, scale=scale), [ps], [ob])
                    fm_store(name, jbase + j, t0, n, ob)
            return wb

        def tm_block(wb, cofs, ncol, name):
            o = outs[name]
            for s in range(NTOK // 128):
                ps = S.get("ps")
                for kc in range(KC):
                    S.op("pe", lambda e, kc=kc: e.matmul(ps[:, 0:ncol], hT[:, kc, s * 128:(s + 1) * 128],
                                                          wb[:, kc, cofs:cofs + ncol], start=(kc == 0), stop=(kc == KC - 1)),
                         [wb, hT], [ps])
                ob = S.get("bfo")
                if s % 2 == 0:
                    S.op("act", lambda e: e.activation(out=ob[:, 0:ncol], in_=ps[:, 0:ncol], func=AF.Copy), [ps], [ob])
                else:
                    S.op("dve", lambda e: e.tensor_copy(out=ob[:, 0:ncol], in_=ps[:, 0:ncol]), [ps], [ob])
                store(S, o, o[s * 128:(s + 1) * 128, 0:ncol], ob, ob[:, 0:ncol])

        qk_block(C_DAQ, "qT_da", 0, True)
        qk_block(C_DAK, "kT_da", 1, True)
        tm_block(load_w(C_DAV, 512), 0, 512, "v_da")
        wb = load_w(C_GQ, 512)
        for j in range(4):
            for (t0, n) in TT:
                ps = proj_fm(wb, j, t0, n)
                ob = S.get("bfo")
                S.op("act", lambda e: e.activation(out=ob[:, 0:n], in_=ps[:, 0:n], func=AF.Copy,
                                                   scale=(0.125 if j < 2 else 1.0)), [ps], [ob])
                fm_store("qT_gl" if j < 2 else "kT_gl", j % 2, t0, n, ob)
        tm_block(wb, 256, 256, "k_gl")
        tm_block(load_w(C_GV, 512), 0, 512, "v_gl")
        act_block(C_GG, 512, "ggT", AF.Silu)
        wb = load_w(C_GA, 32)
        lao = outs["la"]
        for (t0, n) in TT:
            ps = proj_fm(wb, 0, t0, n, m=32)
            S.op("act", lambda e: e.activation(out=gaT[0:32, 0:n], in_=ps[0:32, 0:n], func=AF.Copy), [ps], [gaT])
            for s in range(n // 128):
                ps2 = S.get("ps")
                S.op("pe", lambda e, s=s: e.matmul(ps2[:, :], gaT[0:33, s * 128:(s + 1) * 128], a2s[0:33, :],
                                                    start=True, stop=True), [gaT, a2s], [ps2])
                ex = S.get("f32a")
                S.op("act", lambda e: e.activation(out=ex[:, :], in_=ps2[:, :], func=AF.Exp, scale=-1.0), [ps2], [ex])
                ln = S.get("f32b")
                S.op("act", lambda e: e.activation(out=ln[:, :], in_=ex[:, :], func=AF.Ln, bias=cst[:, 1:2], scale=1.0), [ex, cst], [ln])
                lo = S.get("f32a")
                S.op("dve", lambda e: e.tensor_scalar(out=lo[:, :], in0=ln[:, :], scalar1=-1.0 / 16.0, scalar2=None,
                                                      op0=ALU.mult), [ln], [lo])
                store(S, lao, lao[t0 + s * 128:t0 + (s + 1) * 128, :], lo, lo[:, :])
        qk_block(C_NQ, "qT_na", 2, False)
        qk_block(C_NK, "kT_na", 3, False)
        tm_block(load_w(C_NV, 512), 0, 512, "v_na")
        for g in range(6):
            act_block(C_GATE + g * 512, 512, "gates", AF.Sigmoid, jbase=g * 4)
        S.finish()
    return nc


def rope_tables():
    pos = np.arange(NLAT)
    return pos


def la_consts():
    Pm = np.zeros((128, 128), np.float32)
    for m in range(128):
        w = (m % 64) % 32
        if w < 16:
            Pm[m + 16, m] = -1.0
        else:
            Pm[m - 16, m] = 1.0
    Bones = np.zeros((128, 128), np.float32)
    Bones[:64, :64] = 1.0 / 64
    Bones[64:, 64:] = 1.0 / 64
    Ones = np.full((128, 128), 1.0 / 1024, np.float32)
    return np.concatenate([Pm, Bones, Ones], axis=1)


def rope_cs(qtr):
    tpos = qtr * NLAT + np.arange(NLAT)
    row = (tpos // GRID_W).astype(np.float32)
    colp = (tpos % GRID_W).astype(np.float32)
    freqs = (10000.0 ** (-np.arange(16, dtype=np.float32) / 16)).astype(np.float32)
    C = np.ones((128, NTOK), np.float32)
    Sn = np.zeros((128, NTOK), np.float32)
    for p in range(128):
        u = p % 64
        i = (u % 32) % 16
        ang = (row if u < 32 else colp) * freqs[i]
        C[p, :NLAT] = np.cos(ang.astype(np.float32))
        Sn[p, :NLAT] = np.sin(ang.astype(np.float32))
    return C, Sn


def chunkT(v):
    v = np.asarray(v, np.float32)
    if v.ndim == 1:
        return np.ascontiguousarray(v.reshape(-1, 128).T)
    return np.ascontiguousarray(v.reshape(v.shape[0], -1, 128).transpose(2, 1, 0).reshape(128, -1))


def la_inputs(l, core, xT_core, inp):
    b, qtr = core // 4, core % 4
    cc = np.stack([inp["c"][b], inp["c_ctx"]], 0)
    C, Sn = rope_cs(qtr)
    gc = np.stack([np.tile(inp["da_qn_g"][l], 2) * 0.125, np.tile(inp["da_kn_g"][l], 2),
                   np.tile(inp["na_qn_g"][l], 2) * 0.125, np.tile(inp["na_kn_g"][l], 2)], 1).astype(np.float32)
    a2aug = np.zeros((33, 512), np.float32)
    a2aug[0:16, 0:256] = inp["gla_a2"][l, 0]
    a2aug[16:32, 256:512] = inp["gla_a2"][l, 1]
    a2aug[32, 0:256] = inp["gla_a_b"][l, 0]
    a2aug[32, 256:512] = inp["gla_a_b"][l, 1]
    return {"xT": xT_core, "cT": chunkT(cc), "w_mod": inp["w_mod"][l], "b_modT": chunkT(inp["b_mod"][l]),
            "n1g": chunkT(inp["norm1_g"][l]), "w_in": inp["w_in"][l], "gcols": gc, "ropeC": C, "ropeS": Sn,
            "cmats": la_consts(), "a2aug": a2aug}


_NC = {}


def run_la(l, xTs, inp):
    if "la" not in _NC:
        _NC["la"] = build_la()
    in_maps = [la_inputs(l, c, xTs[c], inp) for c in range(NCORES)]
    res = run_bass_kernel_spmd(_NC["la"], in_maps, core_ids=list(range(NCORES)))
    return res.results


def make_xT(x, ctx):
    xs = []
    for c in range(NCORES):
        b, qtr = c // 4, c % 4
        xs.append(np.ascontiguousarray(
            np.concatenate([x[b, qtr * NLAT:(qtr + 1) * NLAT], ctx[b]], 0).T.astype(np.float32)))
    return xs


NKT = 130
NPF = 96
NHALO = 38 * 128
NA_NTOK = NHALO + NCTX


def barrier(S):
    allv = dict(S.pending)
    for k in ("pe", "act", "dve", "pool"):
        if S.cnt[k]:
            allv[k] = S.cnt[k]
    for en in ("pe", "act", "dve", "pool", "sp"):
        d = {k: v for k, v in allv.items() if k != en or en == "sp"}
        seen = S.seen[en]
        for k, v in d.items():
            if seen.get(k, 0) < v:
                S.eng[en].wait_ge(S.sem[k], v)
                seen[k] = v


def build_lb():
    nc = bass.Bass("TRN2", target_bir_lowering=False)
    es = contextlib.ExitStack()
    with es:
        S = Sched(nc, es)
        di = lambda n, s, d=F32: S.dram(n, s, d, "ExternalInput")
        xT = di("xT", [D, NTOK])
        modT_d = di("modT", [128, 96])
        n2g = di("n2g", [128, KC])
        qT_da = di("qT_da", [512, NTOK], BF16)
        kT_da = di("kT_da", [512, NKT * 128], BF16)
        v_da = di("v_da", [4, 128, NKT * 128], BF16)
        lamtab = di("lamtab", [128, 256])
        lconst = di("lconst", [128, 2])
        gvec = di("gvec", [128, 2])
        qT_gl = di("qT_gl", [256, NTOK], BF16)
        kT_gl = di("kT_gl", [256, NTOK], BF16)
        k_gl = di("k_gl", [NTOK, 256], BF16)
        v_gl = di("v_gl", [NTOK, 512], BF16)
        la_d = di("la", [NTOK, 512])
        ggT = di("ggT", [512, NTOK], BF16)
        pf_la = di("pf_la", [2, NPF * 128, 256])
        pf_k = di("pf_k", [2, NPF * 128, 256], BF16)
        pf_v = di("pf_v", [2, NPF * 128, 512], BF16)
        trim = di("trim", [128, 6 * 128])
        qT_na = di("qT_na", [512, NTOK], BF16)
        kT_na = di("kT_na", [512, NA_NTOK], BF16)
        v_na = di("v_na", [NA_NTOK, 512], BF16)
        nbias = di("nbias", [128, 8 * 7 * 128])
        nmask = di("nmask", [128, 5 * 7 * 128])
        gates = di("gates", [3072, NTOK], BF16)
        w_br = [di("w_br%d" % i, [512, D]) for i in range(3)]
        w_out = di("w_out", [D, D])
        w_ff1 = di("w_ff1", [D, 4 * D])
        w_ff2 = di("w_ff2", [4 * D, D])
        xo = S.dram("xo", [D, NTOK], F32, "ExternalOutput")
        ys = [S.dram("y%d" % i, [512, NTOK], BF16, "Internal") for i in range(3)]
        wbr_b = [S.dram("wbrb%d" % i, [128, 4 * D], BF16, "Internal") for i in range(3)]
        wout_b = S.dram("woutb", [128, KC * D], BF16, "Internal")
        w1_b = S.dram("w1b", [8, 128, KC * 512], BF16, "Internal")
        w2_b = S.dram("w2b", [8, 128, 32 * 128], BF16, "Internal")

        cst = S.sbuf("cst", [128, 2], F32)
        S.op("pool", lambda e: e.memset(cst[:, 0:1], EPS), [], [cst])
        S.op("pool", lambda e: e.memset(cst[:, 1:2], 1.0), [], [cst])
        onesb = S.sbuf("onesb", [128, 128], BF16)
        S.op("pool", lambda e: e.memset(onesb[:], 1.0), [], [onesb])
        o128 = S.sbuf("o128", [128, 128], BF16)
        S.op("pool", lambda e: e.memset(o128[:], 1.0 / 128), [], [o128])
        o1024 = S.sbuf("o1024", [128, 128], BF16)
        S.op("pool", lambda e: e.memset(o1024[:], 1.0 / 1024), [], [o1024])
        modT = S.sbuf("modT", [128, 96], F32)
        load(S, modT, modT[:], modT_d, modT_d[:, :])
        n2gs = S.sbuf("n2gs", [128, KC], F32)
        load(S, n2gs, n2gs[:], n2g, n2g[:, :])
        lt = S.sbuf("lt", [128, 256], F32)
        load(S, lt, lt[:], lamtab, lamtab[:, :])
        lcs = S.sbuf("lcs", [128, 2], F32)
        load(S, lcs, lcs[:], lconst, lconst[:, :])
        gv = S.sbuf("gv", [128, 2], F32)
        load(S, gv, gv[:], gvec, gvec[:, :])
        A2 = S.sbuf("A2", [128, KC * 2], F32)
        for kc in range(KC):
            S.op("dve", lambda e, kc=kc: e.tensor_scalar(
                out=A2[:, kc * 2:kc * 2 + 2], in0=modT[:, (32 + kc) * 2:(32 + kc) * 2 + 2], scalar1=1.0,
                scalar2=n2gs[:, kc:kc + 1], op0=ALU.add, op1=ALU.mult), [modT, n2gs], [A2])
        lw = S.sbuf("lw", [128, 128], F32)
        lsum = S.sbuf("lsum", [128, 8], F32)
        S.op("dve", lambda e: e.tensor_tensor(out=lw[:, 0:64], in0=lt[:, 0:64], in1=lt[:, 64:128], op=ALU.mult), [lt], [lw])
        S.op("dve", lambda e: e.tensor_tensor(out=lw[:, 64:128], in0=lt[:, 128:192], in1=lt[:, 192:256], op=ALU.mult), [lt, lw], [lw])
        S.op("dve", lambda e: e.reduce_sum(out=lsum[:, 0:1], in_=lw[:, 0:64], axis=AX.X), [lw], [lsum])
        S.op("dve", lambda e: e.reduce_sum(out=lsum[:, 1:2], in_=lw[:, 64:128], axis=AX.X), [lw, lsum], [lsum])
        S.op("act", lambda e: e.activation(out=lsum[:, 2:4], in_=lsum[:, 0:2], func=AF.Exp), [lsum], [lsum])
        S.op("dve", lambda e: e.tensor_tensor(out=lsum[:, 4:5], in0=lsum[:, 3:4], in1=lsum[:, 2:3], op=ALU.subtract), [lsum], [lsum])
        S.op("dve", lambda e: e.tensor_tensor(out=lsum[:, 5:6], in0=lsum[:, 4:5], in1=lcs[:, 0:1], op=ALU.subtract), [lsum, lcs], [lsum])
        S.op("dve", lambda e: e.tensor_tensor(out=lsum[:, 6:7], in0=gv[:, 0:1], in1=lcs[:, 1:2], op=ALU.mult), [lsum, gv, lcs], [lsum])
        neglam = lsum[:, 5:6]
        gsub = lsum[:, 6:7]

        def rstd_from(ps, n, pool_tag):
            sq_ = S.get(pool_tag)
            S.op("act", lambda e: e.activation(out=sq_[:, 0:n], in_=ps[:, 0:n], func=AF.Sqrt, bias=cst[:, 0:1], scale=1.0),
                 [ps, cst], [sq_])
            r_ = S.get(pool_tag)
            S.op("dve", lambda e: e.reciprocal(out=r_[:, 0:n], in_=sq_[:, 0:n]), [sq_], [r_])
            return r_

        with contextlib.ExitStack() as es2:
            S.es = es2
            S.pool("stage", 2, [128, KC, 512], F32)
            S.pool("wb", 2, [128, KC, 512], BF16)

            def cast_block(src_t, src_ap, nk, dsts):
                st = S.get("stage")
                load(S, st, st[:, 0:nk, :], src_t, src_ap)
                wb = S.get("wb")
                for kc in range(nk):
                    en = "pool" if kc % 2 == 0 else "dve"
                    S.op(en, lambda e, kc=kc: e.tensor_copy(out=wb[:, kc, :], in_=st[:, kc, :]), [st], [wb])
                for (dt_, dap, sap) in dsts(wb):
                    store(S, dt_, dap, wb, sap)

            for i in range(3):
                v = w_br[i].t.rearrange("(kc p) c -> p kc c", p=128)
                dv = wbr_b[i].t.rearrange("p (kc c) -> p kc c", kc=4)
                for hb in range(2):
                    cast_block(w_br[i], v[:, :, hb * 512:(hb + 1) * 512], 4,
                               lambda wb, dv=dv, hb=hb, i=i: [(wbr_b[i], dv[:, :, hb * 512:(hb + 1) * 512], wb[:, 0:4, :])])
            v = w_out.t.rearrange("(kc p) c -> p kc c", p=128)
            dv = wout_b.t.rearrange("p (kc c) -> p kc c", kc=KC)
            for hb in range(2):
                cast_block(w_out, v[:, :, hb * 512:(hb + 1) * 512], KC,
                           lambda wb, dv=dv, hb=hb: [(wout_b, dv[:, :, hb * 512:(hb + 1) * 512], wb[:, :, :])])
            v = w_ff1.t.rearrange("(kc p) c -> p kc c", p=128)
            for fb in range(8):
                dv = w1_b.t[fb].rearrange("p (kc c) -> p kc c", kc=KC)
                cast_block(w_ff1, v[:, :, fb * 512:(fb + 1) * 512], KC,
                           lambda wb, dv=dv: [(w1_b, dv, wb[:, :, :])])
            v = w_ff2.t.rearrange("(kc p) c -> p kc c", p=128)
            for kg in range(4):
                for hb in range(2):
                    def dsts(wb, kg=kg, hb=hb):
                        r = []
                        for j in range(4):
                            fo = hb * 4 + j
                            dv = w2_b.t[fo].rearrange("p (kc c) -> p kc c", kc=32)
                            r.append((w2_b, dv[:, kg * 8:(kg + 1) * 8, :], wb[:, :, j * 128:(j + 1) * 128]))
                        return r
                    cast_block(w_ff2, v[:, kg * 8:(kg + 1) * 8, hb * 512:(hb + 1) * 512], KC, dsts)
            barrier(S)

        with contextlib.ExitStack() as es2:
            S.es = es2
            S.pool("kT", 2, [128, NKT * 128], BF16)
            S.pool("V", 2, [128, NKT * 128], BF16)
            S.pool("q", 2, [128, NTOK], BF16)
            S.pool("pT", 4, [128, 512], BF16)
            S.pool("f32a", 4, [128, 512], F32)
            S.pool("f32b", 4, [128, 512], F32)
            S.pool("osub", 4, [128, 512], F32)
            S.pool("bfa", 2, [128, 512], BF16)
            S.pool("bfo", 3, [128, 512], BF16)
            S.pool("st", 3, [128, 512], F32, space="psum")
            S.pool("acc", 4, [128, 512], F32, space="psum")
            S.pool("misc", 1, [128, 512], F32, space="psum")
            for h in range(4):
                kT = S.get("kT")
                load(S, kT, kT[:], kT_da, kT_da[h * 128:(h + 1) * 128, :])
                V = S.get("V")
                load(S, V, V[:], v_da, v_da[h])
                q = S.get("q")
                load(S, q, q[:], qT_da, qT_da[h * 128:(h + 1) * 128, :])
                for ti, (t0, n) in enumerate(TT):
                    keys = list(range(NKT)) if ti < 8 else [128, 129]
                    osub = []
                    for sub in range(2):
                        p0 = sub * 64
                        accO = S.get("acc")
                        accS = S.get("acc")
                        sts = {}

                        def qk(j):
                            st = S.get("st")
                            S.op("pe", lambda e: e.matmul(st[:, 0:n], kT[p0:p0 + 64, j * 128:(j + 1) * 128],
                                                          q[p0:p0 + 64, t0:t0 + n], start=True, stop=True), [kT, q], [st])
                            sts[j] = st

                        LA_ = 2
                        for jj in range(min(LA_, len(keys))):
                            qk(keys[jj])
                        for idx, j in enumerate(keys):
                            st = sts.pop(j)
                            pT = S.get("pT")
                            S.op("act", lambda e: e.activation(out=pT[:, 0:n], in_=st[:, 0:n], func=AF.Exp), [st], [pT])
                            if idx + LA_ < len(keys):
                                qk(keys[idx + LA_])
                            first, last = idx == 0, idx == len(keys) - 1
                            S.op("pe", lambda e: e.matmul(accO[:, 0:n], V[:, j * 128:(j + 1) * 128], pT[:, 0:n],
                                                          start=first, stop=last), [V, pT], [accO])
                            S.op("pe", lambda e: e.matmul(accS[:, 0:n], onesb[:], pT[:, 0:n], start=first, stop=last),
                                 [onesb, pT], [accS])
                        rec = S.get("f32a")
                        S.op("dve", lambda e: e.reciprocal(out=rec[:, 0:n], in_=accS[:, 0:n]), [accS], [rec])
                        os_ = S.get("osub")
                        S.op("dve", lambda e: e.tensor_tensor(out=os_[:, 0:n], in0=accO[:, 0:n], in1=rec[:, 0:n], op=ALU.mult),
                             [accO, rec], [os_])
                        osub.append(os_)
                    o = S.get("f32b")
                    S.op("dve", lambda e: e.scalar_tensor_tensor(out=o[:, 0:n], in0=osub[1][:, 0:n], scalar=neglam,
                                                                 in1=osub[0][:, 0:n], op0=ALU.mult, op1=ALU.add),
                         [osub[0], osub[1], lsum], [o])
                    sq = S.get("bfa")
                    S.op("act", lambda e: e.activation(out=sq[:, 0:n], in_=o[:, 0:n], func=AF.Square), [o], [sq])
                    ms = S.get("misc")
                    S.op("pe", lambda e: e.matmul(ms[:, 0:n], o128[:], sq[:, 0:n], start=True, stop=True), [o128, sq], [ms])
                    rstd = rstd_from(ms, n, "f32a")
                    y = S.get("bfo")
                    S.op("dve", lambda e: e.scalar_tensor_tensor(out=y[:, 0:n], in0=o[:, 0:n], scalar=gsub, in1=rstd[:, 0:n],
                                                                 op0=ALU.mult, op1=ALU.mult), [o, lsum, rstd], [y])
                    store(S, ys[0], ys[0][h * 128:(h + 1) * 128, t0:t0 + n], y, y[:, 0:n])
            barrier(S)

        with contextlib.ExitStack() as es2:
            S.es = es2
            tri32 = S.sbuf("tri32", [128, 768], F32)
            load(S, tri32, tri32[:], trim, trim[:, :])
            onec = S.sbuf("onec", [128, 1], F32)
            S.op("pool", lambda e: e.memset(onec[:], 1.0), [], [onec])
            og = [S.sbuf("og%d" % h, [128, NTOK], F32) for h in range(4)]
            Sst = [S.sbuf("Sst%d" % p, [128, 256], F32) for p in range(2)]
            Sb = [S.sbuf("Sb%d" % p, [128, 256], BF16) for p in range(2)]
            S.pool("la", 3, [128, 256], F32)
            S.pool("k", 3, [128, 256], BF16)
            S.pool("v", 3, [128, 512], BF16)
            S.pool("qk", 3, [128, 4, 128], BF16)
            S.pool("ekd", 2, [128, 256], F32)
            S.pool("kd", 2, [128, 256], BF16)
            S.pool("eqk", 4, [128, 128], F32)
            S.pool("qeke", 4, [128, 128], BF16)
            S.pool("aTm", 3, [128, 128], BF16)
            S.pool("dcol", 4, [128, 1], F32)
            S.pool("f32a", 4, [128, 512], F32)
            S.pool("bfa", 2, [128, 512], BF16)
            S.pool("bfo", 3, [128, 512], BF16)
            S.pool("gg", 2, [128, 512], BF16)
            S.pool("pex", 1, [128, 512], F32, space="psum")
            S.pool("pcum", 2, [128, 128], F32, space="psum")
            S.pool("paT", 2, [128, 128], F32, space="psum")
            S.pool("po", 2, [128, 128], F32, space="psum")
            S.pool("pds", 1, [128, 256], F32, space="psum")

            def gla_tile(d, la_t, la_ap, k_t, k_ap, v_t, v_ap, full, tcol):
                b = d * 384
                Tstr, Tincl, Mask = tri32[:, b:b + 128], tri32[:, b + 128:b + 256], tri32[:, b + 256:b + 384]
                la = S.get("la")
                load(S, la, la[:], la_t, la_ap)
                kk = S.get("k")
                load(S, kk, kk[:], k_t, k_ap)
                vv = S.get("v")
                load(S, vv, vv[:], v_t, v_ap)
                pex = S.get("pex")
                S.op("pe", lambda e: e.matmul(pex[:, 0:256], Tstr, la[:], start=True, stop=True), [tri32, la], [pex])
                ekd = S.get("ekd")
                S.op("act", lambda e: e.activation(out=ekd[:], in_=pex[:, 0:256], func=AF.Exp), [pex], [ekd])
                kd = S.get("kd")
                S.op("dve", lambda e: e.tensor_tensor(out=kd[:], in0=kk[:], in1=ekd[:], op=ALU.mult), [kk, ekd], [kd])
                if full:
                    qk = S.get("qk")
                    load(S, qk, qk[:, 0:2, :], qT_gl, qT_gl.t.rearrange("(pr p) t -> p pr t", p=128)[:, :, tcol:tcol + 128])
                    load(S, qk, qk[:, 2:4, :], kT_gl, kT_gl.t.rearrange("(pr p) t -> p pr t", p=128)[:, :, tcol:tcol + 128])
                for pr in range(2):
                    dcol = S.get("dcol")
                    if full:
                        pc = S.get("pcum")
                        S.op("pe", lambda e: e.matmul(pc[:], la[:, pr * 128:(pr + 1) * 128], Tincl, start=True, stop=True),
                             [la, tri32], [pc])
                        eq = S.get("eqk")
                        S.op("act", lambda e: e.activation(out=eq[:], in_=pc[:], func=AF.Exp), [pc], [eq])
                        ek = S.get("eqk")
                        S.op("act", lambda e: e.activation(out=ek[:], in_=pc[:], func=AF.Exp, scale=-1.0), [pc], [ek])
                        qe = S.get("qeke")
                        S.op("dve", lambda e: e.tensor_tensor(out=qe[:], in0=qk[:, pr, :], in1=eq[:], op=ALU.mult), [qk, eq], [qe])
                        ke = S.get("qeke")
                        S.op("pool", lambda e: e.tensor_tensor(out=ke[:], in0=qk[:, 2 + pr, :], in1=ek[:], op=ALU.mult), [qk, ek], [ke])
                        lastc = 127 if d == 0 else 0
                        S.op("act", lambda e: e.activation(out=dcol[:], in_=eq[:, lastc:lastc + 1], func=AF.Copy), [eq], [dcol])
                        for hl in range(2):
                            h = pr * 2 + hl
                            p0 = hl * 64
                            pa = S.get("paT")
                            S.op("pe", lambda e: e.matmul(pa[:], ke[p0:p0 + 64, :], qe[p0:p0 + 64, :], start=True, stop=True),
                                 [ke, qe], [pa])
                            am = S.get("aTm")
                            S.op("dve", lambda e: e.tensor_tensor(out=am[:], in0=pa[:], in1=Mask, op=ALU.mult), [pa, tri32], [am])
                            po = S.get("po")
                            S.op("pe", lambda e: e.matmul(po[:], vv[:, h * 128:(h + 1) * 128], am[:], start=True, stop=False),
                                 [vv, am], [po])
                            S.op("pe", lambda e: e.matmul(po[:], Sb[pr][p0:p0 + 64, hl * 128:(hl + 1) * 128], qe[p0:p0 + 64, :],
                                                          start=False, stop=True), [Sb[pr], qe], [po])
                            if d == 0:
                                S.op("act", lambda e: e.activation(out=og[h][:, tcol:tcol + 128], in_=po[:], func=AF.Copy),
                                     [po], [og[h]])
                            else:
                                S.op("dve", lambda e: e.tensor_tensor(out=og[h][:, tcol:tcol + 128], in0=po[:],
                                                                      in1=og[h][:, tcol:tcol + 128], op=ALU.add), [po, og[h]], [og[h]])
                    else:
                        pc = S.get("pcum")
                        S.op("pe", lambda e: e.matmul(pc[:, 0:1], la[:, pr * 128:(pr + 1) * 128], onec[:], start=True, stop=True),
                             [la, onec], [pc])
                        S.op("act", lambda e: e.activation(out=dcol[:], in_=pc[:, 0:1], func=AF.Exp), [pc], [dcol])
                    pds = S.get("pds")
                    S.op("pe", lambda e: e.matmul(pds[:], kd[:, pr * 128:(pr + 1) * 128], vv[:, pr * 256:(pr + 1) * 256],
                                                  start=True, stop=True), [kd, vv], [pds])
                    S.op("dve", lambda e: e.scalar_tensor_tensor(out=Sst[pr][:], in0=Sst[pr][:], scalar=dcol[:, 0:1], in1=pds[:],
                                                                 op0=ALU.mult, op1=ALU.add), [Sst[pr], dcol, pds], [Sst[pr]])
                    S.op("pool", lambda e: e.tensor_copy(out=Sb[pr][:], in_=Sst[pr][:]), [Sst[pr]], [Sb[pr]])

            for d in range(2):
                for pr in range(2):
                    S.op("pool", lambda e: e.memset(Sst[pr][:], 0.0), [], [Sst[pr]])
                    S.op("pool", lambda e: e.memset(Sb[pr][:], 0.0), [], [Sb[pr]])
                own = lambda t: (la_d, la_d[t * 128:(t + 1) * 128, d * 256:(d + 1) * 256], k_gl, k_gl[t * 128:(t + 1) * 128, :],
                                 v_gl, v_gl[t * 128:(t + 1) * 128, :])
                pre = lambda t: (pf_la, pf_la[d, t * 128:(t + 1) * 128, :], pf_k, pf_k[d, t * 128:(t + 1) * 128, :],
                                 pf_v, pf_v[d, t * 128:(t + 1) * 128, :])
                order = (lambda r: list(r)) if d == 0 else (lambda r: list(r)[::-1])
                for t in order(range(32, 34)):
                    gla_tile(d, *own(t), True, t * 128)
                for t in order(range(NPF)):
                    gla_tile(d, *pre(t), False, 0)
                for t in order(range(32)):
                    gla_tile(d, *own(t), True, t * 128)
            for h in range(4):
                for (t0, n) in TT:
                    sq = S.get("bfa")
                    S.op("act", lambda e: e.activation(out=sq[:, 0:n], in_=og[h][:, t0:t0 + n], func=AF.Square), [og[h]], [sq])
                    ms = S.get("pex")
                    S.op("pe", lambda e: e.matmul(ms[:, 0:n], o128[:], sq[:, 0:n], start=True, stop=True), [o128, sq], [ms])
                    rstd = rstd_from(ms, n, "f32a")
                    g = S.get("gg")
                    load(S, g, g[:, 0:n], ggT, ggT[h * 128:(h + 1) * 128, t0:t0 + n])
                    y0 = S.get("f32a")
                    S.op("dve", lambda e: e.scalar_tensor_tensor(out=y0[:, 0:n], in0=og[h][:, t0:t0 + n], scalar=gv[:, 1:2],
                                                                 in1=rstd[:, 0:n], op0=ALU.mult, op1=ALU.mult), [og[h], gv, rstd], [y0])
                    y = S.get("bfo")
                    S.op("pool", lambda e: e.tensor_tensor(out=y[:, 0:n], in0=y0[:, 0:n], in1=g[:, 0:n], op=ALU.mult), [y0, g], [y])
                    store(S, ys[1], ys[1][h * 128:(h + 1) * 128, t0:t0 + n], y, y[:, 0:n])
            barrier(S)

        with contextlib.ExitStack() as es2:
            S.es = es2
            kTn = S.sbuf("kTn", [128, 4, NA_NTOK], BF16)
            load(S, kTn, kTn[:], kT_na, kT_na.t.rearrange("(c p) t -> p c t", p=128))
            Vn = S.sbuf("Vn", [128, NA_NTOK // 128, 512], BF16)
            load(S, Vn, Vn[:], v_na, v_na.t.rearrange("(t p) c -> p t c", p=128))
            nb = S.sbuf("nb", [128, 56 * 128], F32)
            load(S, nb, nb[:], nbias, nbias[:, :])
            nm = S.sbuf("nm", [128, 35 * 128], F32)
            load(S, nm, nm[:], nmask, nmask[:, :])
            S.pool("q", 2, [128, NTOK], BF16)
            S.pool("s1", 3, [128, 128], F32)
            S.pool("s2", 3, [128, 128], F32)
            S.pool("pT", 4, [128, 128], BF16)
            S.pool("rec", 2, [128, 128], F32)
            S.pool("yt", 3, [128, 128], BF16)
            S.pool("st", 3, [128, 128], F32, space="psum")
            S.pool("acc", 4, [128, 128], F32, space="psum")
            for c in range(4):
                q = S.get("q")
                load(S, q, q[:], qT_na, qT_na[c * 128:(c + 1) * 128, :])
                for qt in range(34):
                    tcol = qt * 128
                    if qt < 32:
                        units = [(qt + o, o) for o in range(7)] + [(38, None), (39, None)]
                        cls = {0: 0, 1: 1, 30: 3, 31: 4}.get(qt, 2)
                    else:
                        units = [(38, None), (39, None)]
                        cls = 2
                    yt = S.get("yt")
                    for hl in range(2):
                        h = 2 * c + hl
                        p0 = hl * 64
                        accO = S.get("acc")
                        accS = S.get("acc")
                        for ui, (kt, o) in enumerate(units):
                            st = S.get("st")
                            S.op("pe", lambda e: e.matmul(st[:], kTn[p0:p0 + 64, c, kt * 128:(kt + 1) * 128],
                                                          q[p0:p0 + 64, tcol:tcol + 128], start=True, stop=True), [kTn, q], [st])
                            pT = S.get("pT")
                            if o is not None:
                                s1 = S.get("s1")
                                bo = (h * 7 + o) * 128
                                S.op("dve", lambda e: e.tensor_tensor(out=s1[:], in0=st[:], in1=nb[:, bo:bo + 128], op=ALU.add),
                                     [st, nb], [s1])
                                s2 = S.get("s2")
                                mo = (cls * 7 + o) * 128
                                S.op("pool", lambda e: e.tensor_tensor(out=s2[:], in0=s1[:], in1=nm[:, mo:mo + 128], op=ALU.add),
                                     [s1, nm], [s2])
                                S.op("act", lambda e: e.activation(out=pT[:], in_=s2[:], func=AF.Exp), [s2], [pT])
                            else:
                                S.op("act", lambda e: e.activation(out=pT[:], in_=st[:], func=AF.Exp), [st], [pT])
                            first, last = ui == 0, ui == len(units) - 1
                            S.op("pe", lambda e: e.matmul(accO[:], Vn[:, kt, c * 128:(c + 1) * 128], pT[:], start=first, stop=last),
                                 [Vn, pT], [accO])
                            S.op("pe", lambda e: e.matmul(accS[:], onesb[:], pT[:], start=first, stop=last), [onesb, pT], [accS])
                        rec = S.get("rec")
                        S.op("dve", lambda e: e.reciprocal(out=rec[p0:p0 + 64, :], in_=accS[p0:p0 + 64, :]), [accS], [rec])
                        S.op("dve", lambda e: e.tensor_tensor(out=yt[p0:p0 + 64, :], in0=accO[p0:p0 + 64, :], in1=rec[p0:p0 + 64, :],
                                                              op=ALU.mult), [accO, rec], [yt])
                    store(S, ys[2], ys[2][c * 128:(c + 1) * 128, tcol:tcol + 128], yt, yt[:])
            barrier(S)

        with contextlib.ExitStack() as es2:
            S.es = es2
            wbr = S.sbuf("wbr", [128, 12, D], BF16)
            for i in range(3):
                load(S, wbr, wbr[:, i * 4:(i + 1) * 4, :], wbr_b[i], wbr_b[i].t.rearrange("p (kc c) -> p kc c", kc=4))
            wo = S.sbuf("wo", [128, KC, D], BF16)
            load(S, wo, wo[:], wout_b, wout_b.t.rearrange("p (kc c) -> p kc c", kc=KC))
            S.pool("x", 1, [128, KC, 512], F32)
            S.pool("y", 1, [128, 12, 512], BF16)
            S.pool("g3", 2, [128, 3, 512], BF16)
            S.pool("m", 1, [128, KC, 512], BF16)
            S.pool("sq8", 1, [128, KC, 512], BF16)
            S.pool("h2", 1, [128, KC, 512], BF16)
            S.pool("a", 1, [128, 32, 512], BF16)
            S.pool("w1", 2, [128, KC, 512], BF16)
            S.pool("w2", 2, [128, 32, 128], BF16)
            S.pool("f32a", 4, [128, 512], F32)
            S.pool("f32b", 4, [128, 512], F32)
            S.pool("xo", 3, [128, 512], F32)
            S.pool("ps", 8, [128, 512], F32, space="psum")
            gv3 = gates.t.rearrange("(br fo p) t -> p br fo t", br=3, fo=8)
            xv = xT.t.rearrange("(kc p) t -> p kc t", p=128)
            for ti, (t0, n) in enumerate(TT):
                col = 0 if ti < 8 else 1
                mcol = lambda ch: modT[:, ch * 2 + col:ch * 2 + col + 1]
                x = S.get("x")
                load(S, x, x[:, :, 0:n], xT, xv[:, :, t0:t0 + n])
                y = S.get("y")
                for br in range(3):
                    load(S, y, y[:, br * 4:(br + 1) * 4, 0:n], ys[br], ys[br].t.rearrange("(kc p) t -> p kc t", p=128)[:, :, t0:t0 + n])
                m = S.get("m")
                for fo in range(8):
                    g3 = S.get("g3")
                    load(S, g3, g3[:, :, 0:n], gates, gv3[:, :, fo, t0:t0 + n])
                    tmps = []
                    for br in range(3):
                        ps = S.get("ps")
                        for kc in range(4):
                            S.op("pe", lambda e, kc=kc: e.matmul(ps[:, 0:n], wbr[:, br * 4 + kc, fo * 128:(fo + 1) * 128],
                                                                  y[:, br * 4 + kc, 0:n], start=(kc == 0), stop=(kc == 3)), [wbr, y], [ps])
                        tb = S.get("f32a")
                        S.op("dve", lambda e: e.tensor_tensor(out=tb[:, 0:n], in0=ps[:, 0:n], in1=g3[:, br, 0:n], op=ALU.mult),
                             [ps, g3], [tb])
                        tmps.append(tb)
                    s01 = S.get("f32b")
                    S.op("pool", lambda e: e.tensor_tensor(out=s01[:, 0:n], in0=tmps[0][:, 0:n], in1=tmps[1][:, 0:n], op=ALU.add),
                         [tmps[0], tmps[1]], [s01])
                    S.op("pool", lambda e: e.tensor_tensor(out=m[:, fo, 0:n], in0=s01[:, 0:n], in1=tmps[2][:, 0:n], op=ALU.add),
                         [s01, tmps[2]], [m])
                for fo in range(8):
                    ps = S.get("ps")
                    for kc in range(KC):
                        S.op("pe", lambda e, kc=kc: e.matmul(ps[:, 0:n], wo[:, kc, fo * 128:(fo + 1) * 128], m[:, kc, 0:n],
                                                              start=(kc == 0), stop=(kc == KC - 1)), [wo, m], [ps])
                    S.op("dve", lambda e: e.scalar_tensor_tensor(out=x[:, fo, 0:n], in0=ps[:, 0:n], scalar=mcol(16 + fo),
                                                                 in1=x[:, fo, 0:n], op0=ALU.mult, op1=ALU.add), [ps, modT, x], [x])
                sq = S.get("sq8")
                S.op("act", lambda e: e.activation(out=sq[:, :, 0:n], in_=x[:, :, 0:n], func=AF.Square), [x], [sq])
                ps = S.get("ps")
                for kc in range(KC):
                    S.op("pe", lambda e, kc=kc: e.matmul(ps[:, 0:n], o1024[:], sq[:, kc, 0:n], start=(kc == 0), stop=(kc == KC - 1)),
                         [o1024, sq], [ps])
                rstd = rstd_from(ps, n, "f32a")
                h2 = S.get("h2")
                for kc in range(KC):
                    tmp = S.get("f32b")
                    S.op("dve", lambda e, kc=kc: e.scalar_tensor_tensor(
                        out=tmp[:, 0:n], in0=x[:, kc, 0:n], scalar=A2[:, kc * 2 + col:kc * 2 + col + 1], in1=rstd[:, 0:n],
                        op0=ALU.mult, op1=ALU.mult), [x, A2, rstd], [tmp])
                    S.op("act", lambda e, kc=kc: e.activation(out=h2[:, kc, 0:n], in_=tmp[:, 0:n], func=AF.Identity,
                                                              bias=mcol(24 + kc), scale=1.0), [tmp, modT], [h2])
                a = S.get("a")
                for fb in range(8):
                    w1 = S.get("w1")
                    load(S, w1, w1[:], w1_b, w1_b.t[fb].rearrange("p (kc c) -> p kc c", kc=KC))
                    for j in range(4):
                        f = fb * 4 + j
                        ps = S.get("ps")
                        for kc in range(KC):
                            S.op("pe", lambda e, kc=kc: e.matmul(ps[:, 0:n], w1[:, kc, j * 128:(j + 1) * 128], h2[:, kc, 0:n],
                                                                  start=(kc == 0), stop=(kc == KC - 1)), [w1, h2], [ps])
                        r = S.get("f32a")
                        S.op("act", lambda e: e.activation(out=r[:, 0:n], in_=ps[:, 0:n], func=AF.Relu), [ps], [r])
                        S.op("pool", lambda e: e.tensor_tensor(out=a[:, f, 0:n], in0=r[:, 0:n], in1=r[:, 0:n], op=ALU.mult), [r], [a])
                for fo in range(8):
                    w2 = S.get("w2")
                    load(S, w2, w2[:], w2_b, w2_b.t[fo].rearrange("p (kc c) -> p kc c", kc=32))
                    ps = S.get("ps")
                    for f in range(32):
                        S.op("pe", lambda e, f=f: e.matmul(ps[:, 0:n], w2[:, f, :], a[:, f, 0:n], start=(f == 0), stop=(f == 31)),
                             [w2, a], [ps])
                    xt = S.get("xo")
                    S.op("dve", lambda e: e.scalar_tensor_tensor(out=xt[:, 0:n], in0=ps[:, 0:n], scalar=mcol(40 + fo),
                                                                 in1=x[:, fo, 0:n], op0=ALU.mult, op1=ALU.add), [ps, modT, x], [xt])
                    store(S, xo, xo[fo * 128:(fo + 1) * 128, t0:t0 + n], xt, xt[:, 0:n])
            S.finish()
            barrier(S)
        S.es = es
    return nc


def tri_consts():
    i = np.arange(128)
    sp, s = i[:, None], i[None, :]
    mats = [(sp > s), (sp <= s), (sp <= s), (sp < s), (sp >= s), (sp >= s)]
    return np.concatenate([m.astype(np.float32) for m in mats], axis=1)


def na_bias_table(rpb):
    ka = np.arange(128)
    a, kc = ka // 64, ka % 64
    bq, cq = ka // 64, ka % 64
    out = np.zeros((128, 8, 7, 128), np.float32)
    for o in range(7):
        rel_row = (-6 + 2 * o + a)[:, None] - bq[None, :] + 7
        rel_col = kc[:, None] - cq[None, :] + 15
        ok = (rel_row >= 0) & (rel_row <= 14) & (rel_col >= 0) & (rel_col <= 30)
        rr = np.clip(rel_row, 0, 14)
        rc = np.clip(rel_col, 0, 30)
        for h in range(8):
            out[:, h, o, :] = np.where(ok, rpb[h][rr, rc], 0.0)
    return out.reshape(128, -1)


def na_mask_table(qtr):
    ka = np.arange(128)
    a, kc = ka // 64, ka % 64
    bq, cq = ka // 64, ka % 64
    out = np.zeros((128, 5, 7, 128), np.float32)
    for cls, qt in enumerate((0, 1, 2, 30, 31)):
        Rq = qtr * 64 + 2 * qt
        R = Rq + bq
        rs = np.clip(R - 4, 0, 256 - 8)
        cs = np.clip(cq - 8, 0, GRID_W - 16)
        for o in range(7):
            kr = (Rq - 6 + 2 * o + a)[:, None]
            ok = (kr >= rs[None, :]) & (kr < rs[None, :] + 8) & (kc[:, None] >= cs[None, :]) & (kc[:, None] < cs[None, :] + 16)
            out[:, cls, o, :] = np.where(ok, 0.0, -30000.0)
    return out.reshape(128, -1)


def lb_inputs(l, core, xT_core, ra, inp):
    b, qtr = core // 4, core % 4
    grp = [ra[4 * b + j] for j in range(4)]
    me = ra[core]
    bf = ml_dtypes.bfloat16
    kT_da = np.concatenate([np.asarray(g["kT_da"])[:, :NLAT] for g in grp] + [np.asarray(me["kT_da"])[:, NLAT:]], axis=1)
    v_all = np.concatenate([np.asarray(g["v_da"])[:NLAT] for g in grp] + [np.asarray(me["v_da"])[NLAT:]], axis=0)
    v_da = np.ascontiguousarray(v_all.reshape(NKT, 128, 4, 128).transpose(2, 1, 0, 3).reshape(4, 128, NKT * 128))
    lam_init = 0.8 - 0.6 * math.exp(-0.3 * l)
    npf = NPF * 128
    pf_la = np.zeros((2, npf, 256), np.float32)
    pf_k = np.zeros((2, npf, 256), bf)
    pf_v = np.zeros((2, npf, 512), bf)
    nb_ = qtr
    if nb_:
        pf_la[0, :nb_ * NLAT] = np.concatenate([np.asarray(grp[j]["la"])[:NLAT, 0:256] for j in range(qtr)], 0)
        pf_k[0, :nb_ * NLAT] = np.concatenate([np.asarray(grp[j]["k_gl"])[:NLAT] for j in range(qtr)], 0)
        pf_v[0, :nb_ * NLAT] = np.concatenate([np.asarray(grp[j]["v_gl"])[:NLAT] for j in range(qtr)], 0)
    na_ = 3 - qtr
    if na_:
        pf_la[1, npf - na_ * NLAT:] = np.concatenate([np.asarray(grp[j]["la"])[:NLAT, 256:512] for j in range(qtr + 1, 4)], 0)
        pf_k[1, npf - na_ * NLAT:] = np.concatenate([np.asarray(grp[j]["k_gl"])[:NLAT] for j in range(qtr + 1, 4)], 0)
        pf_v[1, npf - na_ * NLAT:] = np.concatenate([np.asarray(grp[j]["v_gl"])[:NLAT] for j in range(qtr + 1, 4)], 0)
    kn_all = np.concatenate([np.asarray(g["kT_na"])[:, :NLAT] for g in grp], axis=1)
    vn_all = np.concatenate([np.asarray(g["v_na"])[:NLAT] for g in grp], axis=0)
    lo = (qtr * 64 - 6) * GRID_W
    hi = lo + NHALO
    kT_na = np.zeros((512, NA_NTOK), bf)
    v_na = np.zeros((NA_NTOK, 512), bf)
    s0, s1 = max(lo, 0), min(hi, SEQ)
    kT_na[:, s0 - lo:s1 - lo] = kn_all[:, s0:s1]
    v_na[s0 - lo:s1 - lo] = vn_all[s0:s1]
    kT_na[:, NHALO:] = np.asarray(me["kT_na"])[:, NLAT:]
    v_na[NHALO:] = np.asarray(me["v_na"])[NLAT:]
    return {
        "xT": xT_core, "modT": np.asarray(me["modT"]), "n2g": chunkT(inp["norm2_g"][l]),
        "qT_da": np.asarray(me["qT_da"]), "kT_da": np.ascontiguousarray(kT_da), "v_da": v_da,
        "lamtab": np.ascontiguousarray(np.tile(inp["da_lambda"][l].reshape(1, 256), (128, 1)).astype(np.float32)),
        "lconst": np.tile(np.array([[lam_init, 1.0 - lam_init]], np.float32), (128, 1)),
        "gvec": np.stack([inp["da_subln_g"][l], inp["gla_gn_g"][l]], 1).astype(np.float32),
        "qT_gl": np.asarray(me["qT_gl"]), "kT_gl": np.asarray(me["kT_gl"]), "k_gl": np.asarray(me["k_gl"]),
        "v_gl": np.asarray(me["v_gl"]), "la": np.asarray(me["la"]), "ggT": np.asarray(me["ggT"]),
        "pf_la": pf_la, "pf_k": pf_k, "pf_v": pf_v, "trim": tri_consts(),
        "qT_na": np.asarray(me["qT_na"]), "kT_na": kT_na, "v_na": v_na,
        "nbias": na_bias_table(np.asarray(inp["na_rpb"][l], np.float32)), "nmask": na_mask_table(qtr),
        "gates": np.asarray(me["gates"]),
        "w_br0": inp["w_br_da"][l], "w_br1": inp["w_br_gla"][l], "w_br2": inp["w_br_na"][l],
        "w_out": inp["w_out"][l], "w_ff1": inp["w_ff1"][l], "w_ff2": inp["w_ff2"][l],
    }


def run_lb(l, xTs, ra, inp):
    if "lb" not in _NC:
        _NC["lb"] = build_lb()
    in_maps = [lb_inputs(l, c, xTs[c], ra, inp) for c in range(NCORES)]
    res = run_bass_kernel_spmd(_NC["lb"], in_maps, core_ids=list(range(NCORES)))
    return res.results


def kernel(**inputs):
    inp = {k: np.asarray(v) for k, v in inputs.items()}
    xTs = make_xT(inp["x"], inp["ctx"])
    for l in range(2):
        ra = run_la(l, xTs, inp)
        rb = run_lb(l, xTs, ra, inp)
        xTs = [np.ascontiguousarray(np.asarray(rb[c]["xo"], dtype=np.float32)) for c in range(NCORES)]
    out = np.empty((2, SEQ, D), np.float32)
    for c in range(NCORES):
        b, qtr = c // 4, c % 4
        out[b, qtr * NLAT:(qtr + 1) * NLAT] = xTs[c][:, :NLAT].T
    return out
```

```python
import contextlib
import math
import numpy as np
import ml_dtypes
import concourse.bass as bass
import concourse.mybir as mybir
from concourse.bass_utils import run_bass_kernel_spmd

F32 = mybir.dt.float32
BF16 = mybir.dt.bfloat16
AF = mybir.ActivationFunctionType
ALU = mybir.AluOpType
AX = mybir.AxisListType

NCORES = 8
D = 1024
KC = 8
SEQ = 16384
NLAT = 4096
NCTX = 256
NTOK = NLAT + NCTX
GRID_W = 64
EPS = 1e-6
N_IN = 7712
TT = [(i * 512, 512) for i in range(8)] + [(4096, 256)]


class Tile:
    def __init__(self, t, name):
        self.t = t
        self.name = name
        self.w = {}
        self.r = {}
        self.sem = None
        self.semval = 0
        self.track = True

    def __getitem__(self, k):
        return self.t[k]


def _merge(dst, src):
    for k, v in src.items():
        if dst.get(k, 0) < v:
            dst[k] = v


class Sched:
    def __init__(self, nc, es):
        self.nc = nc
        self.es = es
        self.eng = {"pe": nc.tensor, "act": nc.scalar, "dve": nc.vector, "pool": nc.gpsimd, "sp": nc.sync}
        self.sem = {}
        self.cnt = {}
        for k in ("pe", "act", "dve", "pool"):
            self.sem[k] = es.enter_context(nc.semaphore("s_" + k))
            self.cnt[k] = 0
        self.seen = {k: {} for k in self.eng}
        self.pending = {}
        self.nsem = 0
        self.pools = {}

    def sbuf(self, name, shape, dtype):
        self.uid = getattr(self, "uid", 0) + 1
        return Tile(self.es.enter_context(self.nc.sbuf_tensor("sb%d_%s" % (self.uid, name), list(shape), dtype)),
                    "%s_%d" % (name, self.uid))

    def psum(self, name, shape, dtype=F32):
        self.uid = getattr(self, "uid", 0) + 1
        return Tile(self.es.enter_context(self.nc.psum_tensor("pp%d_%s" % (self.uid, name), list(shape), dtype)),
                    "%s_%d" % (name, self.uid))

    def dram(self, name, shape, dtype, kind):
        t = self.nc.dram_tensor(name, list(shape), dtype, kind=kind)
        tl = Tile(t.ap(), name)
        tl.track = False
        return tl

    def pool(self, tag, n, shape, dtype, space="sbuf"):
        mk = self.sbuf if space == "sbuf" else self.psum
        self.pools[tag] = [[mk("%s%d" % (tag, i), shape, dtype) for i in range(n)], 0]

    def get(self, tag):
        p = self.pools[tag]
        t = p[0][p[1] % len(p[0])]
        p[1] += 1
        return t

    def _wait(self, en, deps):
        seen = self.seen[en]
        for k, v in deps.items():
            if k == "pe" and en == "pe":
                continue
            if seen.get(k, 0) >= v:
                continue
            self.eng[en].wait_ge(self.sem[k], v)
            seen[k] = v

    def op(self, en, fn, reads=(), writes=()):
        deps = {}
        for t in reads:
            _merge(deps, t.w)
        for t in writes:
            _merge(deps, t.w)
            _merge(deps, t.r)
        self._wait(en, deps)
        ins = fn(self.eng[en])
        self.cnt[en] += 1
        ev = {en: self.cnt[en]}
        ins.then_inc(self.sem[en], 1)
        for t in reads:
            _merge(t.r, ev)
        for t in writes:
            t.w = dict(ev)
            t.r = {}
        return ins

    def dma(self, dst, dst_ap, src, src_ap, owner):
        if owner.sem is None:
            self.nsem += 1
            owner.sem = "d%d_%s" % (self.nsem, owner.name)
            self.sem[owner.sem] = self.es.enter_context(self.nc.semaphore(owner.sem))
        deps = {}
        if src.track:
            _merge(deps, src.w)
        if dst.track:
            _merge(deps, dst.w)
            _merge(deps, dst.r)
        self._wait("sp", deps)
        owner.semval += 16
        ev = {owner.sem: owner.semval}
        self.nc.sync.dma_start(out=dst_ap, in_=src_ap).then_inc(self.sem[owner.sem], 16)
        if src.track:
            _merge(src.r, ev)
        if dst.track:
            dst.w = dict(ev)
            dst.r = {}
        _merge(self.pending, ev)

    def finish(self):
        self._wait("sp", self.pending)


def load(S, dst, dst_ap, src, src_ap):
    S.dma(dst, dst_ap, src, src_ap, owner=dst)


def store(S, dst, dst_ap, src, src_ap):
    S.dma(dst, dst_ap, src, src_ap, owner=src)


C_DAQ, C_DAK, C_DAV = 0, 512, 1024
C_GQ, C_GK, C_GV, C_GG, C_GA = 1536, 1792, 2048, 2560, 3072
C_NQ, C_NK, C_NV = 3104, 3616, 4128
C_GATE = 4640

LA_OUT = [("qT_da", [512, NTOK], BF16), ("kT_da", [512, NTOK], BF16), ("v_da", [NTOK, 512], BF16),
          ("qT_gl", [256, NTOK], BF16), ("kT_gl", [256, NTOK], BF16), ("k_gl", [NTOK, 256], BF16),
          ("v_gl", [NTOK, 512], BF16), ("ggT", [512, NTOK], BF16), ("la", [NTOK, 512], F32),
          ("qT_na", [512, NTOK], BF16), ("kT_na", [512, NTOK], BF16), ("v_na", [NTOK, 512], BF16),
          ("gates", [3072, NTOK], BF16), ("modT", [128, 96], F32)]


def build_la():
    nc = bass.Bass("TRN2", target_bir_lowering=False)
    es = contextlib.ExitStack()
    with es:
        S = Sched(nc, es)
        di = lambda n, s, d=F32: S.dram(n, s, d, "ExternalInput")
        xT = di("xT", [D, NTOK])
        cT = di("cT", [128, KC * 2])
        w_mod = di("w_mod", [D, 6 * D])
        b_modT = di("b_modT", [128, 48])
        n1g = di("n1g", [128, KC])
        w_in = di("w_in", [D, N_IN])
        gcols = di("gcols", [128, 4])
        ropeC = di("ropeC", [128, NTOK])
        ropeS = di("ropeS", [128, NTOK])
        cmats = di("cmats", [128, 3 * 128])
        a2aug = di("a2aug", [33, 512])
        outs = {n: S.dram(n, s, d, "ExternalOutput") for n, s, d in LA_OUT}

        hT = S.sbuf("hT", [128, KC, NTOK], BF16)
        S.pool("stage", 2, [128, KC, 512], F32)
        S.pool("wb", 2, [128, KC, 512], BF16)
        S.pool("sq8", 1, [128, KC, 512], BF16)
        S.pool("rope", 4, [128, 512], F32)
        S.pool("f32a", 4, [128, 512], F32)
        S.pool("f32b", 4, [128, 512], F32)
        S.pool("bfa", 3, [128, 512], BF16)
        S.pool("bfo", 4, [128, 512], BF16)
        S.pool("ps", 7, [128, 512], F32, space="psum")
        cm32 = S.sbuf("cm32", [128, 384], F32)
        cmb = S.sbuf("cmb", [128, 384], BF16)
        cTs = S.sbuf("cTs", [128, KC * 2], F32)
        sil = S.sbuf("sil", [128, KC * 2], F32)
        bmod = S.sbuf("bmod", [128, 48], F32)
        n1gs = S.sbuf("n1gs", [128, KC], F32)
        gcs = S.sbuf("gcs", [128, 4], F32)
        modT = S.sbuf("modT", [128, 96], F32)
        A1 = S.sbuf("A1", [128, KC * 2], F32)
        a2s = S.sbuf("a2s", [33, 512], F32)
        gaT = S.sbuf("gaT", [33, 512], F32)
        ps_small = S.psum("ps_small", [128, 2], F32)
        cst = S.sbuf("cst", [128, 2], F32)
        S.op("pool", lambda e: e.memset(cst[:, 0:1], EPS), [], [cst])
        S.op("pool", lambda e: e.memset(cst[:, 1:2], 1.0), [], [cst])

        def rstd_from(ps, n, pool_tag):
            sq_ = S.get(pool_tag)
            S.op("act", lambda e: e.activation(out=sq_[:, 0:n], in_=ps[:, 0:n], func=AF.Sqrt, bias=cst[:, 0:1], scale=1.0),
                 [ps, cst], [sq_])
            r_ = S.get(pool_tag)
            S.op("dve", lambda e: e.reciprocal(out=r_[:, 0:n], in_=sq_[:, 0:n]), [sq_], [r_])
            return r_

        load(S, cm32, cm32[:], cmats, cmats[:, :])
        load(S, cTs, cTs[:], cT, cT[:, :])
        load(S, bmod, bmod[:], b_modT, b_modT[:, :])
        load(S, n1gs, n1gs[:], n1g, n1g[:, :])
        load(S, gcs, gcs[:], gcols, gcols[:, :])
        load(S, a2s, a2s[:], a2aug, a2aug[:, :])
        S.op("dve", lambda e: e.tensor_copy(out=cmb[:], in_=cm32[:]), [cm32], [cmb])
        Pm = cmb[:, 0:128]
        Bones = cmb[:, 128:256]
        Ones = cmb[:, 256:384]
        S.op("act", lambda e: e.activation(out=sil[:], in_=cTs[:], func=AF.Silu), [cTs], [sil])
        S.op("pool", lambda e: e.memset(gaT[32:33, :], 1.0), [], [gaT])

        wm_v = w_mod.t.rearrange("(kc p) f -> p kc f", p=128)
        for sb in range(12):
            st = S.get("stage")
            load(S, st, st[:], w_mod, wm_v[:, :, sb * 512:(sb + 1) * 512])
            for j in range(4):
                fo = sb * 4 + j
                for kc in range(KC):
                    S.op("pe", lambda e, kc=kc, j=j, st=st: e.matmul(
                        ps_small[:], st[:, kc, j * 128:(j + 1) * 128], sil[:, kc * 2:kc * 2 + 2],
                        start=(kc == 0), stop=(kc == KC - 1)), [st, sil], [ps_small])
                S.op("dve", lambda e, fo=fo: e.tensor_scalar(
                    out=modT[:, fo * 2:fo * 2 + 2], in0=ps_small[:], scalar1=bmod[:, fo:fo + 1], scalar2=None,
                    op0=ALU.add), [ps_small, bmod], [modT])
        store(S, outs["modT"], outs["modT"][:, :], modT, modT[:])
        for kc in range(KC):
            S.op("dve", lambda e, kc=kc: e.tensor_scalar(
                out=A1[:, kc * 2:kc * 2 + 2], in0=modT[:, (8 + kc) * 2:(8 + kc) * 2 + 2], scalar1=1.0,
                scalar2=n1gs[:, kc:kc + 1], op0=ALU.add, op1=ALU.mult), [modT, n1gs], [A1])

        xv = xT.t.rearrange("(kc p) t -> p kc t", p=128)
        for ti, (t0, n) in enumerate(TT):
            col = 0 if ti < 8 else 1
            st = S.get("stage")
            load(S, st, st[:, :, 0:n], xT, xv[:, :, t0:t0 + n])
            sq = S.get("sq8")
            S.op("act", lambda e: e.activation(out=sq[:, :, 0:n], in_=st[:, :, 0:n], func=AF.Square), [st], [sq])
            ps = S.get("ps")
            for kc in range(KC):
                S.op("pe", lambda e, kc=kc: e.matmul(ps[:, 0:n], Ones, sq[:, kc, 0:n], start=(kc == 0),
                                                      stop=(kc == KC - 1)), [cmb, sq], [ps])
            rstd = rstd_from(ps, n, "f32a")
            for kc in range(KC):
                tmp = S.get("f32b")
                S.op("dve", lambda e, kc=kc: e.scalar_tensor_tensor(
                    out=tmp[:, 0:n], in0=st[:, kc, 0:n], scalar=A1[:, kc * 2 + col:kc * 2 + col + 1],
                    in1=rstd[:, 0:n], op0=ALU.mult, op1=ALU.mult), [st, A1, rstd], [tmp])
                S.op("act", lambda e, kc=kc: e.activation(
                    out=hT[:, kc, t0:t0 + n], in_=tmp[:, 0:n], func=AF.Identity,
                    bias=modT[:, kc * 2 + col:kc * 2 + col + 1], scale=1.0), [tmp, modT], [hT])

        wv = w_in.t.rearrange("(kc p) c -> p kc c", p=128)

        def load_w(c0, ncol):
            st = S.get("stage")
            load(S, st, st[:, :, 0:ncol], w_in, wv[:, :, c0:c0 + ncol])
            wb = S.get("wb")
            for kc in range(KC):
                en = "pool" if kc % 2 == 0 else "dve"
                S.op(en, lambda e, kc=kc: e.tensor_copy(out=wb[:, kc, 0:ncol], in_=st[:, kc, 0:ncol]), [st], [wb])
            return wb

        def proj_fm(wb, j, t0, n, m=128):
            ps = S.get("ps")
            for kc in range(KC):
                S.op("pe", lambda e, kc=kc: e.matmul(ps[0:m, 0:n], wb[:, kc, j * 128:j * 128 + m], hT[:, kc, t0:t0 + n],
                                                      start=(kc == 0), stop=(kc == KC - 1)), [wb, hT], [ps])
            return ps

        def headnorm(ps, n, gi):
            zs = S.get("f32a")
            S.op("act", lambda e: e.activation(out=zs[:, 0:n], in_=ps[:, 0:n], func=AF.Copy), [ps], [zs])
            sq = S.get("bfa")
            S.op("act", lambda e: e.activation(out=sq[:, 0:n], in_=ps[:, 0:n], func=AF.Square), [ps], [sq])
            ps2 = S.get("ps")
            S.op("pe", lambda e: e.matmul(ps2[:, 0:n], Bones, sq[:, 0:n], start=True, stop=True), [cmb, sq], [ps2])
            rstd = rstd_from(ps2, n, "f32b")
            qh = S.get("bfa")
            S.op("dve", lambda e: e.scalar_tensor_tensor(out=qh[:, 0:n], in0=zs[:, 0:n], scalar=gcs[:, gi:gi + 1],
                                                         in1=rstd[:, 0:n], op0=ALU.mult, op1=ALU.mult),
                 [zs, gcs, rstd], [qh])
            return qh

        def fm_store(name, j, t0, n, src, m=128):
            o = outs[name]
            store(S, o, o[j * 128:j * 128 + m, t0:t0 + n], src, src[0:m, 0:n])

        def qk_block(c0, name, gi, rope):
            wb = load_w(c0, 512)
            for j in range(4):
                for (t0, n) in TT:
                    ps = proj_fm(wb, j, t0, n)
                    qh = headnorm(ps, n, gi)
                    if rope:
                        rc = S.get("rope")
                        load(S, rc, rc[:, 0:n], ropeC, ropeC[:, t0:t0 + n])
                        rs = S.get("rope")
                        load(S, rs, rs[:, 0:n], ropeS, ropeS[:, t0:t0 + n])
                        ps3 = S.get("ps")
                        S.op("pe", lambda e: e.matmul(ps3[:, 0:n], Pm, qh[:, 0:n], start=True, stop=True), [cmb, qh], [ps3])
                        t1 = S.get("f32a")
                        S.op("pool", lambda e: e.tensor_tensor(out=t1[:, 0:n], in0=qh[:, 0:n], in1=rc[:, 0:n], op=ALU.mult),
                             [qh, rc], [t1])
                        t2 = S.get("f32b")
                        S.op("dve", lambda e: e.tensor_tensor(out=t2[:, 0:n], in0=ps3[:, 0:n], in1=rs[:, 0:n], op=ALU.mult),
                             [ps3, rs], [t2])
                        ob = S.get("bfo")
                        S.op("pool", lambda e: e.tensor_tensor(out=ob[:, 0:n], in0=t1[:, 0:n], in1=t2[:, 0:n], op=ALU.add),
                             [t1, t2], [ob])
                    else:
                        ob = qh
                    fm_store(name, j, t0, n, ob)

        def act_block(c0, ncol, name, func, scale=1.0, jbase=0):
            wb = load_w(c0, ncol)
            for j in range(ncol // 128):
                for (t0, n) in TT:
                    ps = proj_fm(wb, j, t0, n)
                    ob = S.get("bfo")
                    S.op("act", lambda e: e.activation(out=ob[:, 0:n], in_=ps[:, 0:n], func=func, scale=scale), [ps], [ob])
                    fm_store(name, jbase + j, t0, n, ob)
            return wb

        def tm_block(wb, cofs, ncol, name):
            o = outs[name]
            for s in range(NTOK // 128):
                ps = S.get("ps")
                for kc in range(KC):
                    S.op("pe", lambda e, kc=kc: e.matmul(ps[:, 0:ncol], hT[:, kc, s * 128:(s + 1) * 128],
                                                          wb[:, kc, cofs:cofs + ncol], start=(kc == 0), stop=(kc == KC - 1)),
                         [wb, hT], [ps])
                ob = S.get("bfo")
                if s % 2 == 0:
                    S.op("act", lambda e: e.activation(out=ob[:, 0:ncol], in_=ps[:, 0:ncol], func=AF.Copy), [ps], [ob])
                else:
                    S.op("dve", lambda e: e.tensor_copy(out=ob[:, 0:ncol], in_=ps[:, 0:ncol]), [ps], [ob])
                store(S, o, o[s * 128:(s + 1) * 128, 0:ncol], ob, ob[:, 0:ncol])

        qk_block(C_DAQ, "qT_da", 0, True)
        qk_block(C_DAK, "kT_da", 1, True)
        tm_block(load_w(C_DAV, 512), 0, 512, "v_da")
        wb = load_w(C_GQ, 512)
        for j in range(4):
            for (t0, n) in TT:
                ps = proj_fm(wb, j, t0, n)
                ob = S.get("bfo")
                S.op("act", lambda e: e.activation(out=ob[:, 0:n], in_=ps[:, 0:n], func=AF.Copy,
                                                   scale=(0.125 if j < 2 else 1.0)), [ps], [ob])
                fm_store("qT_gl" if j < 2 else "kT_gl", j % 2, t0, n, ob)
        tm_block(wb, 256, 256, "k_gl")
        tm_block(load_w(C_GV, 512), 0, 512, "v_gl")
        act_block(C_GG, 512, "ggT", AF.Silu)
        wb = load_w(C_GA, 32)
        lao = outs["la"]
        for (t0, n) in TT:
            ps = proj_fm(wb, 0, t0, n, m=32)
            S.op("act", lambda e: e.activation(out=gaT[0:32, 0:n], in_=ps[0:32, 0:n], func=AF.Copy), [ps], [gaT])
            for s in range(n // 128):
                ps2 = S.get("ps")
                S.op("pe", lambda e, s=s: e.matmul(ps2[:, :], gaT[0:33, s * 128:(s + 1) * 128], a2s[0:33, :],
                                                    start=True, stop=True), [gaT, a2s], [ps2])
                ex = S.get("f32a")
                S.op("act", lambda e: e.activation(out=ex[:, :], in_=ps2[:, :], func=AF.Exp, scale=-1.0), [ps2], [ex])
                ln = S.get("f32b")
                S.op("act", lambda e: e.activation(out=ln[:, :], in_=ex[:, :], func=AF.Ln, bias=cst[:, 1:2], scale=1.0), [ex, cst], [ln])
                lo = S.get("f32a")
                S.op("dve", lambda e: e.tensor_scalar(out=lo[:, :], in0=ln[:, :], scalar1=-1.0 / 16.0, scalar2=None,
                                                      op0=ALU.mult), [ln], [lo])
                store(S, lao, lao[t0 + s * 128:t0 + (s + 1) * 128, :], lo, lo[:, :])
        qk_block(C_NQ, "qT_na", 2, False)
        qk_block(C_NK, "kT_na", 3, False)
        tm_block(load_w(C_NV, 512), 0, 512, "v_na")
        for g in range(6):
            act_block(C_GATE + g * 512, 512, "gates", AF.Sigmoid, jbase=g * 4)
        S.finish()
    return nc


def rope_tables():
    pos = np.arange(NLAT)
    return pos


def la_consts():
    Pm = np.zeros((128, 128), np.float32)
    for m in range(128):
        w = (m % 64) % 32
        if w < 16:
            Pm[m + 16, m] = -1.0
        else:
            Pm[m - 16, m] = 1.0
    Bones = np.zeros((128, 128), np.float32)
    Bones[:64, :64] = 1.0 / 64
    Bones[64:, 64:] = 1.0 / 64
    Ones = np.full((128, 128), 1.0 / 1024, np.float32)
    return np.concatenate([Pm, Bones, Ones], axis=1)


def rope_cs(qtr):
    tpos = qtr * NLAT + np.arange(NLAT)
    row = (tpos // GRID_W).astype(np.float32)
    colp = (tpos % GRID_W).astype(np.float32)
    freqs = (10000.0 ** (-np.arange(16, dtype=np.float32) / 16)).astype(np.float32)
    C = np.ones((128, NTOK), np.float32)
    Sn = np.zeros((128, NTOK), np.float32)
    for p in range(128):
        u = p % 64
        i = (u % 32) % 16
        ang = (row if u < 32 else colp) * freqs[i]
        C[p, :NLAT] = np.cos(ang.astype(np.float32))
        Sn[p, :NLAT] = np.sin(ang.astype(np.float32))
    return C, Sn


def chunkT(v):
    v = np.asarray(v, np.float32)
    if v.ndim == 1:
        return np.ascontiguousarray(v.reshape(-1, 128).T)
    return np.ascontiguousarray(v.reshape(v.shape[0], -1, 128).transpose(2, 1, 0).reshape(128, -1))


def la_inputs(l, core, xT_core, inp):
    b, qtr = core // 4, core % 4
    cc = np.stack([inp["c"][b], inp["c_ctx"]], 0)
    C, Sn = rope_cs(qtr)
    gc = np.stack([np.tile(inp["da_qn_g"][l], 2) * 0.125, np.tile(inp["da_kn_g"][l], 2),
                   np.tile(inp["na_qn_g"][l], 2) * 0.125, np.tile(inp["na_kn_g"][l], 2)], 1).astype(np.float32)
    a2aug = np.zeros((33, 512), np.float32)
    a2aug[0:16, 0:256] = inp["gla_a2"][l, 0]
    a2aug[16:32, 256:512] = inp["gla_a2"][l, 1]
    a2aug[32, 0:256] = inp["gla_a_b"][l, 0]
    a2aug[32, 256:512] = inp["gla_a_b"][l, 1]
    return {"xT": xT_core, "cT": chunkT(cc), "w_mod": inp["w_mod"][l], "b_modT": chunkT(inp["b_mod"][l]),
            "n1g": chunkT(inp["norm1_g"][l]), "w_in": inp["w_in"][l], "gcols": gc, "ropeC": C, "ropeS": Sn,
            "cmats": la_consts(), "a2aug": a2aug}


_NC = {}


def run_la(l, xTs, inp):
    if "la" not in _NC:
        _NC["la"] = build_la()
    in_maps = [la_inputs(l, c, xTs[c], inp) for c in range(NCORES)]
    res = run_bass_kernel_spmd(_NC["la"], in_maps, core_ids=list(range(NCORES)))
    return res.results


def make_xT(x, ctx):
    xs = []
    for c in range(NCORES):
        b, qtr = c // 4, c % 4
        xs.append(np.ascontiguousarray(
            np.concatenate([x[b, qtr * NLAT:(qtr + 1) * NLAT], ctx[b]], 0).T.astype(np.float32)))
    return xs


NKT = 130
NPF = 96
NHALO = 38 * 128
NA_NTOK = NHALO + NCTX


def barrier(S):
    allv = dict(S.pending)
    for k in ("pe", "act", "dve", "pool"):
        if S.cnt[k]:
            allv[k] = S.cnt[k]
    for en in ("pe", "act", "dve", "pool", "sp"):
        d = {k: v for k, v in allv.items() if k != en or en == "sp"}
        seen = S.seen[en]
        for k, v in d.items():
            if seen.get(k, 0) < v:
                S.eng[en].wait_ge(S.sem[k], v)
                seen[k] = v


def build_lb():
    nc = bass.Bass("TRN2", target_bir_lowering=False)
    es = contextlib.ExitStack()
    with es:
        S = Sched(nc, es)
        di = lambda n, s, d=F32: S.dram(n, s, d, "ExternalInput")
        xT = di("xT", [D, NTOK])
        modT_d = di("modT", [128, 96])
        n2g = di("n2g", [128, KC])
        qT_da = di("qT_da", [512, NTOK], BF16)
        kT_da = di("kT_da", [512, NKT * 128], BF16)
        v_da = di("v_da", [4, 128, NKT * 128], BF16)
        lamtab = di("lamtab", [128, 256])
        lconst = di("lconst", [128, 2])
        gvec = di("gvec", [128, 2])
        qT_gl = di("qT_gl", [256, NTOK], BF16)
        kT_gl = di("kT_gl", [256, NTOK], BF16)
        k_gl = di("k_gl", [NTOK, 256], BF16)
        v_gl = di("v_gl", [NTOK, 512], BF16)
        la_d = di("la", [NTOK, 512])
        ggT = di("ggT", [512, NTOK], BF16)
        pf_la = di("pf_la", [2, NPF * 128, 256])
        pf_k = di("pf_k", [2, NPF * 128, 256], BF16)
        pf_v = di("pf_v", [2, NPF * 128, 512], BF16)
        trim = di("trim", [128, 6 * 128])
        qT_na = di("qT_na", [512, NTOK], BF16)
        kT_na = di("kT_na", [512, NA_NTOK], BF16)
        v_na = di("v_na", [NA_NTOK, 512], BF16)
        nbias = di("nbias", [128, 8 * 7 * 128])
        nmask = di("nmask", [128, 5 * 7 * 128])
        gates = di("gates", [3072, NTOK], BF16)
        w_br = [di("w_br%d" % i, [512, D]) for i in range(3)]
        w_out = di("w_out", [D, D])
        w_ff1 = di("w_ff1", [D, 4 * D])
        w_ff2 = di("w_ff2", [4 * D, D])
        xo = S.dram("xo", [D, NTOK], F32, "ExternalOutput")
        ys = [S.dram("y%d" % i, [512, NTOK], BF16, "Internal") for i in range(3)]
        wbr_b = [S.dram("wbrb%d" % i, [128, 4 * D], BF16, "Internal") for i in range(3)]
        wout_b = S.dram("woutb", [128, KC * D], BF16, "Internal")
        w1_b = S.dram("w1b", [8, 128, KC * 512], BF16, "Internal")
        w2_b = S.dram("w2b", [8, 128, 32 * 128], BF16, "Internal")

        cst = S.sbuf("cst", [128, 2], F32)
        S.op("pool", lambda e: e.memset(cst[:, 0:1], EPS), [], [cst])
        S.op("pool", lambda e: e.memset(cst[:, 1:2], 1.0), [], [cst])
        onesb = S.sbuf("onesb", [128, 128], BF16)
        S.op("pool", lambda e: e.memset(onesb[:], 1.0), [], [onesb])
        ones32 = S.sbuf("ones32", [128, 128], F32)
        S.op("pool", lambda e: e.memset(ones32[:], 1.0), [], [ones32])
        o128 = S.sbuf("o128", [128, 128], BF16)
        S.op("pool", lambda e: e.memset(o128[:], 1.0 / 128), [], [o128])
        o1024 = S.sbuf("o1024", [128, 128], BF16)
        S.op("pool", lambda e: e.memset(o1024[:], 1.0 / 1024), [], [o1024])
        modT = S.sbuf("modT", [128, 96], F32)
        load(S, modT, modT[:], modT_d, modT_d[:, :])
        n2gs = S.sbuf("n2gs", [128, KC], F32)
        load(S, n2gs, n2gs[:], n2g, n2g[:, :])
        lt = S.sbuf("lt", [128, 256], F32)
        load(S, lt, lt[:], lamtab, lamtab[:, :])
        lcs = S.sbuf("lcs", [128, 2], F32)
        load(S, lcs, lcs[:], lconst, lconst[:, :])
        gv = S.sbuf("gv", [128, 2], F32)
        load(S, gv, gv[:], gvec, gvec[:, :])
        A2 = S.sbuf("A2", [128, KC * 2], F32)
        for kc in range(KC):
            S.op("dve", lambda e, kc=kc: e.tensor_scalar(
                out=A2[:, kc * 2:kc * 2 + 2], in0=modT[:, (32 + kc) * 2:(32 + kc) * 2 + 2], scalar1=1.0,
                scalar2=n2gs[:, kc:kc + 1], op0=ALU.add, op1=ALU.mult), [modT, n2gs], [A2])
        lw = S.sbuf("lw", [128, 128], F32)
        lsum = S.sbuf("lsum", [128, 8], F32)
        S.op("dve", lambda e: e.tensor_tensor(out=lw[:, 0:64], in0=lt[:, 0:64], in1=lt[:, 64:128], op=ALU.mult), [lt], [lw])
        S.op("dve", lambda e: e.tensor_tensor(out=lw[:, 64:128], in0=lt[:, 128:192], in1=lt[:, 192:256], op=ALU.mult), [lt, lw], [lw])
        S.op("dve", lambda e: e.reduce_sum(out=lsum[:, 0:1], in_=lw[:, 0:64], axis=AX.X), [lw], [lsum])
        S.op("dve", lambda e: e.reduce_sum(out=lsum[:, 1:2], in_=lw[:, 64:128], axis=AX.X), [lw, lsum], [lsum])
        S.op("act", lambda e: e.activation(out=lsum[:, 2:4], in_=lsum[:, 0:2], func=AF.Exp), [lsum], [lsum])
        S.op("dve", lambda e: e.tensor_tensor(out=lsum[:, 4:5], in0=lsum[:, 3:4], in1=lsum[:, 2:3], op=ALU.subtract), [lsum], [lsum])
        S.op("dve", lambda e: e.tensor_tensor(out=lsum[:, 5:6], in0=lsum[:, 4:5], in1=lcs[:, 0:1], op=ALU.subtract), [lsum, lcs], [lsum])
        S.op("dve", lambda e: e.tensor_tensor(out=lsum[:, 6:7], in0=gv[:, 0:1], in1=lcs[:, 1:2], op=ALU.mult), [lsum, gv, lcs], [lsum])
        neglam = lsum[:, 5:6]
        gsub = lsum[:, 6:7]

        def rstd_from(ps, n, pool_tag):
            sq_ = S.get(pool_tag)
            S.op("act", lambda e: e.activation(out=sq_[:, 0:n], in_=ps[:, 0:n], func=AF.Sqrt, bias=cst[:, 0:1], scale=1.0),
                 [ps, cst], [sq_])
            r_ = S.get(pool_tag)
            S.op("dve", lambda e: e.reciprocal(out=r_[:, 0:n], in_=sq_[:, 0:n]), [sq_], [r_])
            return r_

        with contextlib.ExitStack() as es2:
            S.es = es2
            S.pool("stage", 2, [128, KC, 512], F32)
            S.pool("wb", 2, [128, KC, 512], BF16)

            def cast_block(src_t, src_ap, nk, dsts):
                st = S.get("stage")
                load(S, st, st[:, 0:nk, :], src_t, src_ap)
                wb = S.get("wb")
                for kc in range(nk):
                    en = "pool" if kc % 2 == 0 else "dve"
                    S.op(en, lambda e, kc=kc: e.tensor_copy(out=wb[:, kc, :], in_=st[:, kc, :]), [st], [wb])
                for (dt_, dap, sap) in dsts(wb):
                    store(S, dt_, dap, wb, sap)

            for i in range(3):
                v = w_br[i].t.rearrange("(kc p) c -> p kc c", p=128)
                dv = wbr_b[i].t.rearrange("p (kc c) -> p kc c", kc=4)
                for hb in range(2):
                    cast_block(w_br[i], v[:, :, hb * 512:(hb + 1) * 512], 4,
                               lambda wb, dv=dv, hb=hb, i=i: [(wbr_b[i], dv[:, :, hb * 512:(hb + 1) * 512], wb[:, 0:4, :])])
            v = w_out.t.rearrange("(kc p) c -> p kc c", p=128)
            dv = wout_b.t.rearrange("p (kc c) -> p kc c", kc=KC)
            for hb in range(2):
                cast_block(w_out, v[:, :, hb * 512:(hb + 1) * 512], KC,
                           lambda wb, dv=dv, hb=hb: [(wout_b, dv[:, :, hb * 512:(hb + 1) * 512], wb[:, :, :])])
            v = w_ff1.t.rearrange("(kc p) c -> p kc c", p=128)
            for fb in range(8):
                dv = w1_b.t[fb].rearrange("p (kc c) -> p kc c", kc=KC)
                cast_block(w_ff1, v[:, :, fb * 512:(fb + 1) * 512], KC,
                           lambda wb, dv=dv: [(w1_b, dv, wb[:, :, :])])
            v = w_ff2.t.rearrange("(kc p) c -> p kc c", p=128)
            for kg in range(4):
                for hb in range(2):
                    def dsts(wb, kg=kg, hb=hb):
                        r = []
                        for j in range(4):
                            fo = hb * 4 + j
                            dv = w2_b.t[fo].rearrange("p (kc c) -> p kc c", kc=32)
                            r.append((w2_b, dv[:, kg * 8:(kg + 1) * 8, :], wb[:, :, j * 128:(j + 1) * 128]))
                        return r
                    cast_block(w_ff2, v[:, kg * 8:(kg + 1) * 8, hb * 512:(hb + 1) * 512], KC, dsts)
            barrier(S)

        with contextlib.ExitStack() as es2:
            S.es = es2
            S.pool("kT", 2, [128, NKT * 128], BF16)
            S.pool("V", 2, [128, NKT * 128], BF16)
            S.pool("q", 2, [128, NTOK], BF16)
            S.pool("pT", 4, [128, 512], BF16)
            S.pool("f32a", 4, [128, 512], F32)
            S.pool("f32b", 4, [128, 512], F32)
            S.pool("osub", 4, [128, 512], F32)
            S.pool("saccd", 2, [128, 512], F32)
            S.pool("saccp", 2, [128, 512], F32)
            S.pool("bfa", 2, [128, 512], BF16)
            S.pool("bfo", 3, [128, 512], BF16)
            S.pool("st", 3, [128, 512], F32, space="psum")
            S.pool("acc", 4, [128, 512], F32, space="psum")
            S.pool("misc", 1, [128, 512], F32, space="psum")
            for h in range(4):
                kT = S.get("kT")
                load(S, kT, kT[:], kT_da, kT_da[h * 128:(h + 1) * 128, :])
                V = S.get("V")
                load(S, V, V[:], v_da, v_da[h])
                q = S.get("q")
                load(S, q, q[:], qT_da, qT_da[h * 128:(h + 1) * 128, :])
                for ti, (t0, n) in enumerate(TT):
                    keys = list(range(NKT)) if ti < 8 else [128, 129]
                    osub = []
                    for sub in range(2):
                        p0 = sub * 64
                        accO = S.get("acc")
                        accS = S.get("acc")
                        sacc_d = S.get("saccd")
                        sacc_p = S.get("saccp")
                        sts = {}

                        def qk(j):
                            st = S.get("st")
                            S.op("pe", lambda e: e.matmul(st[:, 0:n], kT[p0:p0 + 64, j * 128:(j + 1) * 128],
                                                          q[p0:p0 + 64, t0:t0 + n], start=True, stop=True), [kT, q], [st])
                            sts[j] = st

                        LA_ = 2
                        for jj in range(min(LA_, len(keys))):
                            qk(keys[jj])
                        for idx, j in enumerate(keys):
                            st = sts.pop(j)
                            pT = S.get("pT")
                            S.op("act", lambda e: e.activation(out=pT[:, 0:n], in_=st[:, 0:n], func=AF.Exp), [st], [pT])
                            if idx + LA_ < len(keys):
                                qk(keys[idx + LA_])
                            first, last = idx == 0, idx == len(keys) - 1
                            S.op("pe", lambda e: e.matmul(accO[:, 0:n], V[:, j * 128:(j + 1) * 128], pT[:, 0:n],
                                                          start=first, stop=last), [V, pT], [accO])
                            en_, sa_ = ("dve", sacc_d) if idx % 2 == 0 else ("pool", sacc_p)
                            if idx < 2:
                                S.op(en_, lambda e: e.tensor_copy(out=sa_[:, 0:n], in_=pT[:, 0:n]), [pT], [sa_])
                            else:
                                S.op(en_, lambda e: e.tensor_tensor(out=sa_[:, 0:n], in0=sa_[:, 0:n], in1=pT[:, 0:n], op=ALU.add),
                                     [sa_, pT], [sa_])
                        S.op("pe", lambda e: e.matmul(accS[:, 0:n], ones32[:], sacc_d[:, 0:n], start=True, stop=False),
                             [ones32, sacc_d], [accS])
                        S.op("pe", lambda e: e.matmul(accS[:, 0:n], ones32[:], sacc_p[:, 0:n], start=False, stop=True),
                             [ones32, sacc_p], [accS])
                        rec = S.get("f32a")
                        S.op("dve", lambda e: e.reciprocal(out=rec[:, 0:n], in_=accS[:, 0:n]), [accS], [rec])
                        os_ = S.get("osub")
                        S.op("dve", lambda e: e.tensor_tensor(out=os_[:, 0:n], in0=accO[:, 0:n], in1=rec[:, 0:n], op=ALU.mult),
                             [accO, rec], [os_])
                        osub.append(os_)
                    o = S.get("f32b")
                    S.op("dve", lambda e: e.scalar_tensor_tensor(out=o[:, 0:n], in0=osub[1][:, 0:n], scalar=neglam,
                                                                 in1=osub[0][:, 0:n], op0=ALU.mult, op1=ALU.add),
                         [osub[0], osub[1], lsum], [o])
                    sq = S.get("bfa")
                    S.op("act", lambda e: e.activation(out=sq[:, 0:n], in_=o[:, 0:n], func=AF.Square), [o], [sq])
                    ms = S.get("misc")
                    S.op("pe", lambda e: e.matmul(ms[:, 0:n], o128[:], sq[:, 0:n], start=True, stop=True), [o128, sq], [ms])
                    rstd = rstd_from(ms, n, "f32a")
                    y = S.get("bfo")
                    S.op("dve", lambda e: e.scalar_tensor_tensor(out=y[:, 0:n], in0=o[:, 0:n], scalar=gsub, in1=rstd[:, 0:n],
                                                                 op0=ALU.mult, op1=ALU.mult), [o, lsum, rstd], [y])
                    store(S, ys[0], ys[0][h * 128:(h + 1) * 128, t0:t0 + n], y, y[:, 0:n])
            barrier(S)

        with contextlib.ExitStack() as es2:
            S.es = es2
            tri32 = S.sbuf("tri32", [128, 768], F32)
            load(S, tri32, tri32[:], trim, trim[:, :])
            onec = S.sbuf("onec", [128, 1], F32)
            S.op("pool", lambda e: e.memset(onec[:], 1.0), [], [onec])
            og = [S.sbuf("og%d" % h, [128, NTOK], F32) for h in range(4)]
            Sst = [S.sbuf("Sst%d" % p, [128, 256], F32) for p in range(2)]
            Sb = [S.sbuf("Sb%d" % p, [128, 256], BF16) for p in range(2)]
            S.pool("la", 3, [128, 256], F32)
            S.pool("k", 3, [128, 256], BF16)
            S.pool("v", 3, [128, 512], BF16)
            S.pool("qk", 3, [128, 4, 128], BF16)
            S.pool("ekd", 2, [128, 256], F32)
            S.pool("kd", 2, [128, 256], BF16)
            S.pool("eqk", 4, [128, 128], F32)
            S.pool("qeke", 4, [128, 128], BF16)
            S.pool("aTm", 3, [128, 128], BF16)
            S.pool("dcol", 4, [128, 1], F32)
            S.pool("f32a", 4, [128, 512], F32)
            S.pool("bfa", 2, [128, 512], BF16)
            S.pool("bfo", 3, [128, 512], BF16)
            S.pool("gg", 2, [128, 512], BF16)
            S.pool("pex", 1, [128, 512], F32, space="psum")
            S.pool("pcum", 2, [128, 128], F32, space="psum")
            S.pool("paT", 2, [128, 128], F32, space="psum")
            S.pool("po", 2, [128, 128], F32, space="psum")
            S.pool("pds", 1, [128, 256], F32, space="psum")

            def gla_tile(d, la_t, la_ap, k_t, k_ap, v_t, v_ap, full, tcol):
                b = d * 384
                Tstr, Tincl, Mask = tri32[:, b:b + 128], tri32[:, b + 128:b + 256], tri32[:, b + 256:b + 384]
                la = S.get("la")
                load(S, la, la[:], la_t, la_ap)
                kk = S.get("k")
                load(S, kk, kk[:], k_t, k_ap)
                vv = S.get("v")
                load(S, vv, vv[:], v_t, v_ap)
                pex = S.get("pex")
                S.op("pe", lambda e: e.matmul(pex[:, 0:256], Tstr, la[:], start=True, stop=True), [tri32, la], [pex])
                ekd = S.get("ekd")
                S.op("act", lambda e: e.activation(out=ekd[:], in_=pex[:, 0:256], func=AF.Exp), [pex], [ekd])
                kd = S.get("kd")
                S.op("dve", lambda e: e.tensor_tensor(out=kd[:], in0=kk[:], in1=ekd[:], op=ALU.mult), [kk, ekd], [kd])
                if full:
                    qk = S.get("qk")
                    load(S, qk, qk[:, 0:2, :], qT_gl, qT_gl.t.rearrange("(pr p) t -> p pr t", p=128)[:, :, tcol:tcol + 128])
                    load(S, qk, qk[:, 2:4, :], kT_gl, kT_gl.t.rearrange("(pr p) t -> p pr t", p=128)[:, :, tcol:tcol + 128])
                for pr in range(2):
                    dcol = S.get("dcol")
                    if full:
                        pc = S.get("pcum")
                        S.op("pe", lambda e: e.matmul(pc[:], la[:, pr * 128:(pr + 1) * 128], Tincl, start=True, stop=True),
                             [la, tri32], [pc])
                        eq = S.get("eqk")
                        S.op("act", lambda e: e.activation(out=eq[:], in_=pc[:], func=AF.Exp), [pc], [eq])
                        ek = S.get("eqk")
                        S.op("act", lambda e: e.activation(out=ek[:], in_=pc[:], func=AF.Exp, scale=-1.0), [pc], [ek])
                        qe = S.get("qeke")
                        S.op("dve", lambda e: e.tensor_tensor(out=qe[:], in0=qk[:, pr, :], in1=eq[:], op=ALU.mult), [qk, eq], [qe])
                        ke = S.get("qeke")
                        S.op("pool", lambda e: e.tensor_tensor(out=ke[:], in0=qk[:, 2 + pr, :], in1=ek[:], op=ALU.mult), [qk, ek], [ke])
                        lastc = 127 if d == 0 else 0
                        S.op("act", lambda e: e.activation(out=dcol[:], in_=eq[:, lastc:lastc + 1], func=AF.Copy), [eq], [dcol])
                        for hl in range(2):
                            h = pr * 2 + hl
                            p0 = hl * 64
                            pa = S.get("paT")
                            S.op("pe", lambda e: e.matmul(pa[:], ke[p0:p0 + 64, :], qe[p0:p0 + 64, :], start=True, stop=True),
                                 [ke, qe], [pa])
                            am = S.get("aTm")
                            S.op("dve", lambda e: e.tensor_tensor(out=am[:], in0=pa[:], in1=Mask, op=ALU.mult), [pa, tri32], [am])
                            po = S.get("po")
                            S.op("pe", lambda e: e.matmul(po[:], vv[:, h * 128:(h + 1) * 128], am[:], start=True, stop=False),
                                 [vv, am], [po])
                            S.op("pe", lambda e: e.matmul(po[:], Sb[pr][p0:p0 + 64, hl * 128:(hl + 1) * 128], qe[p0:p0 + 64, :],
                                                          start=False, stop=True), [Sb[pr], qe], [po])
                            if d == 0:
                                S.op("act", lambda e: e.activation(out=og[h][:, tcol:tcol + 128], in_=po[:], func=AF.Copy),
                                     [po], [og[h]])
                            else:
                                S.op("dve", lambda e: e.tensor_tensor(out=og[h][:, tcol:tcol + 128], in0=po[:],
                                                                      in1=og[h][:, tcol:tcol + 128], op=ALU.add), [po, og[h]], [og[h]])
                    else:
                        pc = S.get("pcum")
                        S.op("pe", lambda e: e.matmul(pc[:, 0:1], la[:, pr * 128:(pr + 1) * 128], onec[:], start=True, stop=True),
                             [la, onec], [pc])
                        S.op("act", lambda e: e.activation(out=dcol[:], in_=pc[:, 0:1], func=AF.Exp), [pc], [dcol])
                    pds = S.get("pds")
                    S.op("pe", lambda e: e.matmul(pds[:], kd[:, pr * 128:(pr + 1) * 128], vv[:, pr * 256:(pr + 1) * 256],
                                                  start=True, stop=True), [kd, vv], [pds])
                    S.op("dve", lambda e: e.scalar_tensor_tensor(out=Sst[pr][:], in0=Sst[pr][:], scalar=dcol[:, 0:1], in1=pds[:],
                                                                 op0=ALU.mult, op1=ALU.add), [Sst[pr], dcol, pds], [Sst[pr]])
                    S.op("pool", lambda e: e.tensor_copy(out=Sb[pr][:], in_=Sst[pr][:]), [Sst[pr]], [Sb[pr]])

            for d in range(2):
                for pr in range(2):
                    S.op("pool", lambda e: e.memset(Sst[pr][:], 0.0), [], [Sst[pr]])
                    S.op("pool", lambda e: e.memset(Sb[pr][:], 0.0), [], [Sb[pr]])
                own = lambda t: (la_d, la_d[t * 128:(t + 1) * 128, d * 256:(d + 1) * 256], k_gl, k_gl[t * 128:(t + 1) * 128, :],
                                 v_gl, v_gl[t * 128:(t + 1) * 128, :])
                pre = lambda t: (pf_la, pf_la[d, t * 128:(t + 1) * 128, :], pf_k, pf_k[d, t * 128:(t + 1) * 128, :],
                                 pf_v, pf_v[d, t * 128:(t + 1) * 128, :])
                order = (lambda r: list(r)) if d == 0 else (lambda r: list(r)[::-1])
                for t in order(range(32, 34)):
                    gla_tile(d, *own(t), True, t * 128)
                for t in order(range(NPF)):
                    gla_tile(d, *pre(t), False, 0)
                for t in order(range(32)):
                    gla_tile(d, *own(t), True, t * 128)
            for h in range(4):
                for (t0, n) in TT:
                    sq = S.get("bfa")
                    S.op("act", lambda e: e.activation(out=sq[:, 0:n], in_=og[h][:, t0:t0 + n], func=AF.Square), [og[h]], [sq])
                    ms = S.get("pex")
                    S.op("pe", lambda e: e.matmul(ms[:, 0:n], o128[:], sq[:, 0:n], start=True, stop=True), [o128, sq], [ms])
                    rstd = rstd_from(ms, n, "f32a")
                    g = S.get("gg")
                    load(S, g, g[:, 0:n], ggT, ggT[h * 128:(h + 1) * 128, t0:t0 + n])
                    y0 = S.get("f32a")
                    S.op("dve", lambda e: e.scalar_tensor_tensor(out=y0[:, 0:n], in0=og[h][:, t0:t0 + n], scalar=gv[:, 1:2],
                                                                 in1=rstd[:, 0:n], op0=ALU.mult, op1=ALU.mult), [og[h], gv, rstd], [y0])
                    y = S.get("bfo")
                    S.op("pool", lambda e: e.tensor_tensor(out=y[:, 0:n], in0=y0[:, 0:n], in1=g[:, 0:n], op=ALU.mult), [y0, g], [y])
                    store(S, ys[1], ys[1][h * 128:(h + 1) * 128, t0:t0 + n], y, y[:, 0:n])
            barrier(S)

        with contextlib.ExitStack() as es2:
            S.es = es2
            kTn = S.sbuf("kTn", [128, 4, NA_NTOK], BF16)
            load(S, kTn, kTn[:], kT_na, kT_na.t.rearrange("(c p) t -> p c t", p=128))
            Vn = S.sbuf("Vn", [128, NA_NTOK // 128, 512], BF16)
            load(S, Vn, Vn[:], v_na, v_na.t.rearrange("(t p) c -> p t c", p=128))
            nb = S.sbuf("nb", [128, 56 * 128], F32)
            load(S, nb, nb[:], nbias, nbias[:, :])
            nm = S.sbuf("nm", [128, 35 * 128], F32)
            load(S, nm, nm[:], nmask, nmask[:, :])
            S.pool("q", 2, [128, NTOK], BF16)
            S.pool("s1", 3, [128, 128], F32)
            S.pool("s2", 3, [128, 128], F32)
            S.pool("pT", 4, [128, 128], BF16)
            S.pool("rec", 2, [128, 128], F32)
            S.pool("yt", 3, [128, 128], BF16)
            S.pool("st", 3, [128, 128], F32, space="psum")
            S.pool("acc", 4, [128, 128], F32, space="psum")
            for c in range(4):
                q = S.get("q")
                load(S, q, q[:], qT_na, qT_na[c * 128:(c + 1) * 128, :])
                for qt in range(34):
                    tcol = qt * 128
                    if qt < 32:
                        units = [(qt + o, o) for o in range(7)] + [(38, None), (39, None)]
                        cls = {0: 0, 1: 1, 30: 3, 31: 4}.get(qt, 2)
                    else:
                        units = [(38, None), (39, None)]
                        cls = 2
                    yt = S.get("yt")
                    for hl in range(2):
                        h = 2 * c + hl
                        p0 = hl * 64
                        accO = S.get("acc")
                        accS = S.get("acc")
                        nsts = {}

                        def nqk(ui_):
                            kt_ = units[ui_][0]
                            st_ = S.get("st")
                            S.op("pe", lambda e: e.matmul(st_[:], kTn[p0:p0 + 64, c, kt_ * 128:(kt_ + 1) * 128],
                                                          q[p0:p0 + 64, tcol:tcol + 128], start=True, stop=True), [kTn, q], [st_])
                            nsts[ui_] = st_

                        NLA = 2
                        for u0 in range(min(NLA, len(units))):
                            nqk(u0)
                        for ui, (kt, o) in enumerate(units):
                            st = nsts.pop(ui)
                            if ui + NLA < len(units):
                                nqk(ui + NLA)
                            pT = S.get("pT")
                            if o is not None:
                                s1 = S.get("s1")
                                bo = (h * 7 + o) * 128
                                S.op("dve", lambda e: e.tensor_tensor(out=s1[:], in0=st[:], in1=nb[:, bo:bo + 128], op=ALU.add),
                                     [st, nb], [s1])
                                s2 = S.get("s2")
                                mo = (cls * 7 + o) * 128
                                S.op("pool", lambda e: e.tensor_tensor(out=s2[:], in0=s1[:], in1=nm[:, mo:mo + 128], op=ALU.add),
                                     [s1, nm], [s2])
                                S.op("act", lambda e: e.activation(out=pT[:], in_=s2[:], func=AF.Exp), [s2], [pT])
                            else:
                                S.op("act", lambda e: e.activation(out=pT[:], in_=st[:], func=AF.Exp), [st], [pT])
                            first, last = ui == 0, ui == len(units) - 1
                            S.op("pe", lambda e: e.matmul(accO[:], Vn[:, kt, c * 128:(c + 1) * 128], pT[:], start=first, stop=last),
                                 [Vn, pT], [accO])
                            S.op("pe", lambda e: e.matmul(accS[:], onesb[:], pT[:], start=first, stop=last), [onesb, pT], [accS])
                        rec = S.get("rec")
                        S.op("dve", lambda e: e.reciprocal(out=rec[p0:p0 + 64, :], in_=accS[p0:p0 + 64, :]), [accS], [rec])
                        S.op("dve", lambda e: e.tensor_tensor(out=yt[p0:p0 + 64, :], in0=accO[p0:p0 + 64, :], in1=rec[p0:p0 + 64, :],
                                                              op=ALU.mult), [accO, rec], [yt])
                    store(S, ys[2], ys[2][c * 128:(c + 1) * 128, tcol:tcol + 128], yt, yt[:])
            barrier(S)

        with contextlib.ExitStack() as es2:
            S.es = es2
            wbr = S.sbuf("wbr", [128, 12, D], BF16)
            for i in range(3):
                load(S, wbr, wbr[:, i * 4:(i + 1) * 4, :], wbr_b[i], wbr_b[i].t.rearrange("p (kc c) -> p kc c", kc=4))
            wo = S.sbuf("wo", [128, KC, D], BF16)
            load(S, wo, wo[:], wout_b, wout_b.t.rearrange("p (kc c) -> p kc c", kc=KC))
            S.pool("x", 1, [128, KC, 512], F32)
            S.pool("y", 1, [128, 12, 512], BF16)
            S.pool("g3", 2, [128, 3, 512], BF16)
            S.pool("m", 1, [128, KC, 512], BF16)
            S.pool("sq8", 1, [128, KC, 512], BF16)
            S.pool("h2", 1, [128, KC, 512], BF16)
            S.pool("a", 1, [128, 32, 512], BF16)
            S.pool("w1", 2, [128, KC, 512], BF16)
            S.pool("w2", 2, [128, 32, 128], BF16)
            S.pool("f32a", 4, [128, 512], F32)
            S.pool("f32b", 4, [128, 512], F32)
            S.pool("xo", 3, [128, 512], F32)
            S.pool("ps", 8, [128, 512], F32, space="psum")
            gv3 = gates.t.rearrange("(br fo p) t -> p br fo t", br=3, fo=8)
            xv = xT.t.rearrange("(kc p) t -> p kc t", p=128)
            for ti, (t0, n) in enumerate(TT):
                col = 0 if ti < 8 else 1
                mcol = lambda ch: modT[:, ch * 2 + col:ch * 2 + col + 1]
                x = S.get("x")
                load(S, x, x[:, :, 0:n], xT, xv[:, :, t0:t0 + n])
                y = S.get("y")
                for br in range(3):
                    load(S, y, y[:, br * 4:(br + 1) * 4, 0:n], ys[br], ys[br].t.rearrange("(kc p) t -> p kc t", p=128)[:, :, t0:t0 + n])
                m = S.get("m")
                for fo in range(8):
                    g3 = S.get("g3")
                    load(S, g3, g3[:, :, 0:n], gates, gv3[:, :, fo, t0:t0 + n])
                    tmps = []
                    for br in range(3):
                        ps = S.get("ps")
                        for kc in range(4):
                            S.op("pe", lambda e, kc=kc: e.matmul(ps[:, 0:n], wbr[:, br * 4 + kc, fo * 128:(fo + 1) * 128],
                                                                  y[:, br * 4 + kc, 0:n], start=(kc == 0), stop=(kc == 3)), [wbr, y], [ps])
                        tb = S.get("f32a")
                        S.op("dve", lambda e: e.tensor_tensor(out=tb[:, 0:n], in0=ps[:, 0:n], in1=g3[:, br, 0:n], op=ALU.mult),
                             [ps, g3], [tb])
                        tmps.append(tb)
                    s01 = S.get("f32b")
                    S.op("pool", lambda e: e.tensor_tensor(out=s01[:, 0:n], in0=tmps[0][:, 0:n], in1=tmps[1][:, 0:n], op=ALU.add),
                         [tmps[0], tmps[1]], [s01])
                    S.op("pool", lambda e: e.tensor_tensor(out=m[:, fo, 0:n], in0=s01[:, 0:n], in1=tmps[2][:, 0:n], op=ALU.add),
                         [s01, tmps[2]], [m])
                for fo in range(8):
                    ps = S.get("ps")
                    for kc in range(KC):
                        S.op("pe", lambda e, kc=kc: e.matmul(ps[:, 0:n], wo[:, kc, fo * 128:(fo + 1) * 128], m[:, kc, 0:n],
                                                              start=(kc == 0), stop=(kc == KC - 1)), [wo, m], [ps])
                    S.op("dve", lambda e: e.scalar_tensor_tensor(out=x[:, fo, 0:n], in0=ps[:, 0:n], scalar=mcol(16 + fo),
                                                                 in1=x[:, fo, 0:n], op0=ALU.mult, op1=ALU.add), [ps, modT, x], [x])
                sq = S.get("sq8")
                S.op("act", lambda e: e.activation(out=sq[:, :, 0:n], in_=x[:, :, 0:n], func=AF.Square), [x], [sq])
                ps = S.get("ps")
                for kc in range(KC):
                    S.op("pe", lambda e, kc=kc: e.matmul(ps[:, 0:n], o1024[:], sq[:, kc, 0:n], start=(kc == 0), stop=(kc == KC - 1)),
                         [o1024, sq], [ps])
                rstd = rstd_from(ps, n, "f32a")
                h2 = S.get("h2")
                for kc in range(KC):
                    tmp = S.get("f32b")
                    S.op("dve", lambda e, kc=kc: e.scalar_tensor_tensor(
                        out=tmp[:, 0:n], in0=x[:, kc, 0:n], scalar=A2[:, kc * 2 + col:kc * 2 + col + 1], in1=rstd[:, 0:n],
                        op0=ALU.mult, op1=ALU.mult), [x, A2, rstd], [tmp])
                    S.op("act", lambda e, kc=kc: e.activation(out=h2[:, kc, 0:n], in_=tmp[:, 0:n], func=AF.Identity,
                                                              bias=mcol(24 + kc), scale=1.0), [tmp, modT], [h2])
                a = S.get("a")
                for fb in range(8):
                    w1 = S.get("w1")
                    load(S, w1, w1[:], w1_b, w1_b.t[fb].rearrange("p (kc c) -> p kc c", kc=KC))
                    for j in range(4):
                        f = fb * 4 + j
                        ps = S.get("ps")
                        for kc in range(KC):
                            S.op("pe", lambda e, kc=kc: e.matmul(ps[:, 0:n], w1[:, kc, j * 128:(j + 1) * 128], h2[:, kc, 0:n],
                                                                  start=(kc == 0), stop=(kc == KC - 1)), [w1, h2], [ps])
                        r = S.get("f32a")
                        S.op("act", lambda e: e.activation(out=r[:, 0:n], in_=ps[:, 0:n], func=AF.Relu), [ps], [r])
                        S.op("pool", lambda e: e.tensor_tensor(out=a[:, f, 0:n], in0=r[:, 0:n], in1=r[:, 0:n], op=ALU.mult), [r], [a])
                for fo in range(8):
                    w2 = S.get("w2")
                    load(S, w2, w2[:], w2_b, w2_b.t[fo].rearrange("p (kc c) -> p kc c", kc=32))
                    ps = S.get("ps")
                    for f in range(32):
                        S.op("pe", lambda e, f=f: e.matmul(ps[:, 0:n], w2[:, f, :], a[:, f, 0:n], start=(f == 0), stop=(f == 31)),
                             [w2, a], [ps])
                    xt = S.get("xo")
                    S.op("dve", lambda e: e.scalar_tensor_tensor(out=xt[:, 0:n], in0=ps[:, 0:n], scalar=mcol(40 + fo),
                                                                 in1=x[:, fo, 0:n], op0=ALU.mult, op1=ALU.add), [ps, modT, x], [xt])
                    store(S, xo, xo[fo * 128:(fo + 1) * 128, t0:t0 + n], xt, xt[:, 0:n])
            S.finish()
            barrier(S)
        S.es = es
    return nc


def tri_consts():
    i = np.arange(128)
    sp, s = i[:, None], i[None, :]
    mats = [(sp > s), (sp <= s), (sp <= s), (sp < s), (sp >= s), (sp >= s)]
    return np.concatenate([m.astype(np.float32) for m in mats], axis=1)


def na_bias_table(rpb):
    ka = np.arange(128)
    a, kc = ka // 64, ka % 64
    bq, cq = ka // 64, ka % 64
    out = np.zeros((128, 8, 7, 128), np.float32)
    for o in range(7):
        rel_row = (-6 + 2 * o + a)[:, None] - bq[None, :] + 7
        rel_col = kc[:, None] - cq[None, :] + 15
        ok = (rel_row >= 0) & (rel_row <= 14) & (rel_col >= 0) & (rel_col <= 30)
        rr = np.clip(rel_row, 0, 14)
        rc = np.clip(rel_col, 0, 30)
        for h in range(8):
            out[:, h, o, :] = np.where(ok, rpb[h][rr, rc], 0.0)
    return out.reshape(128, -1)


def na_mask_table(qtr):
    ka = np.arange(128)
    a, kc = ka // 64, ka % 64
    bq, cq = ka // 64, ka % 64
    out = np.zeros((128, 5, 7, 128), np.float32)
    for cls, qt in enumerate((0, 1, 2, 30, 31)):
        Rq = qtr * 64 + 2 * qt
        R = Rq + bq
        rs = np.clip(R - 4, 0, 256 - 8)
        cs = np.clip(cq - 8, 0, GRID_W - 16)
        for o in range(7):
            kr = (Rq - 6 + 2 * o + a)[:, None]
            ok = (kr >= rs[None, :]) & (kr < rs[None, :] + 8) & (kc[:, None] >= cs[None, :]) & (kc[:, None] < cs[None, :] + 16)
            out[:, cls, o, :] = np.where(ok, 0.0, -30000.0)
    return out.reshape(128, -1)


def lb_inputs(l, core, xT_core, ra, inp):
    b, qtr = core // 4, core % 4
    grp = [ra[4 * b + j] for j in range(4)]
    me = ra[core]
    bf = ml_dtypes.bfloat16
    kT_da = np.concatenate([np.asarray(g["kT_da"])[:, :NLAT] for g in grp] + [np.asarray(me["kT_da"])[:, NLAT:]], axis=1)
    v_all = np.concatenate([np.asarray(g["v_da"])[:NLAT] for g in grp] + [np.asarray(me["v_da"])[NLAT:]], axis=0)
    v_da = np.ascontiguousarray(v_all.reshape(NKT, 128, 4, 128).transpose(2, 1, 0, 3).reshape(4, 128, NKT * 128))
    lam_init = 0.8 - 0.6 * math.exp(-0.3 * l)
    npf = NPF * 128
    pf_la = np.zeros((2, npf, 256), np.float32)
    pf_k = np.zeros((2, npf, 256), bf)
    pf_v = np.zeros((2, npf, 512), bf)
    nb_ = qtr
    if nb_:
        pf_la[0, :nb_ * NLAT] = np.concatenate([np.asarray(grp[j]["la"])[:NLAT, 0:256] for j in range(qtr)], 0)
        pf_k[0, :nb_ * NLAT] = np.concatenate([np.asarray(grp[j]["k_gl"])[:NLAT] for j in range(qtr)], 0)
        pf_v[0, :nb_ * NLAT] = np.concatenate([np.asarray(grp[j]["v_gl"])[:NLAT] for j in range(qtr)], 0)
    na_ = 3 - qtr
    if na_:
        pf_la[1, npf - na_ * NLAT:] = np.concatenate([np.asarray(grp[j]["la"])[:NLAT, 256:512] for j in range(qtr + 1, 4)], 0)
        pf_k[1, npf - na_ * NLAT:] = np.concatenate([np.asarray(grp[j]["k_gl"])[:NLAT] for j in range(qtr + 1, 4)], 0)
        pf_v[1, npf - na_ * NLAT:] = np.concatenate([np.asarray(grp[j]["v_gl"])[:NLAT] for j in range(qtr + 1, 4)], 0)
    kn_all = np.concatenate([np.asarray(g["kT_na"])[:, :NLAT] for g in grp], axis=1)
    vn_all = np.concatenate([np.asarray(g["v_na"])[:NLAT] for g in grp], axis=0)
    lo = (qtr * 64 - 6) * GRID_W
    hi = lo + NHALO
    kT_na = np.zeros((512, NA_NTOK), bf)
    v_na = np.zeros((NA_NTOK, 512), bf)
    s0, s1 = max(lo, 0), min(hi, SEQ)
    kT_na[:, s0 - lo:s1 - lo] = kn_all[:, s0:s1]
    v_na[s0 - lo:s1 - lo] = vn_all[s0:s1]
    kT_na[:, NHALO:] = np.asarray(me["kT_na"])[:, NLAT:]
    v_na[NHALO:] = np.asarray(me["v_na"])[NLAT:]
    return {
        "xT": xT_core, "modT": np.asarray(me["modT"]), "n2g": chunkT(inp["norm2_g"][l]),
        "qT_da": np.asarray(me["qT_da"]), "kT_da": np.ascontiguousarray(kT_da), "v_da": v_da,
        "lamtab": np.ascontiguousarray(np.tile(inp["da_lambda"][l].reshape(1, 256), (128, 1)).astype(np.float32)),
        "lconst": np.tile(np.array([[lam_init, 1.0 - lam_init]], np.float32), (128, 1)),
        "gvec": np.stack([inp["da_subln_g"][l], inp["gla_gn_g"][l]], 1).astype(np.float32),
        "qT_gl": np.asarray(me["qT_gl"]), "kT_gl": np.asarray(me["kT_gl"]), "k_gl": np.asarray(me["k_gl"]),
        "v_gl": np.asarray(me["v_gl"]), "la": np.asarray(me["la"]), "ggT": np.asarray(me["ggT"]),
        "pf_la": pf_la, "pf_k": pf_k, "pf_v": pf_v, "trim": tri_consts(),
        "qT_na": np.asarray(me["qT_na"]), "kT_na": kT_na, "v_na": v_na,
        "nbias": na_bias_table(np.asarray(inp["na_rpb"][l], np.float32)), "nmask": na_mask_table(qtr),
        "gates": np.asarray(me["gates"]),
        "w_br0": inp["w_br_da"][l], "w_br1": inp["w_br_gla"][l], "w_br2": inp["w_br_na"][l],
        "w_out": inp["w_out"][l], "w_ff1": inp["w_ff1"][l], "w_ff2": inp["w_ff2"][l],
    }


def run_lb(l, xTs, ra, inp):
    if "lb" not in _NC:
        _NC["lb"] = build_lb()
    in_maps = [lb_inputs(l, c, xTs[c], ra, inp) for c in range(NCORES)]
    res = run_bass_kernel_spmd(_NC["lb"], in_maps, core_ids=list(range(NCORES)))
    return res.results


def kernel(**inputs):
    inp = {k: np.asarray(v) for k, v in inputs.items()}
    xTs = make_xT(inp["x"], inp["ctx"])
    for l in range(2):
        ra = run_la(l, xTs, inp)
        rb = run_lb(l, xTs, ra, inp)
        xTs = [np.ascontiguousarray(np.asarray(rb[c]["xo"], dtype=np.float32)) for c in range(NCORES)]
    out = np.empty((2, SEQ, D), np.float32)
    for c in range(NCORES):
        b, qtr = c // 4, c % 4
        out[b, qtr * NLAT:(qtr + 1) * NLAT] = xTs[c][:, :NLAT].T
    return out
```

```python
import contextlib
import math
import numpy as np
import ml_dtypes
import concourse.bass as bass
import concourse.mybir as mybir
from concourse.bass_utils import run_bass_kernel_spmd

F32 = mybir.dt.float32
BF16 = mybir.dt.bfloat16
AF = mybir.ActivationFunctionType
ALU = mybir.AluOpType
AX = mybir.AxisListType

NCORES = 8
D = 1024
KC = 8
SEQ = 16384
NLAT = 4096
NCTX = 256
NTOK = NLAT + NCTX
GRID_W = 64
EPS = 1e-6
N_IN = 7712
TT = [(i * 512, 512) for i in range(8)] + [(4096, 256)]


class Tile:
    def __init__(self, t, name):
        self.t = t
        self.name = name
        self.w = {}
        self.r = {}
        self.sem = None
        self.semval = 0
        self.track = True

    def __getitem__(self, k):
        return self.t[k]


def _merge(dst, src):
    for k, v in src.items():
        if dst.get(k, 0) < v:
            dst[k] = v


class Sched:
    def __init__(self, nc, es):
        self.nc = nc
        self.es = es
        self.eng = {"pe": nc.tensor, "act": nc.scalar, "dve": nc.vector, "pool": nc.gpsimd, "sp": nc.sync}
        self.sem = {}
        self.cnt = {}
        for k in ("pe", "act", "dve", "pool"):
            self.sem[k] = es.enter_context(nc.semaphore("s_" + k))
            self.cnt[k] = 0
        self.seen = {k: {} for k in self.eng}
        self.pending = {}
        self.nsem = 0
        self.pools = {}

    def sbuf(self, name, shape, dtype):
        self.uid = getattr(self, "uid", 0) + 1
        return Tile(self.es.enter_context(self.nc.sbuf_tensor("sb%d_%s" % (self.uid, name), list(shape), dtype)),
                    "%s_%d" % (name, self.uid))

    def psum(self, name, shape, dtype=F32):
        self.uid = getattr(self, "uid", 0) + 1
        return Tile(self.es.enter_context(self.nc.psum_tensor("pp%d_%s" % (self.uid, name), list(shape), dtype)),
                    "%s_%d" % (name, self.uid))

    def dram(self, name, shape, dtype, kind):
        t = self.nc.dram_tensor(name, list(shape), dtype, kind=kind)
        tl = Tile(t.ap(), name)
        tl.track = False
        return tl

    def pool(self, tag, n, shape, dtype, space="sbuf"):
        mk = self.sbuf if space == "sbuf" else self.psum
        self.pools[tag] = [[mk("%s%d" % (tag, i), shape, dtype) for i in range(n)], 0]

    def get(self, tag):
        p = self.pools[tag]
        t = p[0][p[1] % len(p[0])]
        p[1] += 1
        return t

    def _wait(self, en, deps):
        seen = self.seen[en]
        for k, v in deps.items():
            if k == "pe" and en == "pe":
                continue
            if seen.get(k, 0) >= v:
                continue
            self.eng[en].wait_ge(self.sem[k], v)
            seen[k] = v

    def op(self, en, fn, reads=(), writes=()):
        deps = {}
        for t in reads:
            _merge(deps, t.w)
        for t in writes:
            _merge(deps, t.w)
            _merge(deps, t.r)
        self._wait(en, deps)
        ins = fn(self.eng[en])
        self.cnt[en] += 1
        ev = {en: self.cnt[en]}
        ins.then_inc(self.sem[en], 1)
        for t in reads:
            _merge(t.r, ev)
        for t in writes:
            t.w = dict(ev)
            t.r = {}
        return ins

    def dma(self, dst, dst_ap, src, src_ap, owner):
        if owner.sem is None:
            self.nsem += 1
            owner.sem = "d%d_%s" % (self.nsem, owner.name)
            self.sem[owner.sem] = self.es.enter_context(self.nc.semaphore(owner.sem))
        deps = {}
        if src.track:
            _merge(deps, src.w)
        if dst.track:
            _merge(deps, dst.w)
            _merge(deps, dst.r)
        self._wait("sp", deps)
        owner.semval += 16
        ev = {owner.sem: owner.semval}
        self.nc.sync.dma_start(out=dst_ap, in_=src_ap).then_inc(self.sem[owner.sem], 16)
        if src.track:
            _merge(src.r, ev)
        if dst.track:
            dst.w = dict(ev)
            dst.r = {}
        _merge(self.pending, ev)

    def finish(self):
        self._wait("sp", self.pending)


def load(S, dst, dst_ap, src, src_ap):
    S.dma(dst, dst_ap, src, src_ap, owner=dst)


def store(S, dst, dst_ap, src, src_ap):
    S.dma(dst, dst_ap, src, src_ap, owner=src)


C_DAQ, C_DAK, C_DAV = 0, 512, 1024
C_GQ, C_GK, C_GV, C_GG, C_GA = 1536, 1792, 2048, 2560, 3072
C_NQ, C_NK, C_NV = 3104, 3616, 4128
C_GATE = 4640

LA_OUT = [("qT_da", [512, NTOK], BF16), ("kT_da", [512, NTOK], BF16), ("v_da", [NTOK, 512], BF16),
          ("qT_gl", [256, NTOK], BF16), ("kT_gl", [256, NTOK], BF16), ("k_gl", [NTOK, 256], BF16),
          ("v_gl", [NTOK, 512], BF16), ("ggT", [512, NTOK], BF16), ("la", [NTOK, 512], F32),
          ("qT_na", [512, NTOK], BF16), ("kT_na", [512, NTOK], BF16), ("v_na", [NTOK, 512], BF16),
          ("gates", [3072, NTOK], BF16), ("modT", [128, 96], F32)]


def build_la():
    nc = bass.Bass("TRN2", target_bir_lowering=False)
    es = contextlib.ExitStack()
    with es:
        S = Sched(nc, es)
        di = lambda n, s, d=F32: S.dram(n, s, d, "ExternalInput")
        xT = di("xT", [D, NTOK])
        cT = di("cT", [128, KC * 2])
        w_mod = di("w_mod", [D, 6 * D])
        b_modT = di("b_modT", [128, 48])
        n1g = di("n1g", [128, KC])
        w_in = di("w_in", [D, N_IN])
        gcols = di("gcols", [128, 4])
        ropeC = di("ropeC", [128, NTOK])
        ropeS = di("ropeS", [128, NTOK])
        cmats = di("cmats", [128, 3 * 128])
        a2aug = di("a2aug", [33, 512])
        outs = {n: S.dram(n, s, d, "ExternalOutput") for n, s, d in LA_OUT}

        hT = S.sbuf("hT", [128, KC, NTOK], BF16)
        S.pool("stage", 2, [128, KC, 512], F32)
        S.pool("wb", 2, [128, KC, 512], BF16)
        S.pool("sq8", 1, [128, KC, 512], BF16)
        S.pool("rope", 4, [128, 512], F32)
        S.pool("f32a", 4, [128, 512], F32)
        S.pool("f32b", 4, [128, 512], F32)
        S.pool("bfa", 3, [128, 512], BF16)
        S.pool("bfo", 4, [128, 512], BF16)
        S.pool("ps", 7, [128, 512], F32, space="psum")
        cm32 = S.sbuf("cm32", [128, 384], F32)
        cmb = S.sbuf("cmb", [128, 384], BF16)
        cTs = S.sbuf("cTs", [128, KC * 2], F32)
        sil = S.sbuf("sil", [128, KC * 2], F32)
        bmod = S.sbuf("bmod", [128, 48], F32)
        n1gs = S.sbuf("n1gs", [128, KC], F32)
        gcs = S.sbuf("gcs", [128, 4], F32)
        modT = S.sbuf("modT", [128, 96], F32)
        A1 = S.sbuf("A1", [128, KC * 2], F32)
        a2s = S.sbuf("a2s", [33, 512], F32)
        gaT = S.sbuf("gaT", [33, 512], F32)
        ps_small = S.psum("ps_small", [128, 2], F32)
        cst = S.sbuf("cst", [128, 2], F32)
        S.op("pool", lambda e: e.memset(cst[:, 0:1], EPS), [], [cst])
        S.op("pool", lambda e: e.memset(cst[:, 1:2], 1.0), [], [cst])

        def rstd_from(ps, n, pool_tag):
            sq_ = S.get(pool_tag)
            S.op("act", lambda e: e.activation(out=sq_[:, 0:n], in_=ps[:, 0:n], func=AF.Sqrt, bias=cst[:, 0:1], scale=1.0),
                 [ps, cst], [sq_])
            r_ = S.get(pool_tag)
            S.op("dve", lambda e: e.reciprocal(out=r_[:, 0:n], in_=sq_[:, 0:n]), [sq_], [r_])
            return r_

        load(S, cm32, cm32[:], cmats, cmats[:, :])
        load(S, cTs, cTs[:], cT, cT[:, :])
        load(S, bmod, bmod[:], b_modT, b_modT[:, :])
        load(S, n1gs, n1gs[:], n1g, n1g[:, :])
        load(S, gcs, gcs[:], gcols, gcols[:, :])
        load(S, a2s, a2s[:], a2aug, a2aug[:, :])
        S.op("dve", lambda e: e.tensor_copy(out=cmb[:], in_=cm32[:]), [cm32], [cmb])
        Pm = cmb[:, 0:128]
        Bones = cmb[:, 128:256]
        Ones = cmb[:, 256:384]
        S.op("act", lambda e: e.activation(out=sil[:], in_=cTs[:], func=AF.Silu), [cTs], [sil])
        S.op("pool", lambda e: e.memset(gaT[32:33, :], 1.0), [], [gaT])

        wm_v = w_mod.t.rearrange("(kc p) f -> p kc f", p=128)
        for sb in range(12):
            st = S.get("stage")
            load(S, st, st[:], w_mod, wm_v[:, :, sb * 512:(sb + 1) * 512])
            for j in range(4):
                fo = sb * 4 + j
                for kc in range(KC):
                    S.op("pe", lambda e, kc=kc, j=j, st=st: e.matmul(
                        ps_small[:], st[:, kc, j * 128:(j + 1) * 128], sil[:, kc * 2:kc * 2 + 2],
                        start=(kc == 0), stop=(kc == KC - 1)), [st, sil], [ps_small])
                S.op("dve", lambda e, fo=fo: e.tensor_scalar(
                    out=modT[:, fo * 2:fo * 2 + 2], in0=ps_small[:], scalar1=bmod[:, fo:fo + 1], scalar2=None,
                    op0=ALU.add), [ps_small, bmod], [modT])
        store(S, outs["modT"], outs["modT"][:, :], modT, modT[:])
        for kc in range(KC):
            S.op("dve", lambda e, kc=kc: e.tensor_scalar(
                out=A1[:, kc * 2:kc * 2 + 2], in0=modT[:, (8 + kc) * 2:(8 + kc) * 2 + 2], scalar1=1.0,
                scalar2=n1gs[:, kc:kc + 1], op0=ALU.add, op1=ALU.mult), [modT, n1gs], [A1])

        xv = xT.t.rearrange("(kc p) t -> p kc t", p=128)
        for ti, (t0, n) in enumerate(TT):
            col = 0 if ti < 8 else 1
            st = S.get("stage")
            load(S, st, st[:, :, 0:n], xT, xv[:, :, t0:t0 + n])
            sq = S.get("sq8")
            S.op("act", lambda e: e.activation(out=sq[:, :, 0:n], in_=st[:, :, 0:n], func=AF.Square), [st], [sq])
            ps = S.get("ps")
            for kc in range(KC):
                S.op("pe", lambda e, kc=kc: e.matmul(ps[:, 0:n], Ones, sq[:, kc, 0:n], start=(kc == 0),
                                                      stop=(kc == KC - 1)), [cmb, sq], [ps])
            rstd = rstd_from(ps, n, "f32a")
            for kc in range(KC):
                tmp = S.get("f32b")
                S.op("dve", lambda e, kc=kc: e.scalar_tensor_tensor(
                    out=tmp[:, 0:n], in0=st[:, kc, 0:n], scalar=A1[:, kc * 2 + col:kc * 2 + col + 1],
                    in1=rstd[:, 0:n], op0=ALU.mult, op1=ALU.mult), [st, A1, rstd], [tmp])
                S.op("act", lambda e, kc=kc: e.activation(
                    out=hT[:, kc, t0:t0 + n], in_=tmp[:, 0:n], func=AF.Identity,
                    bias=modT[:, kc * 2 + col:kc * 2 + col + 1], scale=1.0), [tmp, modT], [hT])

        wv = w_in.t.rearrange("(kc p) c -> p kc c", p=128)

        def load_w(c0, ncol):
            st = S.get("stage")
            load(S, st, st[:, :, 0:ncol], w_in, wv[:, :, c0:c0 + ncol])
            wb = S.get("wb")
            for kc in range(KC):
                en = "pool" if kc % 2 == 0 else "dve"
                S.op(en, lambda e, kc=kc: e.tensor_copy(out=wb[:, kc, 0:ncol], in_=st[:, kc, 0:ncol]), [st], [wb])
            return wb

        def proj_fm(wb, j, t0, n, m=128):
            ps = S.get("ps")
            for kc in range(KC):
                S.op("pe", lambda e, kc=kc: e.matmul(ps[0:m, 0:n], wb[:, kc, j * 128:j * 128 + m], hT[:, kc, t0:t0 + n],
                                                      start=(kc == 0), stop=(kc == KC - 1)), [wb, hT], [ps])
            return ps

        def headnorm(ps, n, gi):
            zs = S.get("f32a")
            S.op("act", lambda e: e.activation(out=zs[:, 0:n], in_=ps[:, 0:n], func=AF.Copy), [ps], [zs])
            sq = S.get("bfa")
            S.op("act", lambda e: e.activation(out=sq[:, 0:n], in_=ps[:, 0:n], func=AF.Square), [ps], [sq])
            ps2 = S.get("ps")
            S.op("pe", lambda e: e.matmul(ps2[:, 0:n], Bones, sq[:, 0:n], start=True, stop=True), [cmb, sq], [ps2])
            rstd = rstd_from(ps2, n, "f32b")
            qh = S.get("bfa")
            S.op("dve", lambda e: e.scalar_tensor_tensor(out=qh[:, 0:n], in0=zs[:, 0:n], scalar=gcs[:, gi:gi + 1],
                                                         in1=rstd[:, 0:n], op0=ALU.mult, op1=ALU.mult),
                 [zs, gcs, rstd], [qh])
            return qh

        def fm_store(name, j, t0, n, src, m=128):
            o = outs[name]
            store(S, o, o[j * 128:j * 128 + m, t0:t0 + n], src, src[0:m, 0:n])

        def qk_block(c0, name, gi, rope):
            wb = load_w(c0, 512)
            for j in range(4):
                for (t0, n) in TT:
                    ps = proj_fm(wb, j, t0, n)
                    qh = headnorm(ps, n, gi)
                    if rope:
                        rc = S.get("rope")
                        load(S, rc, rc[:, 0:n], ropeC, ropeC[:, t0:t0 + n])
                        rs = S.get("rope")
                        load(S, rs, rs[:, 0:n], ropeS, ropeS[:, t0:t0 + n])
                        ps3 = S.get("ps")
                        S.op("pe", lambda e: e.matmul(ps3[:, 0:n], Pm, qh[:, 0:n], start=True, stop=True), [cmb, qh], [ps3])
                        t1 = S.get("f32a")
                        S.op("pool", lambda e: e.tensor_tensor(out=t1[:, 0:n], in0=qh[:, 0:n], in1=rc[:, 0:n], op=ALU.mult),
                             [qh, rc], [t1])
                        t2 = S.get("f32b")
                        S.op("dve", lambda e: e.tensor_tensor(out=t2[:, 0:n], in0=ps3[:, 0:n], in1=rs[:, 0:n], op=ALU.mult),
                             [ps3, rs], [t2])
                        ob = S.get("bfo")
                        S.op("pool", lambda e: e.tensor_tensor(out=ob[:, 0:n], in0=t1[:, 0:n], in1=t2[:, 0:n], op=ALU.add),
                             [t1, t2], [ob])
                    else:
                        ob = qh
                    fm_store(name, j, t0, n, ob)

        def act_block(c0, ncol, name, func, scale=1.0, jbase=0):
            wb = load_w(c0, ncol)
            for j in range(ncol // 128):
                for (t0, n) in TT:
                    ps = proj_fm(wb, j, t0, n)
                    ob = S.get("bfo")
                    S.op("act", lambda e: e.activation(out=ob[:, 0:n], in_=ps[:, 0:n], func=func, scale=scale), [ps], [ob])
                    fm_store(name, jbase + j, t0, n, ob)
            return wb

        def tm_block(wb, cofs, ncol, name):
            o = outs[name]
            for s in range(NTOK // 128):
                ps = S.get("ps")
                for kc in range(KC):
                    S.op("pe", lambda e, kc=kc: e.matmul(ps[:, 0:ncol], hT[:, kc, s * 128:(s + 1) * 128],
                                                          wb[:, kc, cofs:cofs + ncol], start=(kc == 0), stop=(kc == KC - 1)),
                         [wb, hT], [ps])
                ob = S.get("bfo")
                if s % 2 == 0:
                    S.op("act", lambda e: e.activation(out=ob[:, 0:ncol], in_=ps[:, 0:ncol], func=AF.Copy), [ps], [ob])
                else:
                    S.op("dve", lambda e: e.tensor_copy(out=ob[:, 0:ncol], in_=ps[:, 0:ncol]), [ps], [ob])
                store(S, o, o[s * 128:(s + 1) * 128, 0:ncol], ob, ob[:, 0:ncol])

        qk_block(C_DAQ, "qT_da", 0, True)
        qk_block(C_DAK, "kT_da", 1, True)
        tm_block(load_w(C_DAV, 512), 0, 512, "v_da")
        wb = load_w(C_GQ, 512)
        for j in range(4):
            for (t0, n) in TT:
                ps = proj_fm(wb, j, t0, n)
                ob = S.get("bfo")
                S.op("act", lambda e: e.activation(out=ob[:, 0:n], in_=ps[:, 0:n], func=AF.Copy,
                                                   scale=(0.125 if j < 2 else 1.0)), [ps], [ob])
                fm_store("qT_gl" if j < 2 else "kT_gl", j % 2, t0, n, ob)
        tm_block(wb, 256, 256, "k_gl")
        tm_block(load_w(C_GV, 512), 0, 512, "v_gl")
        act_block(C_GG, 512, "ggT", AF.Silu)
        wb = load_w(C_GA, 32)
        lao = outs["la"]
        for (t0, n) in TT:
            ps = proj_fm(wb, 0, t0, n, m=32)
            S.op("act", lambda e: e.activation(out=gaT[0:32, 0:n], in_=ps[0:32, 0:n], func=AF.Copy), [ps], [gaT])
            for s in range(n // 128):
                ps2 = S.get("ps")
                S.op("pe", lambda e, s=s: e.matmul(ps2[:, :], gaT[0:33, s * 128:(s + 1) * 128], a2s[0:33, :],
                                                    start=True, stop=True), [gaT, a2s], [ps2])
                ex = S.get("f32a")
                S.op("act", lambda e: e.activation(out=ex[:, :], in_=ps2[:, :], func=AF.Exp, scale=-1.0), [ps2], [ex])
                ln = S.get("f32b")
                S.op("act", lambda e: e.activation(out=ln[:, :], in_=ex[:, :], func=AF.Ln, bias=cst[:, 1:2], scale=1.0), [ex, cst], [ln])
                lo = S.get("f32a")
                S.op("dve", lambda e: e.tensor_scalar(out=lo[:, :], in0=ln[:, :], scalar1=-1.0 / 16.0, scalar2=None,
                                                      op0=ALU.mult), [ln], [lo])
                store(S, lao, lao[t0 + s * 128:t0 + (s + 1) * 128, :], lo, lo[:, :])
        qk_block(C_NQ, "qT_na", 2, False)
        qk_block(C_NK, "kT_na", 3, False)
        tm_block(load_w(C_NV, 512), 0, 512, "v_na")
        for g in range(6):
            act_block(C_GATE + g * 512, 512, "gates", AF.Sigmoid, jbase=g * 4)
        S.finish()
    return nc


def rope_tables():
    pos = np.arange(NLAT)
    return pos


def la_consts():
    Pm = np.zeros((128, 128), np.float32)
    for m in range(128):
        w = (m % 64) % 32
        if w < 16:
            Pm[m + 16, m] = -1.0
        else:
            Pm[m - 16, m] = 1.0
    Bones = np.zeros((128, 128), np.float32)
    Bones[:64, :64] = 1.0 / 64
    Bones[64:, 64:] = 1.0 / 64
    Ones = np.full((128, 128), 1.0 / 1024, np.float32)
    return np.concatenate([Pm, Bones, Ones], axis=1)


def rope_cs(qtr):
    tpos = qtr * NLAT + np.arange(NLAT)
    row = (tpos // GRID_W).astype(np.float32)
    colp = (tpos % GRID_W).astype(np.float32)
    freqs = (10000.0 ** (-np.arange(16, dtype=np.float32) / 16)).astype(np.float32)
    C = np.ones((128, NTOK), np.float32)
    Sn = np.zeros((128, NTOK), np.float32)
    for p in range(128):
        u = p % 64
        i = (u % 32) % 16
        ang = (row if u < 32 else colp) * freqs[i]
        C[p, :NLAT] = np.cos(ang.astype(np.float32))
        Sn[p, :NLAT] = np.sin(ang.astype(np.float32))
    return C, Sn


def chunkT(v):
    v = np.asarray(v, np.float32)
    if v.ndim == 1:
        return np.ascontiguousarray(v.reshape(-1, 128).T)
    return np.ascontiguousarray(v.reshape(v.shape[0], -1, 128).transpose(2, 1, 0).reshape(128, -1))


def la_inputs(l, core, xT_core, inp):
    b, qtr = core // 4, core % 4
    cc = np.stack([inp["c"][b], inp["c_ctx"]], 0)
    C, Sn = rope_cs(qtr)
    gc = np.stack([np.tile(inp["da_qn_g"][l], 2) * 0.125, np.tile(inp["da_kn_g"][l], 2),
                   np.tile(inp["na_qn_g"][l], 2) * 0.125, np.tile(inp["na_kn_g"][l], 2)], 1).astype(np.float32)
    a2aug = np.zeros((33, 512), np.float32)
    a2aug[0:16, 0:256] = inp["gla_a2"][l, 0]
    a2aug[16:32, 256:512] = inp["gla_a2"][l, 1]
    a2aug[32, 0:256] = inp["gla_a_b"][l, 0]
    a2aug[32, 256:512] = inp["gla_a_b"][l, 1]
    return {"xT": xT_core, "cT": chunkT(cc), "w_mod": inp["w_mod"][l], "b_modT": chunkT(inp["b_mod"][l]),
            "n1g": chunkT(inp["norm1_g"][l]), "w_in": inp["w_in"][l], "gcols": gc, "ropeC": C, "ropeS": Sn,
            "cmats": la_consts(), "a2aug": a2aug}


_NC = {}


def run_la(l, xTs, inp):
    if "la" not in _NC:
        _NC["la"] = build_la()
    in_maps = [la_inputs(l, c, xTs[c], inp) for c in range(NCORES)]
    res = run_bass_kernel_spmd(_NC["la"], in_maps, core_ids=list(range(NCORES)))
    return res.results


def make_xT(x, ctx):
    xs = []
    for c in range(NCORES):
        b, qtr = c // 4, c % 4
        xs.append(np.ascontiguousarray(
            np.concatenate([x[b, qtr * NLAT:(qtr + 1) * NLAT], ctx[b]], 0).T.astype(np.float32)))
    return xs


NKT = 130
NPF = 96
NHALO = 38 * 128
NA_NTOK = NHALO + NCTX


def barrier(S):
    allv = dict(S.pending)
    for k in ("pe", "act", "dve", "pool"):
        if S.cnt[k]:
            allv[k] = S.cnt[k]
    for en in ("pe", "act", "dve", "pool", "sp"):
        d = {k: v for k, v in allv.items() if k != en or en == "sp"}
        seen = S.seen[en]
        for k, v in d.items():
            if seen.get(k, 0) < v:
                S.eng[en].wait_ge(S.sem[k], v)
                seen[k] = v


def build_lb():
    nc = bass.Bass("TRN2", target_bir_lowering=False)
    es = contextlib.ExitStack()
    with es:
        S = Sched(nc, es)
        di = lambda n, s, d=F32: S.dram(n, s, d, "ExternalInput")
        xT = di("xT", [D, NTOK])
        modT_d = di("modT", [128, 96])
        n2g = di("n2g", [128, KC])
        qT_da = di("qT_da", [512, NTOK], BF16)
        kT_da = di("kT_da", [512, NKT * 128], BF16)
        v_da = di("v_da", [4, 128, NKT * 128], BF16)
        lamtab = di("lamtab", [128, 256])
        lconst = di("lconst", [128, 2])
        gvec = di("gvec", [128, 2])
        qT_gl = di("qT_gl", [256, NTOK], BF16)
        kT_gl = di("kT_gl", [256, NTOK], BF16)
        k_gl = di("k_gl", [NTOK, 256], BF16)
        v_gl = di("v_gl", [NTOK, 512], BF16)
        la_d = di("la", [NTOK, 512])
        ggT = di("ggT", [512, NTOK], BF16)
        pf_la = di("pf_la", [2, NPF * 128, 256])
        pf_k = di("pf_k", [2, NPF * 128, 256], BF16)
        pf_v = di("pf_v", [2, NPF * 128, 512], BF16)
        trim = di("trim", [128, 6 * 128])
        qT_na = di("qT_na", [512, NTOK], BF16)
        kT_na = di("kT_na", [512, NA_NTOK], BF16)
        v_na = di("v_na", [NA_NTOK, 512], BF16)
        nbias = di("nbias", [128, 8 * 7 * 128])
        nmask = di("nmask", [128, 5 * 7 * 128])
        gates = di("gates", [3072, NTOK], BF16)
        w_br = [di("w_br%d" % i, [512, D]) for i in range(3)]
        w_out = di("w_out", [D, D])
        w_ff1 = di("w_ff1", [D, 4 * D])
        w_ff2 = di("w_ff2", [4 * D, D])
        xo = S.dram("xo", [D, NTOK], F32, "ExternalOutput")
        ys = [S.dram("y%d" % i, [512, NTOK], BF16, "Internal") for i in range(3)]
        wbr_b = [S.dram("wbrb%d" % i, [128, 4 * D], BF16, "Internal") for i in range(3)]
        wout_b = S.dram("woutb", [128, KC * D], BF16, "Internal")
        w1_b = S.dram("w1b", [8, 128, KC * 512], BF16, "Internal")
        w2_b = S.dram("w2b", [8, 128, 32 * 128], BF16, "Internal")

        cst = S.sbuf("cst", [128, 2], F32)
        S.op("pool", lambda e: e.memset(cst[:, 0:1], EPS), [], [cst])
        S.op("pool", lambda e: e.memset(cst[:, 1:2], 1.0), [], [cst])
        onesb = S.sbuf("onesb", [128, 128], BF16)
        S.op("pool", lambda e: e.memset(onesb[:], 1.0), [], [onesb])
        ones32 = S.sbuf("ones32", [128, 128], F32)
        S.op("pool", lambda e: e.memset(ones32[:], 1.0), [], [ones32])
        o128 = S.sbuf("o128", [128, 128], BF16)
        S.op("pool", lambda e: e.memset(o128[:], 1.0 / 128), [], [o128])
        o1024 = S.sbuf("o1024", [128, 128], BF16)
        S.op("pool", lambda e: e.memset(o1024[:], 1.0 / 1024), [], [o1024])
        modT = S.sbuf("modT", [128, 96], F32)
        load(S, modT, modT[:], modT_d, modT_d[:, :])
        n2gs = S.sbuf("n2gs", [128, KC], F32)
        load(S, n2gs, n2gs[:], n2g, n2g[:, :])
        lt = S.sbuf("lt", [128, 256], F32)
        load(S, lt, lt[:], lamtab, lamtab[:, :])
        lcs = S.sbuf("lcs", [128, 2], F32)
        load(S, lcs, lcs[:], lconst, lconst[:, :])
        gv = S.sbuf("gv", [128, 2], F32)
        load(S, gv, gv[:], gvec, gvec[:, :])
        A2 = S.sbuf("A2", [128, KC * 2], F32)
        for kc in range(KC):
            S.op("dve", lambda e, kc=kc: e.tensor_scalar(
                out=A2[:, kc * 2:kc * 2 + 2], in0=modT[:, (32 + kc) * 2:(32 + kc) * 2 + 2], scalar1=1.0,
                scalar2=n2gs[:, kc:kc + 1], op0=ALU.add, op1=ALU.mult), [modT, n2gs], [A2])
        lw = S.sbuf("lw", [128, 128], F32)
        lsum = S.sbuf("lsum", [128, 8], F32)
        S.op("dve", lambda e: e.tensor_tensor(out=lw[:, 0:64], in0=lt[:, 0:64], in1=lt[:, 64:128], op=ALU.mult), [lt], [lw])
        S.op("dve", lambda e: e.tensor_tensor(out=lw[:, 64:128], in0=lt[:, 128:192], in1=lt[:, 192:256], op=ALU.mult), [lt, lw], [lw])
        S.op("dve", lambda e: e.reduce_sum(out=lsum[:, 0:1], in_=lw[:, 0:64], axis=AX.X), [lw], [lsum])
        S.op("dve", lambda e: e.reduce_sum(out=lsum[:, 1:2], in_=lw[:, 64:128], axis=AX.X), [lw, lsum], [lsum])
        S.op("act", lambda e: e.activation(out=lsum[:, 2:4], in_=lsum[:, 0:2], func=AF.Exp), [lsum], [lsum])
        S.op("dve", lambda e: e.tensor_tensor(out=lsum[:, 4:5], in0=lsum[:, 3:4], in1=lsum[:, 2:3], op=ALU.subtract), [lsum], [lsum])
        S.op("dve", lambda e: e.tensor_tensor(out=lsum[:, 5:6], in0=lsum[:, 4:5], in1=lcs[:, 0:1], op=ALU.subtract), [lsum, lcs], [lsum])
        S.op("dve", lambda e: e.tensor_tensor(out=lsum[:, 6:7], in0=gv[:, 0:1], in1=lcs[:, 1:2], op=ALU.mult), [lsum, gv, lcs], [lsum])
        neglam = lsum[:, 5:6]
        gsub = lsum[:, 6:7]

        def rstd_from(ps, n, pool_tag):
            sq_ = S.get(pool_tag)
            S.op("act", lambda e: e.activation(out=sq_[:, 0:n], in_=ps[:, 0:n], func=AF.Sqrt, bias=cst[:, 0:1], scale=1.0),
                 [ps, cst], [sq_])
            r_ = S.get(pool_tag)
            S.op("dve", lambda e: e.reciprocal(out=r_[:, 0:n], in_=sq_[:, 0:n]), [sq_], [r_])
            return r_

        with contextlib.ExitStack() as es2:
            S.es = es2
            S.pool("stage", 2, [128, KC, 512], F32)
            S.pool("wb", 2, [128, KC, 512], BF16)

            def cast_block(src_t, src_ap, nk, dsts):
                st = S.get("stage")
                load(S, st, st[:, 0:nk, :], src_t, src_ap)
                wb = S.get("wb")
                for kc in range(nk):
                    en = "pool" if kc % 2 == 0 else "dve"
                    S.op(en, lambda e, kc=kc: e.tensor_copy(out=wb[:, kc, :], in_=st[:, kc, :]), [st], [wb])
                for (dt_, dap, sap) in dsts(wb):
                    store(S, dt_, dap, wb, sap)

            for i in range(3):
                v = w_br[i].t.rearrange("(kc p) c -> p kc c", p=128)
                dv = wbr_b[i].t.rearrange("p (kc c) -> p kc c", kc=4)
                for hb in range(2):
                    cast_block(w_br[i], v[:, :, hb * 512:(hb + 1) * 512], 4,
                               lambda wb, dv=dv, hb=hb, i=i: [(wbr_b[i], dv[:, :, hb * 512:(hb + 1) * 512], wb[:, 0:4, :])])
            v = w_out.t.rearrange("(kc p) c -> p kc c", p=128)
            dv = wout_b.t.rearrange("p (kc c) -> p kc c", kc=KC)
            for hb in range(2):
                cast_block(w_out, v[:, :, hb * 512:(hb + 1) * 512], KC,
                           lambda wb, dv=dv, hb=hb: [(wout_b, dv[:, :, hb * 512:(hb + 1) * 512], wb[:, :, :])])
            v = w_ff1.t.rearrange("(kc p) c -> p kc c", p=128)
            for fb in range(8):
                dv = w1_b.t[fb].rearrange("p (kc c) -> p kc c", kc=KC)
                cast_block(w_ff1, v[:, :, fb * 512:(fb + 1) * 512], KC,
                           lambda wb, dv=dv: [(w1_b, dv, wb[:, :, :])])
            v = w_ff2.t.rearrange("(kc p) c -> p kc c", p=128)
            for kg in range(4):
                for hb in range(2):
                    def dsts(wb, kg=kg, hb=hb):
                        r = []
                        for j in range(4):
                            fo = hb * 4 + j
                            dv = w2_b.t[fo].rearrange("p (kc c) -> p kc c", kc=32)
                            r.append((w2_b, dv[:, kg * 8:(kg + 1) * 8, :], wb[:, :, j * 128:(j + 1) * 128]))
                        return r
                    cast_block(w_ff2, v[:, kg * 8:(kg + 1) * 8, hb * 512:(hb + 1) * 512], KC, dsts)
            barrier(S)

        with contextlib.ExitStack() as es2:
            S.es = es2
            S.pool("kT", 2, [128, NKT * 128], BF16)
            S.pool("V", 2, [128, NKT * 128], BF16)
            S.pool("q", 2, [128, NTOK], BF16)
            S.pool("pT", 6, [128, 512], BF16)
            S.pool("f32a", 4, [128, 512], F32)
            S.pool("f32b", 4, [128, 512], F32)
            S.pool("osub", 4, [128, 512], F32)
            S.pool("saccd", 4, [128, 512], F32)
            S.pool("saccp", 4, [128, 512], F32)
            S.pool("bfa", 2, [128, 512], BF16)
            S.pool("bfo", 3, [128, 512], BF16)
            S.pool("st", 4, [128, 512], F32, space="psum")
            S.pool("acc", 4, [128, 512], F32, space="psum")
            for h in range(4):
                kT = S.get("kT")
                load(S, kT, kT[:], kT_da, kT_da[h * 128:(h + 1) * 128, :])
                V = S.get("V")
                load(S, V, V[:], v_da, v_da[h])
                q = S.get("q")
                load(S, q, q[:], qT_da, qT_da[h * 128:(h + 1) * 128, :])
                for ti, (t0, n) in enumerate(TT):
                    keys = list(range(NKT)) if ti < 8 else [128, 129]
                    accO = [S.get("acc"), S.get("acc")]
                    sacc_d = [S.get("saccd"), S.get("saccd")]
                    sacc_p = [S.get("saccp"), S.get("saccp")]
                    sts = {}

                    def qk(j):
                        pair = []
                        for sub in range(2):
                            p0 = sub * 64
                            st = S.get("st")
                            S.op("pe", lambda e: e.matmul(st[:, 0:n], kT[p0:p0 + 64, j * 128:(j + 1) * 128],
                                                          q[p0:p0 + 64, t0:t0 + n], start=True, stop=True), [kT, q], [st])
                            pair.append(st)
                        sts[j] = pair

                    qk(keys[0])
                    for idx, j in enumerate(keys):
                        pair = sts.pop(j)
                        pTs = []
                        for sub in range(2):
                            pT = S.get("pT")
                            S.op("act", lambda e: e.activation(out=pT[:, 0:n], in_=pair[sub][:, 0:n], func=AF.Exp), [pair[sub]], [pT])
                            pTs.append(pT)
                        if idx + 1 < len(keys):
                            qk(keys[idx + 1])
                        first, last = idx == 0, idx == len(keys) - 1
                        for sub in range(2):
                            pT = pTs[sub]
                            S.op("pe", lambda e: e.matmul(accO[sub][:, 0:n], V[:, j * 128:(j + 1) * 128], pT[:, 0:n],
                                                          start=first, stop=last), [V, pT], [accO[sub]])
                            en_, sa_ = ("dve", sacc_d[sub]) if idx % 2 == 0 else ("pool", sacc_p[sub])
                            if idx < 2:
                                S.op(en_, lambda e: e.tensor_copy(out=sa_[:, 0:n], in_=pT[:, 0:n]), [pT], [sa_])
                            else:
                                S.op(en_, lambda e: e.tensor_tensor(out=sa_[:, 0:n], in0=sa_[:, 0:n], in1=pT[:, 0:n], op=ALU.add),
                                     [sa_, pT], [sa_])
                    osub = []
                    for sub in range(2):
                        accS = S.get("acc")
                        S.op("pe", lambda e: e.matmul(accS[:, 0:n], ones32[:], sacc_d[sub][:, 0:n], start=True, stop=False),
                             [ones32, sacc_d[sub]], [accS])
                        S.op("pe", lambda e: e.matmul(accS[:, 0:n], ones32[:], sacc_p[sub][:, 0:n], start=False, stop=True),
                             [ones32, sacc_p[sub]], [accS])
                        rec = S.get("f32a")
                        S.op("dve", lambda e: e.reciprocal(out=rec[:, 0:n], in_=accS[:, 0:n]), [accS], [rec])
                        os_ = S.get("osub")
                        S.op("dve", lambda e: e.tensor_tensor(out=os_[:, 0:n], in0=accO[sub][:, 0:n], in1=rec[:, 0:n], op=ALU.mult),
                             [accO[sub], rec], [os_])
                        osub.append(os_)
                    o = S.get("f32b")
                    S.op("dve", lambda e: e.scalar_tensor_tensor(out=o[:, 0:n], in0=osub[1][:, 0:n], scalar=neglam,
                                                                 in1=osub[0][:, 0:n], op0=ALU.mult, op1=ALU.add),
                         [osub[0], osub[1], lsum], [o])
                    sq = S.get("bfa")
                    S.op("act", lambda e: e.activation(out=sq[:, 0:n], in_=o[:, 0:n], func=AF.Square), [o], [sq])
                    ms = S.get("acc")
                    S.op("pe", lambda e: e.matmul(ms[:, 0:n], o128[:], sq[:, 0:n], start=True, stop=True), [o128, sq], [ms])
                    rstd = rstd_from(ms, n, "f32a")
                    y = S.get("bfo")
                    S.op("dve", lambda e: e.scalar_tensor_tensor(out=y[:, 0:n], in0=o[:, 0:n], scalar=gsub, in1=rstd[:, 0:n],
                                                                 op0=ALU.mult, op1=ALU.mult), [o, lsum, rstd], [y])
                    store(S, ys[0], ys[0][h * 128:(h + 1) * 128, t0:t0 + n], y, y[:, 0:n])
            barrier(S)

        with contextlib.ExitStack() as es2:
            S.es = es2
            tri32 = S.sbuf("tri32", [128, 768], F32)
            load(S, tri32, tri32[:], trim, trim[:, :])
            onec = S.sbuf("onec", [128, 1], F32)
            S.op("pool", lambda e: e.memset(onec[:], 1.0), [], [onec])
            og = [S.sbuf("og%d" % h, [128, NTOK], F32) for h in range(4)]
            Sst = [S.sbuf("Sst%d" % p, [128, 256], F32) for p in range(2)]
            Sb = [S.sbuf("Sb%d" % p, [128, 256], BF16) for p in range(2)]
            S.pool("la", 3, [128, 256], F32)
            S.pool("k", 3, [128, 256], BF16)
            S.pool("v", 3, [128, 512], BF16)
            S.pool("qk", 3, [128, 4, 128], BF16)
            S.pool("ekd", 2, [128, 256], F32)
            S.pool("kd", 2, [128, 256], BF16)
            S.pool("eqk", 4, [128, 128], F32)
            S.pool("qeke", 4, [128, 128], BF16)
            S.pool("aTm", 3, [128, 128], BF16)
            S.pool("dcol", 4, [128, 1], F32)
            S.pool("f32a", 4, [128, 512], F32)
            S.pool("bfa", 2, [128, 512], BF16)
            S.pool("bfo", 3, [128, 512], BF16)
            S.pool("gg", 2, [128, 512], BF16)
            S.pool("pex", 1, [128, 512], F32, space="psum")
            S.pool("pcum", 2, [128, 128], F32, space="psum")
            S.pool("paT", 2, [128, 128], F32, space="psum")
            S.pool("po", 2, [128, 128], F32, space="psum")
            S.pool("pds", 1, [128, 256], F32, space="psum")

            def gla_tile(d, la_t, la_ap, k_t, k_ap, v_t, v_ap, full, tcol):
                b = d * 384
                Tstr, Tincl, Mask = tri32[:, b:b + 128], tri32[:, b + 128:b + 256], tri32[:, b + 256:b + 384]
                la = S.get("la")
                load(S, la, la[:], la_t, la_ap)
                kk = S.get("k")
                load(S, kk, kk[:], k_t, k_ap)
                vv = S.get("v")
                load(S, vv, vv[:], v_t, v_ap)
                pex = S.get("pex")
                S.op("pe", lambda e: e.matmul(pex[:, 0:256], Tstr, la[:], start=True, stop=True), [tri32, la], [pex])
                ekd = S.get("ekd")
                S.op("act", lambda e: e.activation(out=ekd[:], in_=pex[:, 0:256], func=AF.Exp), [pex], [ekd])
                kd = S.get("kd")
                S.op("dve", lambda e: e.tensor_tensor(out=kd[:], in0=kk[:], in1=ekd[:], op=ALU.mult), [kk, ekd], [kd])
                if full:
                    qk = S.get("qk")
                    load(S, qk, qk[:, 0:2, :], qT_gl, qT_gl.t.rearrange("(pr p) t -> p pr t", p=128)[:, :, tcol:tcol + 128])
                    load(S, qk, qk[:, 2:4, :], kT_gl, kT_gl.t.rearrange("(pr p) t -> p pr t", p=128)[:, :, tcol:tcol + 128])
                for pr in range(2):
                    dcol = S.get("dcol")
                    if full:
                        pc = S.get("pcum")
                        S.op("pe", lambda e: e.matmul(pc[:], la[:, pr * 128:(pr + 1) * 128], Tincl, start=True, stop=True),
                             [la, tri32], [pc])
                        eq = S.get("eqk")
                        S.op("act", lambda e: e.activation(out=eq[:], in_=pc[:], func=AF.Exp), [pc], [eq])
                        ek = S.get("eqk")
                        S.op("act", lambda e: e.activation(out=ek[:], in_=pc[:], func=AF.Exp, scale=-1.0), [pc], [ek])
                        qe = S.get("qeke")
                        S.op("dve", lambda e: e.tensor_tensor(out=qe[:], in0=qk[:, pr, :], in1=eq[:], op=ALU.mult), [qk, eq], [qe])
                        ke = S.get("qeke")
                        S.op("pool", lambda e: e.tensor_tensor(out=ke[:], in0=qk[:, 2 + pr, :], in1=ek[:], op=ALU.mult), [qk, ek], [ke])
                        lastc = 127 if d == 0 else 0
                        S.op("act", lambda e: e.activation(out=dcol[:], in_=eq[:, lastc:lastc + 1], func=AF.Copy), [eq], [dcol])
                        for hl in range(2):
                            h = pr * 2 + hl
                            p0 = hl * 64
                            pa = S.get("paT")
                            S.op("pe", lambda e: e.matmul(pa[:], ke[p0:p0 + 64, :], qe[p0:p0 + 64, :], start=True, stop=True),
                                 [ke, qe], [pa])
                            am = S.get("aTm")
                            S.op("dve", lambda e: e.tensor_tensor(out=am[:], in0=pa[:], in1=Mask, op=ALU.mult), [pa, tri32], [am])
                            po = S.get("po")
                            S.op("pe", lambda e: e.matmul(po[:], vv[:, h * 128:(h + 1) * 128], am[:], start=True, stop=False),
                                 [vv, am], [po])
                            S.op("pe", lambda e: e.matmul(po[:], Sb[pr][p0:p0 + 64, hl * 128:(hl + 1) * 128], qe[p0:p0 + 64, :],
                                                          start=False, stop=True), [Sb[pr], qe], [po])
                            if d == 0:
                                S.op("act", lambda e: e.activation(out=og[h][:, tcol:tcol + 128], in_=po[:], func=AF.Copy),
                                     [po], [og[h]])
                            else:
                                S.op("dve", lambda e: e.tensor_tensor(out=og[h][:, tcol:tcol + 128], in0=po[:],
                                                                      in1=og[h][:, tcol:tcol + 128], op=ALU.add), [po, og[h]], [og[h]])
                    else:
                        pc = S.get("pcum")
                        S.op("pe", lambda e: e.matmul(pc[:, 0:1], la[:, pr * 128:(pr + 1) * 128], onec[:], start=True, stop=True),
                             [la, onec], [pc])
                        S.op("act", lambda e: e.activation(out=dcol[:], in_=pc[:, 0:1], func=AF.Exp), [pc], [dcol])
                    pds = S.get("pds")
                    S.op("pe", lambda e: e.matmul(pds[:], kd[:, pr * 128:(pr + 1) * 128], vv[:, pr * 256:(pr + 1) * 256],
                                                  start=True, stop=True), [kd, vv], [pds])
                    S.op("dve", lambda e: e.scalar_tensor_tensor(out=Sst[pr][:], in0=Sst[pr][:], scalar=dcol[:, 0:1], in1=pds[:],
                                                                 op0=ALU.mult, op1=ALU.add), [Sst[pr], dcol, pds], [Sst[pr]])
                    S.op("pool", lambda e: e.tensor_copy(out=Sb[pr][:], in_=Sst[pr][:]), [Sst[pr]], [Sb[pr]])

            for d in range(2):
                for pr in range(2):
                    S.op("pool", lambda e: e.memset(Sst[pr][:], 0.0), [], [Sst[pr]])
                    S.op("pool", lambda e: e.memset(Sb[pr][:], 0.0), [], [Sb[pr]])
                own = lambda t: (la_d, la_d[t * 128:(t + 1) * 128, d * 256:(d + 1) * 256], k_gl, k_gl[t * 128:(t + 1) * 128, :],
                                 v_gl, v_gl[t * 128:(t + 1) * 128, :])
                pre = lambda t: (pf_la, pf_la[d, t * 128:(t + 1) * 128, :], pf_k, pf_k[d, t * 128:(t + 1) * 128, :],
                                 pf_v, pf_v[d, t * 128:(t + 1) * 128, :])
                order = (lambda r: list(r)) if d == 0 else (lambda r: list(r)[::-1])
                for t in order(range(32, 34)):
                    gla_tile(d, *own(t), True, t * 128)
                for t in order(range(NPF)):
                    gla_tile(d, *pre(t), False, 0)
                for t in order(range(32)):
                    gla_tile(d, *own(t), True, t * 128)
            for h in range(4):
                for (t0, n) in TT:
                    sq = S.get("bfa")
                    S.op("act", lambda e: e.activation(out=sq[:, 0:n], in_=og[h][:, t0:t0 + n], func=AF.Square), [og[h]], [sq])
                    ms = S.get("pex")
                    S.op("pe", lambda e: e.matmul(ms[:, 0:n], o128[:], sq[:, 0:n], start=True, stop=True), [o128, sq], [ms])
                    rstd = rstd_from(ms, n, "f32a")
                    g = S.get("gg")
                    load(S, g, g[:, 0:n], ggT, ggT[h * 128:(h + 1) * 128, t0:t0 + n])
                    y0 = S.get("f32a")
                    S.op("dve", lambda e: e.scalar_tensor_tensor(out=y0[:, 0:n], in0=og[h][:, t0:t0 + n], scalar=gv[:, 1:2],
                                                                 in1=rstd[:, 0:n], op0=ALU.mult, op1=ALU.mult), [og[h], gv, rstd], [y0])
                    y = S.get("bfo")
                    S.op("pool", lambda e: e.tensor_tensor(out=y[:, 0:n], in0=y0[:, 0:n], in1=g[:, 0:n], op=ALU.mult), [y0, g], [y])
                    store(S, ys[1], ys[1][h * 128:(h + 1) * 128, t0:t0 + n], y, y[:, 0:n])
            barrier(S)

        with contextlib.ExitStack() as es2:
            S.es = es2
            kTn = S.sbuf("kTn", [128, 4, NA_NTOK], BF16)
            load(S, kTn, kTn[:], kT_na, kT_na.t.rearrange("(c p) t -> p c t", p=128))
            Vn = S.sbuf("Vn", [128, NA_NTOK // 128, 512], BF16)
            load(S, Vn, Vn[:], v_na, v_na.t.rearrange("(t p) c -> p t c", p=128))
            nb = S.sbuf("nb", [128, 56 * 128], F32)
            load(S, nb, nb[:], nbias, nbias[:, :])
            nm = S.sbuf("nm", [128, 35 * 128], F32)
            load(S, nm, nm[:], nmask, nmask[:, :])
            S.pool("q", 2, [128, NTOK], BF16)
            S.pool("s1", 3, [128, 128], F32)
            S.pool("s2", 3, [128, 128], F32)
            S.pool("pT", 4, [128, 128], BF16)
            S.pool("rec", 2, [128, 128], F32)
            S.pool("yt", 3, [128, 128], BF16)
            S.pool("st", 3, [128, 128], F32, space="psum")
            S.pool("acc", 4, [128, 128], F32, space="psum")
            for c in range(4):
                q = S.get("q")
                load(S, q, q[:], qT_na, qT_na[c * 128:(c + 1) * 128, :])
                for qt in range(34):
                    tcol = qt * 128
                    if qt < 32:
                        units = [(qt + o, o) for o in range(7)] + [(38, None), (39, None)]
                        cls = {0: 0, 1: 1, 30: 3, 31: 4}.get(qt, 2)
                    else:
                        units = [(38, None), (39, None)]
                        cls = 2
                    yt = S.get("yt")
                    for hl in range(2):
                        h = 2 * c + hl
                        p0 = hl * 64
                        accO = S.get("acc")
                        accS = S.get("acc")
                        nsts = {}

                        def nqk(ui_):
                            kt_ = units[ui_][0]
                            st_ = S.get("st")
                            S.op("pe", lambda e: e.matmul(st_[:], kTn[p0:p0 + 64, c, kt_ * 128:(kt_ + 1) * 128],
                                                          q[p0:p0 + 64, tcol:tcol + 128], start=True, stop=True), [kTn, q], [st_])
                            nsts[ui_] = st_

                        NLA = 2
                        for u0 in range(min(NLA, len(units))):
                            nqk(u0)
                        for ui, (kt, o) in enumerate(units):
                            st = nsts.pop(ui)
                            if ui + NLA < len(units):
                                nqk(ui + NLA)
                            pT = S.get("pT")
                            if o is not None:
                                s1 = S.get("s1")
                                bo = (h * 7 + o) * 128
                                S.op("dve", lambda e: e.tensor_tensor(out=s1[:], in0=st[:], in1=nb[:, bo:bo + 128], op=ALU.add),
                                     [st, nb], [s1])
                                s2 = S.get("s2")
                                mo = (cls * 7 + o) * 128
                                S.op("pool", lambda e: e.tensor_tensor(out=s2[:], in0=s1[:], in1=nm[:, mo:mo + 128], op=ALU.add),
                                     [s1, nm], [s2])
                                S.op("act", lambda e: e.activation(out=pT[:], in_=s2[:], func=AF.Exp), [s2], [pT])
                            else:
                                S.op("act", lambda e: e.activation(out=pT[:], in_=st[:], func=AF.Exp), [st], [pT])
                            first, last = ui == 0, ui == len(units) - 1
                            S.op("pe", lambda e: e.matmul(accO[:], Vn[:, kt, c * 128:(c + 1) * 128], pT[:], start=first, stop=last),
                                 [Vn, pT], [accO])
                            S.op("pe", lambda e: e.matmul(accS[:], onesb[:], pT[:], start=first, stop=last), [onesb, pT], [accS])
                        rec = S.get("rec")
                        S.op("dve", lambda e: e.reciprocal(out=rec[p0:p0 + 64, :], in_=accS[p0:p0 + 64, :]), [accS], [rec])
                        S.op("dve", lambda e: e.tensor_tensor(out=yt[p0:p0 + 64, :], in0=accO[p0:p0 + 64, :], in1=rec[p0:p0 + 64, :],
                                                              op=ALU.mult), [accO, rec], [yt])
                    store(S, ys[2], ys[2][c * 128:(c + 1) * 128, tcol:tcol + 128], yt, yt[:])
            barrier(S)

        with contextlib.ExitStack() as es2:
            S.es = es2
            wbr = S.sbuf("wbr", [128, 12, D], BF16)
            for i in range(3):
                load(S, wbr, wbr[:, i * 4:(i + 1) * 4, :], wbr_b[i], wbr_b[i].t.rearrange("p (kc c) -> p kc c", kc=4))
            wo = S.sbuf("wo", [128, KC, D], BF16)
            load(S, wo, wo[:], wout_b, wout_b.t.rearrange("p (kc c) -> p kc c", kc=KC))
            S.pool("x", 1, [128, KC, 512], F32)
            S.pool("y", 1, [128, 12, 512], BF16)
            S.pool("g3", 2, [128, 3, 512], BF16)
            S.pool("m", 1, [128, KC, 512], BF16)
            S.pool("sq8", 1, [128, KC, 512], BF16)
            S.pool("h2", 1, [128, KC, 512], BF16)
            S.pool("a", 1, [128, 32, 512], BF16)
            S.pool("w1", 2, [128, KC, 512], BF16)
            S.pool("w2", 2, [128, 32, 128], BF16)
            S.pool("f32a", 4, [128, 512], F32)
            S.pool("f32b", 4, [128, 512], F32)
            S.pool("xo", 3, [128, 512], F32)
            S.pool("ps", 8, [128, 512], F32, space="psum")
            gv3 = gates.t.rearrange("(br fo p) t -> p br fo t", br=3, fo=8)
            xv = xT.t.rearrange("(kc p) t -> p kc t", p=128)
            for ti, (t0, n) in enumerate(TT):
                col = 0 if ti < 8 else 1
                mcol = lambda ch: modT[:, ch * 2 + col:ch * 2 + col + 1]
                x = S.get("x")
                load(S, x, x[:, :, 0:n], xT, xv[:, :, t0:t0 + n])
                y = S.get("y")
                for br in range(3):
                    load(S, y, y[:, br * 4:(br + 1) * 4, 0:n], ys[br], ys[br].t.rearrange("(kc p) t -> p kc t", p=128)[:, :, t0:t0 + n])
                m = S.get("m")
                for fo in range(8):
                    g3 = S.get("g3")
                    load(S, g3, g3[:, :, 0:n], gates, gv3[:, :, fo, t0:t0 + n])
                    tmps = []
                    for br in range(3):
                        ps = S.get("ps")
                        for kc in range(4):
                            S.op("pe", lambda e, kc=kc: e.matmul(ps[:, 0:n], wbr[:, br * 4 + kc, fo * 128:(fo + 1) * 128],
                                                                  y[:, br * 4 + kc, 0:n], start=(kc == 0), stop=(kc == 3)), [wbr, y], [ps])
                        tb = S.get("f32a")
                        S.op("dve", lambda e: e.tensor_tensor(out=tb[:, 0:n], in0=ps[:, 0:n], in1=g3[:, br, 0:n], op=ALU.mult),
                             [ps, g3], [tb])
                        tmps.append(tb)
                    s01 = S.get("f32b")
                    S.op("pool", lambda e: e.tensor_tensor(out=s01[:, 0:n], in0=tmps[0][:, 0:n], in1=tmps[1][:, 0:n], op=ALU.add),
                         [tmps[0], tmps[1]], [s01])
                    S.op("pool", lambda e: e.tensor_tensor(out=m[:, fo, 0:n], in0=s01[:, 0:n], in1=tmps[2][:, 0:n], op=ALU.add),
                         [s01, tmps[2]], [m])
                for fo in range(8):
                    ps = S.get("ps")
                    for kc in range(KC):
                        S.op("pe", lambda e, kc=kc: e.matmul(ps[:, 0:n], wo[:, kc, fo * 128:(fo + 1) * 128], m[:, kc, 0:n],
                                                              start=(kc == 0), stop=(kc == KC - 1)), [wo, m], [ps])
                    S.op("dve", lambda e: e.scalar_tensor_tensor(out=x[:, fo, 0:n], in0=ps[:, 0:n], scalar=mcol(16 + fo),
                                                                 in1=x[:, fo, 0:n], op0=ALU.mult, op1=ALU.add), [ps, modT, x], [x])
                sq = S.get("sq8")
                S.op("act", lambda e: e.activation(out=sq[:, :, 0:n], in_=x[:, :, 0:n], func=AF.Square), [x], [sq])
                ps = S.get("ps")
                for kc in range(KC):
                    S.op("pe", lambda e, kc=kc: e.matmul(ps[:, 0:n], o1024[:], sq[:, kc, 0:n], start=(kc == 0), stop=(kc == KC - 1)),
                         [o1024, sq], [ps])
                rstd = rstd_from(ps, n, "f32a")
                h2 = S.get("h2")
                for kc in range(KC):
                    tmp = S.get("f32b")
                    S.op("dve", lambda e, kc=kc: e.scalar_tensor_tensor(
                        out=tmp[:, 0:n], in0=x[:, kc, 0:n], scalar=A2[:, kc * 2 + col:kc * 2 + col + 1], in1=rstd[:, 0:n],
                        op0=ALU.mult, op1=ALU.mult), [x, A2, rstd], [tmp])
                    S.op("act", lambda e, kc=kc: e.activation(out=h2[:, kc, 0:n], in_=tmp[:, 0:n], func=AF.Identity,
                                                              bias=mcol(24 + kc), scale=1.0), [tmp, modT], [h2])
                a = S.get("a")
                for fb in range(8):
                    w1 = S.get("w1")
                    load(S, w1, w1[:], w1_b, w1_b.t[fb].rearrange("p (kc c) -> p kc c", kc=KC))
                    for j in range(4):
                        f = fb * 4 + j
                        ps = S.get("ps")
                        for kc in range(KC):
                            S.op("pe", lambda e, kc=kc: e.matmul(ps[:, 0:n], w1[:, kc, j * 128:(j + 1) * 128], h2[:, kc, 0:n],
                                                                  start=(kc == 0), stop=(kc == KC - 1)), [w1, h2], [ps])
                        r = S.get("f32a")
                        S.op("act", lambda e: e.activation(out=r[:, 0:n], in_=ps[:, 0:n], func=AF.Relu), [ps], [r])
                        S.op("pool", lambda e: e.tensor_tensor(out=a[:, f, 0:n], in0=r[:, 0:n], in1=r[:, 0:n], op=ALU.mult), [r], [a])
                for fo in range(8):
                    w2 = S.get("w2")
                    load(S, w2, w2[:], w2_b, w2_b.t[fo].rearrange("p (kc c) -> p kc c", kc=32))
                    ps = S.get("ps")
                    for f in range(32):
                        S.op("pe", lambda e, f=f: e.matmul(ps[:, 0:n], w2[:, f, :], a[:, f, 0:n], start=(f == 0), stop=(f == 31)),
                             [w2, a], [ps])
                    xt = S.get("xo")
                    S.op("dve", lambda e: e.scalar_tensor_tensor(out=xt[:, 0:n], in0=ps[:, 0:n], scalar=mcol(40 + fo),
                                                                 in1=x[:, fo, 0:n], op0=ALU.mult, op1=ALU.add), [ps, modT, x], [xt])
                    store(S, xo, xo[fo * 128:(fo + 1) * 128, t0:t0 + n], xt, xt[:, 0:n])
            S.finish()
            barrier(S)
        S.es = es
    return nc


def tri_consts():
    i = np.arange(128)
    sp, s = i[:, None], i[None, :]
    mats = [(sp > s), (sp <= s), (sp <= s), (sp < s), (sp >= s), (sp >= s)]
    return np.concatenate([m.astype(np.float32) for m in mats], axis=1)


def na_bias_table(rpb):
    ka = np.arange(128)
    a, kc = ka // 64, ka % 64
    bq, cq = ka // 64, ka % 64
    out = np.zeros((128, 8, 7, 128), np.float32)
    for o in range(7):
        rel_row = (-6 + 2 * o + a)[:, None] - bq[None, :] + 7
        rel_col = kc[:, None] - cq[None, :] + 15
        ok = (rel_row >= 0) & (rel_row <= 14) & (rel_col >= 0) & (rel_col <= 30)
        rr = np.clip(rel_row, 0, 14)
        rc = np.clip(rel_col, 0, 30)
        for h in range(8):
            out[:, h, o, :] = np.where(ok, rpb[h][rr, rc], 0.0)
    return out.reshape(128, -1)


def na_mask_table(qtr):
    ka = np.arange(128)
    a, kc = ka // 64, ka % 64
    bq, cq = ka // 64, ka % 64
    out = np.zeros((128, 5, 7, 128), np.float32)
    for cls, qt in enumerate((0, 1, 2, 30, 31)):
        Rq = qtr * 64 + 2 * qt
        R = Rq + bq
        rs = np.clip(R - 4, 0, 256 - 8)
        cs = np.clip(cq - 8, 0, GRID_W - 16)
        for o in range(7):
            kr = (Rq - 6 + 2 * o + a)[:, None]
            ok = (kr >= rs[None, :]) & (kr < rs[None, :] + 8) & (kc[:, None] >= cs[None, :]) & (kc[:, None] < cs[None, :] + 16)
            out[:, cls, o, :] = np.where(ok, 0.0, -30000.0)
    return out.reshape(128, -1)


def lb_inputs(l, core, xT_core, ra, inp):
    b, qtr = core // 4, core % 4
    grp = [ra[4 * b + j] for j in range(4)]
    me = ra[core]
    bf = ml_dtypes.bfloat16
    kT_da = np.concatenate([np.asarray(g["kT_da"])[:, :NLAT] for g in grp] + [np.asarray(me["kT_da"])[:, NLAT:]], axis=1)
    v_all = np.concatenate([np.asarray(g["v_da"])[:NLAT] for g in grp] + [np.asarray(me["v_da"])[NLAT:]], axis=0)
    v_da = np.ascontiguousarray(v_all.reshape(NKT, 128, 4, 128).transpose(2, 1, 0, 3).reshape(4, 128, NKT * 128))
    lam_init = 0.8 - 0.6 * math.exp(-0.3 * l)
    npf = NPF * 128
    pf_la = np.zeros((2, npf, 256), np.float32)
    pf_k = np.zeros((2, npf, 256), bf)
    pf_v = np.zeros((2, npf, 512), bf)
    nb_ = qtr
    if nb_:
        pf_la[0, :nb_ * NLAT] = np.concatenate([np.asarray(grp[j]["la"])[:NLAT, 0:256] for j in range(qtr)], 0)
        pf_k[0, :nb_ * NLAT] = np.concatenate([np.asarray(grp[j]["k_gl"])[:NLAT] for j in range(qtr)], 0)
        pf_v[0, :nb_ * NLAT] = np.concatenate([np.asarray(grp[j]["v_gl"])[:NLAT] for j in range(qtr)], 0)
    na_ = 3 - qtr
    if na_:
        pf_la[1, npf - na_ * NLAT:] = np.concatenate([np.asarray(grp[j]["la"])[:NLAT, 256:512] for j in range(qtr + 1, 4)], 0)
        pf_k[1, npf - na_ * NLAT:] = np.concatenate([np.asarray(grp[j]["k_gl"])[:NLAT] for j in range(qtr + 1, 4)], 0)
        pf_v[1, npf - na_ * NLAT:] = np.concatenate([np.asarray(grp[j]["v_gl"])[:NLAT] for j in range(qtr + 1, 4)], 0)
    kn_all = np.concatenate([np.asarray(g["kT_na"])[:, :NLAT] for g in grp], axis=1)
    vn_all = np.concatenate([np.asarray(g["v_na"])[:NLAT] for g in grp], axis=0)
    lo = (qtr * 64 - 6) * GRID_W
    hi = lo + NHALO
    kT_na = np.zeros((512, NA_NTOK), bf)
    v_na = np.zeros((NA_NTOK, 512), bf)
    s0, s1 = max(lo, 0), min(hi, SEQ)
    kT_na[:, s0 - lo:s1 - lo] = kn_all[:, s0:s1]
    v_na[s0 - lo:s1 - lo] = vn_all[s0:s1]
    kT_na[:, NHALO:] = np.asarray(me["kT_na"])[:, NLAT:]
    v_na[NHALO:] = np.asarray(me["v_na"])[NLAT:]
    return {
        "xT": xT_core, "modT": np.asarray(me["modT"]), "n2g": chunkT(inp["norm2_g"][l]),
        "qT_da": np.asarray(me["qT_da"]), "kT_da": np.ascontiguousarray(kT_da), "v_da": v_da,
        "lamtab": np.ascontiguousarray(np.tile(inp["da_lambda"][l].reshape(1, 256), (128, 1)).astype(np.float32)),
        "lconst": np.tile(np.array([[lam_init, 1.0 - lam_init]], np.float32), (128, 1)),
        "gvec": np.stack([inp["da_subln_g"][l], inp["gla_gn_g"][l]], 1).astype(np.float32),
        "qT_gl": np.asarray(me["qT_gl"]), "kT_gl": np.asarray(me["kT_gl"]), "k_gl": np.asarray(me["k_gl"]),
        "v_gl": np.asarray(me["v_gl"]), "la": np.asarray(me["la"]), "ggT": np.asarray(me["ggT"]),
        "pf_la": pf_la, "pf_k": pf_k, "pf_v": pf_v, "trim": tri_consts(),
        "qT_na": np.asarray(me["qT_na"]), "kT_na": kT_na, "v_na": v_na,
        "nbias": na_bias_table(np.asarray(inp["na_rpb"][l], np.float32)), "nmask": na_mask_table(qtr),
        "gates": np.asarray(me["gates"]),
        "w_br0": inp["w_br_da"][l], "w_br1": inp["w_br_gla"][l], "w_br2": inp["w_br_na"][l],
        "w_out": inp["w_out"][l], "w_ff1": inp["w_ff1"][l], "w_ff2": inp["w_ff2"][l],
    }


def run_lb(l, xTs, ra, inp):
    if "lb" not in _NC:
        _NC["lb"] = build_lb()
    in_maps = [lb_inputs(l, c, xTs[c], ra, inp) for c in range(NCORES)]
    res = run_bass_kernel_spmd(_NC["lb"], in_maps, core_ids=list(range(NCORES)))
    return res.results


def kernel(**inputs):
    inp = {k: np.asarray(v) for k, v in inputs.items()}
    xTs = make_xT(inp["x"], inp["ctx"])
    for l in range(2):
        ra = run_la(l, xTs, inp)
        rb = run_lb(l, xTs, ra, inp)
        xTs = [np.ascontiguousarray(np.asarray(rb[c]["xo"], dtype=np.float32)) for c in range(NCORES)]
    out = np.empty((2, SEQ, D), np.float32)
    for c in range(NCORES):
        b, qtr = c // 4, c % 4
        out[b, qtr * NLAT:(qtr + 1) * NLAT] = xTs[c][:, :NLAT].T
    return out
```

```python
import contextlib
import math
import numpy as np
import ml_dtypes
import concourse.bass as bass
import concourse.mybir as mybir
from concourse.bass_utils import run_bass_kernel_spmd

F32 = mybir.dt.float32
BF16 = mybir.dt.bfloat16
AF = mybir.ActivationFunctionType
ALU = mybir.AluOpType
AX = mybir.AxisListType

NCORES = 8
D = 1024
KC = 8
SEQ = 16384
NLAT = 4096
NCTX = 256
NTOK = NLAT + NCTX
GRID_W = 64
EPS = 1e-6
N_IN = 7712
TT = [(i * 512, 512) for i in range(8)] + [(4096, 256)]


class Tile:
    def __init__(self, t, name):
        self.t = t
        self.name = name
        self.w = {}
        self.r = {}
        self.sem = None
        self.semval = 0
        self.track = True

    def __getitem__(self, k):
        return self.t[k]


def _merge(dst, src):
    for k, v in src.items():
        if dst.get(k, 0) < v:
            dst[k] = v


class Sched:
    def __init__(self, nc, es):
        self.nc = nc
        self.es = es
        self.eng = {"pe": nc.tensor, "act": nc.scalar, "dve": nc.vector, "pool": nc.gpsimd, "sp": nc.sync}
        self.sem = {}
        self.cnt = {}
        for k in ("pe", "act", "dve", "pool"):
            self.sem[k] = es.enter_context(nc.semaphore("s_" + k))
            self.cnt[k] = 0
        self.seen = {k: {} for k in self.eng}
        self.pending = {}
        self.nsem = 0
        self.pools = {}

    def sbuf(self, name, shape, dtype):
        self.uid = getattr(self, "uid", 0) + 1
        return Tile(self.es.enter_context(self.nc.sbuf_tensor("sb%d_%s" % (self.uid, name), list(shape), dtype)),
                    "%s_%d" % (name, self.uid))

    def psum(self, name, shape, dtype=F32):
        self.uid = getattr(self, "uid", 0) + 1
        return Tile(self.es.enter_context(self.nc.psum_tensor("pp%d_%s" % (self.uid, name), list(shape), dtype)),
                    "%s_%d" % (name, self.uid))

    def dram(self, name, shape, dtype, kind):
        t = self.nc.dram_tensor(name, list(shape), dtype, kind=kind)
        tl = Tile(t.ap(), name)
        tl.track = False
        return tl

    def pool(self, tag, n, shape, dtype, space="sbuf"):
        mk = self.sbuf if space == "sbuf" else self.psum
        self.pools[tag] = [[mk("%s%d" % (tag, i), shape, dtype) for i in range(n)], 0]

    def get(self, tag):
        p = self.pools[tag]
        t = p[0][p[1] % len(p[0])]
        p[1] += 1
        return t

    def _wait(self, en, deps):
        seen = self.seen[en]
        for k, v in deps.items():
            if k == "pe" and en == "pe":
                continue
            if seen.get(k, 0) >= v:
                continue
            self.eng[en].wait_ge(self.sem[k], v)
            seen[k] = v

    def op(self, en, fn, reads=(), writes=()):
        deps = {}
        for t in reads:
            _merge(deps, t.w)
        for t in writes:
            _merge(deps, t.w)
            _merge(deps, t.r)
        self._wait(en, deps)
        ins = fn(self.eng[en])
        self.cnt[en] += 1
        ev = {en: self.cnt[en]}
        ins.then_inc(self.sem[en], 1)
        for t in reads:
            _merge(t.r, ev)
        for t in writes:
            t.w = dict(ev)
            t.r = {}
        return ins

    def dma(self, dst, dst_ap, src, src_ap, owner):
        if owner.sem is None:
            self.nsem += 1
            owner.sem = "d%d_%s" % (self.nsem, owner.name)
            self.sem[owner.sem] = self.es.enter_context(self.nc.semaphore(owner.sem))
        deps = {}
        if src.track:
            _merge(deps, src.w)
        if dst.track:
            _merge(deps, dst.w)
            _merge(deps, dst.r)
        self._wait("sp", deps)
        owner.semval += 16
        ev = {owner.sem: owner.semval}
        self.nc.sync.dma_start(out=dst_ap, in_=src_ap).then_inc(self.sem[owner.sem], 16)
        if src.track:
            _merge(src.r, ev)
        if dst.track:
            dst.w = dict(ev)
            dst.r = {}
        _merge(self.pending, ev)

    def finish(self):
        self._wait("sp", self.pending)


def load(S, dst, dst_ap, src, src_ap):
    S.dma(dst, dst_ap, src, src_ap, owner=dst)


def store(S, dst, dst_ap, src, src_ap):
    S.dma(dst, dst_ap, src, src_ap, owner=src)


C_DAQ, C_DAK, C_DAV = 0, 512, 1024
C_GQ, C_GK, C_GV, C_GG, C_GA = 1536, 1792, 2048, 2560, 3072
C_NQ, C_NK, C_NV = 3104, 3616, 4128
C_GATE = 4640

LA_OUT = [("qT_da", [512, NTOK], BF16), ("kT_da", [512, NTOK], BF16), ("v_da", [NTOK, 512], BF16),
          ("qT_gl", [256, NTOK], BF16), ("kT_gl", [256, NTOK], BF16), ("k_gl", [NTOK, 256], BF16),
          ("v_gl", [NTOK, 512], BF16), ("ggT", [512, NTOK], BF16), ("la", [NTOK, 512], F32),
          ("qT_na", [512, NTOK], BF16), ("kT_na", [512, NTOK], BF16), ("v_na", [NTOK, 512], BF16),
          ("gates", [3072, NTOK], BF16), ("modT", [128, 96], F32)]


def build_la():
    nc = bass.Bass("TRN2", target_bir_lowering=False)
    es = contextlib.ExitStack()
    with es:
        S = Sched(nc, es)
        di = lambda n, s, d=F32: S.dram(n, s, d, "ExternalInput")
        xT = di("xT", [D, NTOK])
        cT = di("cT", [128, KC * 2])
        w_mod = di("w_mod", [D, 6 * D])
        b_modT = di("b_modT", [128, 48])
        n1g = di("n1g", [128, KC])
        w_in = di("w_in", [D, N_IN])
        gcols = di("gcols", [128, 4])
        ropeC = di("ropeC", [128, NTOK])
        ropeS = di("ropeS", [128, NTOK])
        cmats = di("cmats", [128, 3 * 128])
        a2aug = di("a2aug", [33, 512])
        outs = {n: S.dram(n, s, d, "ExternalOutput") for n, s, d in LA_OUT}

        hT = S.sbuf("hT", [128, KC, NTOK], BF16)
        S.pool("stage", 2, [128, KC, 512], F32)
        S.pool("wb", 2, [128, KC, 512], BF16)
        S.pool("sq8", 1, [128, KC, 512], BF16)
        S.pool("rope", 4, [128, 512], F32)
        S.pool("f32a", 4, [128, 512], F32)
        S.pool("f32b", 4, [128, 512], F32)
        S.pool("bfa", 3, [128, 512], BF16)
        S.pool("bfo", 4, [128, 512], BF16)
        S.pool("ps", 7, [128, 512], F32, space="psum")
        cm32 = S.sbuf("cm32", [128, 384], F32)
        cmb = S.sbuf("cmb", [128, 384], BF16)
        cTs = S.sbuf("cTs", [128, KC * 2], F32)
        sil = S.sbuf("sil", [128, KC * 2], F32)
        bmod = S.sbuf("bmod", [128, 48], F32)
        n1gs = S.sbuf("n1gs", [128, KC], F32)
        gcs = S.sbuf("gcs", [128, 4], F32)
        modT = S.sbuf("modT", [128, 96], F32)
        A1 = S.sbuf("A1", [128, KC * 2], F32)
        a2s = S.sbuf("a2s", [33, 512], F32)
        gaT = S.sbuf("gaT", [33, 512], F32)
        ps_small = S.psum("ps_small", [128, 2], F32)
        cst = S.sbuf("cst", [128, 2], F32)
        S.op("pool", lambda e: e.memset(cst[:, 0:1], EPS), [], [cst])
        S.op("pool", lambda e: e.memset(cst[:, 1:2], 1.0), [], [cst])

        def rstd_from(ps, n, pool_tag):
            sq_ = S.get(pool_tag)
            S.op("act", lambda e: e.activation(out=sq_[:, 0:n], in_=ps[:, 0:n], func=AF.Sqrt, bias=cst[:, 0:1], scale=1.0),
                 [ps, cst], [sq_])
            r_ = S.get(pool_tag)
            S.op("dve", lambda e: e.reciprocal(out=r_[:, 0:n], in_=sq_[:, 0:n]), [sq_], [r_])
            return r_

        load(S, cm32, cm32[:], cmats, cmats[:, :])
        load(S, cTs, cTs[:], cT, cT[:, :])
        load(S, bmod, bmod[:], b_modT, b_modT[:, :])
        load(S, n1gs, n1gs[:], n1g, n1g[:, :])
        load(S, gcs, gcs[:], gcols, gcols[:, :])
        load(S, a2s, a2s[:], a2aug, a2aug[:, :])
        S.op("dve", lambda e: e.tensor_copy(out=cmb[:], in_=cm32[:]), [cm32], [cmb])
        Pm = cmb[:, 0:128]
        Bones = cmb[:, 128:256]
        Ones = cmb[:, 256:384]
        S.op("act", lambda e: e.activation(out=sil[:], in_=cTs[:], func=AF.Silu), [cTs], [sil])
        S.op("pool", lambda e: e.memset(gaT[32:33, :], 1.0), [], [gaT])

        wm_v = w_mod.t.rearrange("(kc p) f -> p kc f", p=128)
        for sb in range(12):
            st = S.get("stage")
            load(S, st, st[:], w_mod, wm_v[:, :, sb * 512:(sb + 1) * 512])
            for j in range(4):
                fo = sb * 4 + j
                for kc in range(KC):
                    S.op("pe", lambda e, kc=kc, j=j, st=st: e.matmul(
                        ps_small[:], st[:, kc, j * 128:(j + 1) * 128], sil[:, kc * 2:kc * 2 + 2],
                        start=(kc == 0), stop=(kc == KC - 1)), [st, sil], [ps_small])
                S.op("dve", lambda e, fo=fo: e.tensor_scalar(
                    out=modT[:, fo * 2:fo * 2 + 2], in0=ps_small[:], scalar1=bmod[:, fo:fo + 1], scalar2=None,
                    op0=ALU.add), [ps_small, bmod], [modT])
        store(S, outs["modT"], outs["modT"][:, :], modT, modT[:])
        for kc in range(KC):
            S.op("dve", lambda e, kc=kc: e.tensor_scalar(
                out=A1[:, kc * 2:kc * 2 + 2], in0=modT[:, (8 + kc) * 2:(8 + kc) * 2 + 2], scalar1=1.0,
                scalar2=n1gs[:, kc:kc + 1], op0=ALU.add, op1=ALU.mult), [modT, n1gs], [A1])

        xv = xT.t.rearrange("(kc p) t -> p kc t", p=128)
        for ti, (t0, n) in enumerate(TT):
            col = 0 if ti < 8 else 1
            st = S.get("stage")
            load(S, st, st[:, :, 0:n], xT, xv[:, :, t0:t0 + n])
            sq = S.get("sq8")
            S.op("act", lambda e: e.activation(out=sq[:, :, 0:n], in_=st[:, :, 0:n], func=AF.Square), [st], [sq])
            ps = S.get("ps")
            for kc in range(KC):
                S.op("pe", lambda e, kc=kc: e.matmul(ps[:, 0:n], Ones, sq[:, kc, 0:n], start=(kc == 0),
                                                      stop=(kc == KC - 1)), [cmb, sq], [ps])
            rstd = rstd_from(ps, n, "f32a")
            for kc in range(KC):
                tmp = S.get("f32b")
                S.op("dve", lambda e, kc=kc: e.scalar_tensor_tensor(
                    out=tmp[:, 0:n], in0=st[:, kc, 0:n], scalar=A1[:, kc * 2 + col:kc * 2 + col + 1],
                    in1=rstd[:, 0:n], op0=ALU.mult, op1=ALU.mult), [st, A1, rstd], [tmp])
                S.op("act", lambda e, kc=kc: e.activation(
                    out=hT[:, kc, t0:t0 + n], in_=tmp[:, 0:n], func=AF.Identity,
                    bias=modT[:, kc * 2 + col:kc * 2 + col + 1], scale=1.0), [tmp, modT], [hT])

        wv = w_in.t.rearrange("(kc p) c -> p kc c", p=128)

        def load_w(c0, ncol):
            st = S.get("stage")
            load(S, st, st[:, :, 0:ncol], w_in, wv[:, :, c0:c0 + ncol])
            wb = S.get("wb")
            for kc in range(KC):
                en = "pool" if kc % 2 == 0 else "dve"
                S.op(en, lambda e, kc=kc: e.tensor_copy(out=wb[:, kc, 0:ncol], in_=st[:, kc, 0:ncol]), [st], [wb])
            return wb

        def proj_fm(wb, j, t0, n, m=128):
            ps = S.get("ps")
            for kc in range(KC):
                S.op("pe", lambda e, kc=kc: e.matmul(ps[0:m, 0:n], wb[:, kc, j * 128:j * 128 + m], hT[:, kc, t0:t0 + n],
                                                      start=(kc == 0), stop=(kc == KC - 1)), [wb, hT], [ps])
            return ps

        def headnorm(ps, n, gi):
            zs = S.get("f32a")
            S.op("act", lambda e: e.activation(out=zs[:, 0:n], in_=ps[:, 0:n], func=AF.Copy), [ps], [zs])
            sq = S.get("bfa")
            S.op("act", lambda e: e.activation(out=sq[:, 0:n], in_=ps[:, 0:n], func=AF.Square), [ps], [sq])
            ps2 = S.get("ps")
            S.op("pe", lambda e: e.matmul(ps2[:, 0:n], Bones, sq[:, 0:n], start=True, stop=True), [cmb, sq], [ps2])
            rstd = rstd_from(ps2, n, "f32b")
            qh = S.get("bfa")
            S.op("dve", lambda e: e.scalar_tensor_tensor(out=qh[:, 0:n], in0=zs[:, 0:n], scalar=gcs[:, gi:gi + 1],
                                                         in1=rstd[:, 0:n], op0=ALU.mult, op1=ALU.mult),
                 [zs, gcs, rstd], [qh])
            return qh

        def fm_store(name, j, t0, n, src, m=128):
            o = outs[name]
            store(S, o, o[j * 128:j * 128 + m, t0:t0 + n], src, src[0:m, 0:n])

        def qk_block(c0, name, gi, rope):
            wb = load_w(c0, 512)
            for j in range(4):
                for (t0, n) in TT:
                    ps = proj_fm(wb, j, t0, n)
                    qh = headnorm(ps, n, gi)
                    if rope:
                        rc = S.get("rope")
                        load(S, rc, rc[:, 0:n], ropeC, ropeC[:, t0:t0 + n])
                        rs = S.get("rope")
                        load(S, rs, rs[:, 0:n], ropeS, ropeS[:, t0:t0 + n])
                        ps3 = S.get("ps")
                        S.op("pe", lambda e: e.matmul(ps3[:, 0:n], Pm, qh[:, 0:n], start=True, stop=True), [cmb, qh], [ps3])
                        t1 = S.get("f32a")
                        S.op("pool", lambda e: e.tensor_tensor(out=t1[:, 0:n], in0=qh[:, 0:n], in1=rc[:, 0:n], op=ALU.mult),
                             [qh, rc], [t1])
                        t2 = S.get("f32b")
                        S.op("dve", lambda e: e.tensor_tensor(out=t2[:, 0:n], in0=ps3[:, 0:n], in1=rs[:, 0:n], op=ALU.mult),
                             [ps3, rs], [t2])
                        ob = S.get("bfo")
                        S.op("pool", lambda e: e.tensor_tensor(out=ob[:, 0:n], in0=t1[:, 0:n], in1=t2[:, 0:n], op=ALU.add),
                             [t1, t2], [ob])
                    else:
                        ob = qh
                    fm_store(name, j, t0, n, ob)

        def act_block(c0, ncol, name, func, scale=1.0, jbase=0):
            wb = load_w(c0, ncol)
            for j in range(ncol // 128):
                for (t0, n) in TT:
                    ps = proj_fm(wb, j, t0, n)
                    ob = S.get("bfo")
                    S.op("act", lambda e: e.activation(out=ob[:, 0:n], in_=ps[:, 0:n], func=func, scale=scale), [ps], [ob])
                    fm_store(name, jbase + j, t0, n, ob)
            return wb

        def tm_block(wb, cofs, ncol, name):
            o = outs[name]
            for s in range(NTOK // 128):
                ps = S.get("ps")
                for kc in range(KC):
                    S.op("pe", lambda e, kc=kc: e.matmul(ps[:, 0:ncol], hT[:, kc, s * 128:(s + 1) * 128],
                                                          wb[:, kc, cofs:cofs + ncol], start=(kc == 0), stop=(kc == KC - 1)),
                         [wb, hT], [ps])
                ob = S.get("bfo")
                if s % 2 == 0:
                    S.op("act", lambda e: e.activation(out=ob[:, 0:ncol], in_=ps[:, 0:ncol], func=AF.Copy), [ps], [ob])
                else:
                    S.op("dve", lambda e: e.tensor_copy(out=ob[:, 0:ncol], in_=ps[:, 0:ncol]), [ps], [ob])
                store(S, o, o[s * 128:(s + 1) * 128, 0:ncol], ob, ob[:, 0:ncol])

        qk_block(C_DAQ, "qT_da", 0, True)
        qk_block(C_DAK, "kT_da", 1, True)
        tm_block(load_w(C_DAV, 512), 0, 512, "v_da")
        wb = load_w(C_GQ, 512)
        for j in range(4):
            for (t0, n) in TT:
                ps = proj_fm(wb, j, t0, n)
                ob = S.get("bfo")
                S.op("act", lambda e: e.activation(out=ob[:, 0:n], in_=ps[:, 0:n], func=AF.Copy,
                                                   scale=(0.125 if j < 2 else 1.0)), [ps], [ob])
                fm_store("qT_gl" if j < 2 else "kT_gl", j % 2, t0, n, ob)
        tm_block(wb, 256, 256, "k_gl")
        tm_block(load_w(C_GV, 512), 0, 512, "v_gl")
        act_block(C_GG, 512, "ggT", AF.Silu)
        wb = load_w(C_GA, 32)
        lao = outs["la"]
        for (t0, n) in TT:
            ps = proj_fm(wb, 0, t0, n, m=32)
            S.op("act", lambda e: e.activation(out=gaT[0:32, 0:n], in_=ps[0:32, 0:n], func=AF.Copy), [ps], [gaT])
            for s in range(n // 128):
                ps2 = S.get("ps")
                S.op("pe", lambda e, s=s: e.matmul(ps2[:, :], gaT[0:33, s * 128:(s + 1) * 128], a2s[0:33, :],
                                                    start=True, stop=True), [gaT, a2s], [ps2])
                ex = S.get("f32a")
                S.op("act", lambda e: e.activation(out=ex[:, :], in_=ps2[:, :], func=AF.Exp, scale=-1.0), [ps2], [ex])
                ln = S.get("f32b")
                S.op("act", lambda e: e.activation(out=ln[:, :], in_=ex[:, :], func=AF.Ln, bias=cst[:, 1:2], scale=1.0), [ex, cst], [ln])
                lo = S.get("f32a")
                S.op("dve", lambda e: e.tensor_scalar(out=lo[:, :], in0=ln[:, :], scalar1=-1.0 / 16.0, scalar2=None,
                                                      op0=ALU.mult), [ln], [lo])
                store(S, lao, lao[t0 + s * 128:t0 + (s + 1) * 128, :], lo, lo[:, :])
        qk_block(C_NQ, "qT_na", 2, False)
        qk_block(C_NK, "kT_na", 3, False)
        tm_block(load_w(C_NV, 512), 0, 512, "v_na")
        for g in range(6):
            act_block(C_GATE + g * 512, 512, "gates", AF.Sigmoid, jbase=g * 4)
        S.finish()
    return nc


def rope_tables():
    pos = np.arange(NLAT)
    return pos


def la_consts():
    Pm = np.zeros((128, 128), np.float32)
    for m in range(128):
        w = (m % 64) % 32
        if w < 16:
            Pm[m + 16, m] = -1.0
        else:
            Pm[m - 16, m] = 1.0
    Bones = np.zeros((128, 128), np.float32)
    Bones[:64, :64] = 1.0 / 64
    Bones[64:, 64:] = 1.0 / 64
    Ones = np.full((128, 128), 1.0 / 1024, np.float32)
    return np.concatenate([Pm, Bones, Ones], axis=1)


def rope_cs(qtr):
    tpos = qtr * NLAT + np.arange(NLAT)
    row = (tpos // GRID_W).astype(np.float32)
    colp = (tpos % GRID_W).astype(np.float32)
    freqs = (10000.0 ** (-np.arange(16, dtype=np.float32) / 16)).astype(np.float32)
    C = np.ones((128, NTOK), np.float32)
    Sn = np.zeros((128, NTOK), np.float32)
    for p in range(128):
        u = p % 64
        i = (u % 32) % 16
        ang = (row if u < 32 else colp) * freqs[i]
        C[p, :NLAT] = np.cos(ang.astype(np.float32))
        Sn[p, :NLAT] = np.sin(ang.astype(np.float32))
    return C, Sn


def chunkT(v):
    v = np.asarray(v, np.float32)
    if v.ndim == 1:
        return np.ascontiguousarray(v.reshape(-1, 128).T)
    return np.ascontiguousarray(v.reshape(v.shape[0], -1, 128).transpose(2, 1, 0).reshape(128, -1))


def la_inputs(l, core, xT_core, inp):
    b, qtr = core // 4, core % 4
    cc = np.stack([inp["c"][b], inp["c_ctx"]], 0)
    C, Sn = rope_cs(qtr)
    gc = np.stack([np.tile(inp["da_qn_g"][l], 2) * 0.125, np.tile(inp["da_kn_g"][l], 2),
                   np.tile(inp["na_qn_g"][l], 2) * 0.125, np.tile(inp["na_kn_g"][l], 2)], 1).astype(np.float32)
    a2aug = np.zeros((33, 512), np.float32)
    a2aug[0:16, 0:256] = inp["gla_a2"][l, 0]
    a2aug[16:32, 256:512] = inp["gla_a2"][l, 1]
    a2aug[32, 0:256] = inp["gla_a_b"][l, 0]
    a2aug[32, 256:512] = inp["gla_a_b"][l, 1]
    return {"xT": xT_core, "cT": chunkT(cc), "w_mod": inp["w_mod"][l], "b_modT": chunkT(inp["b_mod"][l]),
            "n1g": chunkT(inp["norm1_g"][l]), "w_in": inp["w_in"][l], "gcols": gc, "ropeC": C, "ropeS": Sn,
            "cmats": la_consts(), "a2aug": a2aug}


_NC = {}


def run_la(l, xTs, inp):
    if "la" not in _NC:
        _NC["la"] = build_la()
    in_maps = [la_inputs(l, c, xTs[c], inp) for c in range(NCORES)]
    res = run_bass_kernel_spmd(_NC["la"], in_maps, core_ids=list(range(NCORES)))
    return res.results


def make_xT(x, ctx):
    xs = []
    for c in range(NCORES):
        b, qtr = c // 4, c % 4
        xs.append(np.ascontiguousarray(
            np.concatenate([x[b, qtr * NLAT:(qtr + 1) * NLAT], ctx[b]], 0).T.astype(np.float32)))
    return xs


NKT = 130
NPF = 96
NHALO = 38 * 128
NA_NTOK = NHALO + NCTX


def barrier(S):
    allv = dict(S.pending)
    for k in ("pe", "act", "dve", "pool"):
        if S.cnt[k]:
            allv[k] = S.cnt[k]
    for en in ("pe", "act", "dve", "pool", "sp"):
        d = {k: v for k, v in allv.items() if k != en or en == "sp"}
        seen = S.seen[en]
        for k, v in d.items():
            if seen.get(k, 0) < v:
                S.eng[en].wait_ge(S.sem[k], v)
                seen[k] = v


def build_lb():
    nc = bass.Bass("TRN2", target_bir_lowering=False)
    es = contextlib.ExitStack()
    with es:
        S = Sched(nc, es)
        di = lambda n, s, d=F32: S.dram(n, s, d, "ExternalInput")
        xT = di("xT", [D, NTOK])
        modT_d = di("modT", [128, 96])
        n2g = di("n2g", [128, KC])
        qT_da = di("qT_da", [512, NTOK], BF16)
        kT_da = di("kT_da", [512, NKT * 128], BF16)
        v_da = di("v_da", [4, 128, NKT * 128], BF16)
        lamtab = di("lamtab", [128, 256])
        lconst = di("lconst", [128, 2])
        gvec = di("gvec", [128, 2])
        qT_gl = di("qT_gl", [256, NTOK], BF16)
        kT_gl = di("kT_gl", [256, NTOK], BF16)
        k_gl = di("k_gl", [NTOK, 256], BF16)
        v_gl = di("v_gl", [NTOK, 512], BF16)
        la_d = di("la", [NTOK, 512])
        ggT = di("ggT", [512, NTOK], BF16)
        pf_la = di("pf_la", [2, NPF * 128, 256])
        pf_k = di("pf_k", [2, NPF * 128, 256], BF16)
        pf_v = di("pf_v", [2, NPF * 128, 512], BF16)
        trim = di("trim", [128, 6 * 128])
        qT_na = di("qT_na", [512, NTOK], BF16)
        kT_na = di("kT_na", [512, NA_NTOK], BF16)
        v_na = di("v_na", [NA_NTOK, 512], BF16)
        nbias = di("nbias", [128, 8 * 7 * 128])
        nmask = di("nmask", [128, 5 * 7 * 128])
        gates = di("gates", [3072, NTOK], BF16)
        w_br = [di("w_br%d" % i, [512, D]) for i in range(3)]
        w_out = di("w_out", [D, D])
        w_ff1 = di("w_ff1", [D, 4 * D])
        w_ff2 = di("w_ff2", [4 * D, D])
        xo = S.dram("xo", [D, NTOK], F32, "ExternalOutput")
        ys = [S.dram("y%d" % i, [512, NTOK], BF16, "Internal") for i in range(3)]
        wbr_b = [S.dram("wbrb%d" % i, [128, 4 * D], BF16, "Internal") for i in range(3)]
        wout_b = S.dram("woutb", [128, KC * D], BF16, "Internal")
        w1_b = S.dram("w1b", [8, 128, KC * 512], BF16, "Internal")
        w2_b = S.dram("w2b", [8, 128, 32 * 128], BF16, "Internal")

        cst = S.sbuf("cst", [128, 2], F32)
        S.op("pool", lambda e: e.memset(cst[:, 0:1], EPS), [], [cst])
        S.op("pool", lambda e: e.memset(cst[:, 1:2], 1.0), [], [cst])
        onesb = S.sbuf("onesb", [128, 128], BF16)
        S.op("pool", lambda e: e.memset(onesb[:], 1.0), [], [onesb])
        ones32 = S.sbuf("ones32", [128, 128], F32)
        S.op("pool", lambda e: e.memset(ones32[:], 1.0), [], [ones32])
        o128 = S.sbuf("o128", [128, 128], BF16)
        S.op("pool", lambda e: e.memset(o128[:], 1.0 / 128), [], [o128])
        o1024 = S.sbuf("o1024", [128, 128], BF16)
        S.op("pool", lambda e: e.memset(o1024[:], 1.0 / 1024), [], [o1024])
        modT = S.sbuf("modT", [128, 96], F32)
        load(S, modT, modT[:], modT_d, modT_d[:, :])
        n2gs = S.sbuf("n2gs", [128, KC], F32)
        load(S, n2gs, n2gs[:], n2g, n2g[:, :])
        lt = S.sbuf("lt", [128, 256], F32)
        load(S, lt, lt[:], lamtab, lamtab[:, :])
        lcs = S.sbuf("lcs", [128, 2], F32)
        load(S, lcs, lcs[:], lconst, lconst[:, :])
        gv = S.sbuf("gv", [128, 2], F32)
        load(S, gv, gv[:], gvec, gvec[:, :])
        A2 = S.sbuf("A2", [128, KC * 2], F32)
        for kc in range(KC):
            S.op("dve", lambda e, kc=kc: e.tensor_scalar(
                out=A2[:, kc * 2:kc * 2 + 2], in0=modT[:, (32 + kc) * 2:(32 + kc) * 2 + 2], scalar1=1.0,
                scalar2=n2gs[:, kc:kc + 1], op0=ALU.add, op1=ALU.mult), [modT, n2gs], [A2])
        lw = S.sbuf("lw", [128, 128], F32)
        lsum = S.sbuf("lsum", [128, 8], F32)
        S.op("dve", lambda e: e.tensor_tensor(out=lw[:, 0:64], in0=lt[:, 0:64], in1=lt[:, 64:128], op=ALU.mult), [lt], [lw])
        S.op("dve", lambda e: e.tensor_tensor(out=lw[:, 64:128], in0=lt[:, 128:192], in1=lt[:, 192:256], op=ALU.mult), [lt, lw], [lw])
        S.op("dve", lambda e: e.reduce_sum(out=lsum[:, 0:1], in_=lw[:, 0:64], axis=AX.X), [lw], [lsum])
        S.op("dve", lambda e: e.reduce_sum(out=lsum[:, 1:2], in_=lw[:, 64:128], axis=AX.X), [lw, lsum], [lsum])
        S.op("act", lambda e: e.activation(out=lsum[:, 2:4], in_=lsum[:, 0:2], func=AF.Exp), [lsum], [lsum])
        S.op("dve", lambda e: e.tensor_tensor(out=lsum[:, 4:5], in0=lsum[:, 3:4], in1=lsum[:, 2:3], op=ALU.subtract), [lsum], [lsum])
        S.op("dve", lambda e: e.tensor_tensor(out=lsum[:, 5:6], in0=lsum[:, 4:5], in1=lcs[:, 0:1], op=ALU.subtract), [lsum, lcs], [lsum])
        S.op("dve", lambda e: e.tensor_tensor(out=lsum[:, 6:7], in0=gv[:, 0:1], in1=lcs[:, 1:2], op=ALU.mult), [lsum, gv, lcs], [lsum])
        neglam = lsum[:, 5:6]
        gsub = lsum[:, 6:7]

        def rstd_from(ps, n, pool_tag):
            sq_ = S.get(pool_tag)
            S.op("act", lambda e: e.activation(out=sq_[:, 0:n], in_=ps[:, 0:n], func=AF.Sqrt, bias=cst[:, 0:1], scale=1.0),
                 [ps, cst], [sq_])
            r_ = S.get(pool_tag)
            S.op("dve", lambda e: e.reciprocal(out=r_[:, 0:n], in_=sq_[:, 0:n]), [sq_], [r_])
            return r_

        with contextlib.ExitStack() as es2:
            S.es = es2
            S.pool("stage", 2, [128, KC, 512], F32)
            S.pool("wb", 2, [128, KC, 512], BF16)

            def cast_block(src_t, src_ap, nk, dsts):
                st = S.get("stage")
                load(S, st, st[:, 0:nk, :], src_t, src_ap)
                wb = S.get("wb")
                for kc in range(nk):
                    en = "pool" if kc % 2 == 0 else "dve"
                    S.op(en, lambda e, kc=kc: e.tensor_copy(out=wb[:, kc, :], in_=st[:, kc, :]), [st], [wb])
                for (dt_, dap, sap) in dsts(wb):
                    store(S, dt_, dap, wb, sap)

            for i in range(3):
                v = w_br[i].t.rearrange("(kc p) c -> p kc c", p=128)
                dv = wbr_b[i].t.rearrange("p (kc c) -> p kc c", kc=4)
                for hb in range(2):
                    cast_block(w_br[i], v[:, :, hb * 512:(hb + 1) * 512], 4,
                               lambda wb, dv=dv, hb=hb, i=i: [(wbr_b[i], dv[:, :, hb * 512:(hb + 1) * 512], wb[:, 0:4, :])])
            v = w_out.t.rearrange("(kc p) c -> p kc c", p=128)
            dv = wout_b.t.rearrange("p (kc c) -> p kc c", kc=KC)
            for hb in range(2):
                cast_block(w_out, v[:, :, hb * 512:(hb + 1) * 512], KC,
                           lambda wb, dv=dv, hb=hb: [(wout_b, dv[:, :, hb * 512:(hb + 1) * 512], wb[:, :, :])])
            v = w_ff1.t.rearrange("(kc p) c -> p kc c", p=128)
            for fb in range(8):
                dv = w1_b.t[fb].rearrange("p (kc c) -> p kc c", kc=KC)
                cast_block(w_ff1, v[:, :, fb * 512:(fb + 1) * 512], KC,
                           lambda wb, dv=dv: [(w1_b, dv, wb[:, :, :])])
            v = w_ff2.t.rearrange("(kc p) c -> p kc c", p=128)
            for kg in range(4):
                for hb in range(2):
                    def dsts(wb, kg=kg, hb=hb):
                        r = []
                        for j in range(4):
                            fo = hb * 4 + j
                            dv = w2_b.t[fo].rearrange("p (kc c) -> p kc c", kc=32)
                            r.append((w2_b, dv[:, kg * 8:(kg + 1) * 8, :], wb[:, :, j * 128:(j + 1) * 128]))
                        return r
                    cast_block(w_ff2, v[:, kg * 8:(kg + 1) * 8, hb * 512:(hb + 1) * 512], KC, dsts)
            barrier(S)

        with contextlib.ExitStack() as es2:
            S.es = es2
            S.pool("kT", 2, [128, NKT * 128], BF16)
            S.pool("V", 2, [128, NKT * 128], BF16)
            S.pool("q", 2, [128, NTOK], BF16)
            S.pool("pT", 6, [128, 512], BF16)
            S.pool("f32a", 4, [128, 512], F32)
            S.pool("f32b", 4, [128, 512], F32)
            S.pool("osub", 4, [128, 512], F32)
            S.pool("saccd", 4, [128, 512], F32)
            S.pool("saccp", 4, [128, 512], F32)
            S.pool("bfa", 2, [128, 512], BF16)
            S.pool("bfo", 3, [128, 512], BF16)
            S.pool("st", 4, [128, 512], F32, space="psum")
            S.pool("acc", 4, [128, 512], F32, space="psum")
            for h in range(4):
                kT = S.get("kT")
                load(S, kT, kT[:], kT_da, kT_da[h * 128:(h + 1) * 128, :])
                V = S.get("V")
                load(S, V, V[:], v_da, v_da[h])
                q = S.get("q")
                load(S, q, q[:], qT_da, qT_da[h * 128:(h + 1) * 128, :])
                for ti, (t0, n) in enumerate(TT):
                    keys = list(range(NKT)) if ti < 8 else [128, 129]
                    accO = [S.get("acc"), S.get("acc")]
                    accS0 = S.get("acc")
                    sacc_d = [S.get("saccd"), S.get("saccd")]
                    sacc_p = [S.get("saccp"), S.get("saccp")]
                    sts = {}

                    def qk(j):
                        pair = []
                        for sub in range(2):
                            p0 = sub * 64
                            st = S.get("st")
                            S.op("pe", lambda e: e.matmul(st[:, 0:n], kT[p0:p0 + 64, j * 128:(j + 1) * 128],
                                                          q[p0:p0 + 64, t0:t0 + n], start=True, stop=True), [kT, q], [st])
                            pair.append(st)
                        sts[j] = pair

                    qk(keys[0])
                    for idx, j in enumerate(keys):
                        pair = sts.pop(j)
                        pTs = []
                        for sub in range(2):
                            pT = S.get("pT")
                            S.op("act", lambda e: e.activation(out=pT[:, 0:n], in_=pair[sub][:, 0:n], func=AF.Exp), [pair[sub]], [pT])
                            pTs.append(pT)
                        if idx + 1 < len(keys):
                            qk(keys[idx + 1])
                        first, last = idx == 0, idx == len(keys) - 1
                        for sub in range(2):
                            pT = pTs[sub]
                            S.op("pe", lambda e: e.matmul(accO[sub][:, 0:n], V[:, j * 128:(j + 1) * 128], pT[:, 0:n],
                                                          start=first, stop=last), [V, pT], [accO[sub]])
                            if sub == 0:
                                S.op("pe", lambda e: e.matmul(accS0[:, 0:n], onesb[:], pT[:, 0:n], start=first, stop=last),
                                     [onesb, pT], [accS0])
                                continue
                            en_, sa_ = ("dve", sacc_d[sub]) if idx % 2 == 0 else ("pool", sacc_p[sub])
                            if idx < 2:
                                S.op(en_, lambda e: e.tensor_copy(out=sa_[:, 0:n], in_=pT[:, 0:n]), [pT], [sa_])
                            else:
                                S.op(en_, lambda e: e.tensor_tensor(out=sa_[:, 0:n], in0=sa_[:, 0:n], in1=pT[:, 0:n], op=ALU.add),
                                     [sa_, pT], [sa_])
                    osub = []
                    for sub in range(2):
                        if sub == 0:
                            accS = accS0
                        else:
                            accS = S.get("acc")
                            S.op("pe", lambda e: e.matmul(accS[:, 0:n], ones32[:], sacc_d[sub][:, 0:n], start=True, stop=False),
                                 [ones32, sacc_d[sub]], [accS])
                            S.op("pe", lambda e: e.matmul(accS[:, 0:n], ones32[:], sacc_p[sub][:, 0:n], start=False, stop=True),
                                 [ones32, sacc_p[sub]], [accS])
                        rec = S.get("f32a")
                        S.op("dve", lambda e: e.reciprocal(out=rec[:, 0:n], in_=accS[:, 0:n]), [accS], [rec])
                        os_ = S.get("osub")
                        S.op("dve", lambda e: e.tensor_tensor(out=os_[:, 0:n], in0=accO[sub][:, 0:n], in1=rec[:, 0:n], op=ALU.mult),
                             [accO[sub], rec], [os_])
                        osub.append(os_)
                    o = S.get("f32b")
                    S.op("dve", lambda e: e.scalar_tensor_tensor(out=o[:, 0:n], in0=osub[1][:, 0:n], scalar=neglam,
                                                                 in1=osub[0][:, 0:n], op0=ALU.mult, op1=ALU.add),
                         [osub[0], osub[1], lsum], [o])
                    sq = S.get("bfa")
                    S.op("act", lambda e: e.activation(out=sq[:, 0:n], in_=o[:, 0:n], func=AF.Square), [o], [sq])
                    ms = S.get("acc")
                    S.op("pe", lambda e: e.matmul(ms[:, 0:n], o128[:], sq[:, 0:n], start=True, stop=True), [o128, sq], [ms])
                    rstd = rstd_from(ms, n, "f32a")
                    y = S.get("bfo")
                    S.op("dve", lambda e: e.scalar_tensor_tensor(out=y[:, 0:n], in0=o[:, 0:n], scalar=gsub, in1=rstd[:, 0:n],
                                                                 op0=ALU.mult, op1=ALU.mult), [o, lsum, rstd], [y])
                    store(S, ys[0], ys[0][h * 128:(h + 1) * 128, t0:t0 + n], y, y[:, 0:n])
            barrier(S)

        with contextlib.ExitStack() as es2:
            S.es = es2
            tri32 = S.sbuf("tri32", [128, 768], F32)
            load(S, tri32, tri32[:], trim, trim[:, :])
            onec = S.sbuf("onec", [128, 1], F32)
            S.op("pool", lambda e: e.memset(onec[:], 1.0), [], [onec])
            og = [S.sbuf("og%d" % h, [128, NTOK], F32) for h in range(4)]
            Sst = [S.sbuf("Sst%d" % p, [128, 256], F32) for p in range(2)]
            Sb = [S.sbuf("Sb%d" % p, [128, 256], BF16) for p in range(2)]
            S.pool("la", 3, [128, 256], F32)
            S.pool("k", 3, [128, 256], BF16)
            S.pool("v", 3, [128, 512], BF16)
            S.pool("qk", 3, [128, 4, 128], BF16)
            S.pool("ekd", 2, [128, 256], F32)
            S.pool("kd", 2, [128, 256], BF16)
            S.pool("eqk", 4, [128, 128], F32)
            S.pool("qeke", 4, [128, 128], BF16)
            S.pool("aTm", 3, [128, 128], BF16)
            S.pool("dcol", 4, [128, 1], F32)
            S.pool("f32a", 4, [128, 512], F32)
            S.pool("bfa", 2, [128, 512], BF16)
            S.pool("bfo", 3, [128, 512], BF16)
            S.pool("gg", 2, [128, 512], BF16)
            S.pool("pex", 1, [128, 512], F32, space="psum")
            S.pool("pcum", 2, [128, 128], F32, space="psum")
            S.pool("paT", 2, [128, 128], F32, space="psum")
            S.pool("po", 2, [128, 128], F32, space="psum")
            S.pool("pds", 1, [128, 256], F32, space="psum")

            def gla_tile(d, la_t, la_ap, k_t, k_ap, v_t, v_ap, full, tcol):
                b = d * 384
                Tstr, Tincl, Mask = tri32[:, b:b + 128], tri32[:, b + 128:b + 256], tri32[:, b + 256:b + 384]
                la = S.get("la")
                load(S, la, la[:], la_t, la_ap)
                kk = S.get("k")
                load(S, kk, kk[:], k_t, k_ap)
                vv = S.get("v")
                load(S, vv, vv[:], v_t, v_ap)
                pex = S.get("pex")
                S.op("pe", lambda e: e.matmul(pex[:, 0:256], Tstr, la[:], start=True, stop=True), [tri32, la], [pex])
                ekd = S.get("ekd")
                S.op("act", lambda e: e.activation(out=ekd[:], in_=pex[:, 0:256], func=AF.Exp), [pex], [ekd])
                kd = S.get("kd")
                S.op("dve", lambda e: e.tensor_tensor(out=kd[:], in0=kk[:], in1=ekd[:], op=ALU.mult), [kk, ekd], [kd])
                if full:
                    qk = S.get("qk")
                    load(S, qk, qk[:, 0:2, :], qT_gl, qT_gl.t.rearrange("(pr p) t -> p pr t", p=128)[:, :, tcol:tcol + 128])
                    load(S, qk, qk[:, 2:4, :], kT_gl, kT_gl.t.rearrange("(pr p) t -> p pr t", p=128)[:, :, tcol:tcol + 128])
                for pr in range(2):
                    dcol = S.get("dcol")
                    if full:
                        pc = S.get("pcum")
                        S.op("pe", lambda e: e.matmul(pc[:], la[:, pr * 128:(pr + 1) * 128], Tincl, start=True, stop=True),
                             [la, tri32], [pc])
                        eq = S.get("eqk")
                        S.op("act", lambda e: e.activation(out=eq[:], in_=pc[:], func=AF.Exp), [pc], [eq])
                        ek = S.get("eqk")
                        S.op("act", lambda e: e.activation(out=ek[:], in_=pc[:], func=AF.Exp, scale=-1.0), [pc], [ek])
                        qe = S.get("qeke")
                        S.op("dve", lambda e: e.tensor_tensor(out=qe[:], in0=qk[:, pr, :], in1=eq[:], op=ALU.mult), [qk, eq], [qe])
                        ke = S.get("qeke")
                        S.op("pool", lambda e: e.tensor_tensor(out=ke[:], in0=qk[:, 2 + pr, :], in1=ek[:], op=ALU.mult), [qk, ek], [ke])
                        lastc = 127 if d == 0 else 0
                        S.op("act", lambda e: e.activation(out=dcol[:], in_=eq[:, lastc:lastc + 1], func=AF.Copy), [eq], [dcol])
                        for hl in range(2):
                            h = pr * 2 + hl
                            p0 = hl * 64
                            pa = S.get("paT")
                            S.op("pe", lambda e: e.matmul(pa[:], ke[p0:p0 + 64, :], qe[p0:p0 + 64, :], start=True, stop=True),
                                 [ke, qe], [pa])
                            am = S.get("aTm")
                            S.op("dve", lambda e: e.tensor_tensor(out=am[:], in0=pa[:], in1=Mask, op=ALU.mult), [pa, tri32], [am])
                            po = S.get("po")
                            S.op("pe", lambda e: e.matmul(po[:], vv[:, h * 128:(h + 1) * 128], am[:], start=True, stop=False),
                                 [vv, am], [po])
                            S.op("pe", lambda e: e.matmul(po[:], Sb[pr][p0:p0 + 64, hl * 128:(hl + 1) * 128], qe[p0:p0 + 64, :],
                                                          start=False, stop=True), [Sb[pr], qe], [po])
                            if d == 0:
                                S.op("act", lambda e: e.activation(out=og[h][:, tcol:tcol + 128], in_=po[:], func=AF.Copy),
                                     [po], [og[h]])
                            else:
                                S.op("dve", lambda e: e.tensor_tensor(out=og[h][:, tcol:tcol + 128], in0=po[:],
                                                                      in1=og[h][:, tcol:tcol + 128], op=ALU.add), [po, og[h]], [og[h]])
                    else:
                        pc = S.get("pcum")
                        S.op("pe", lambda e: e.matmul(pc[:, 0:1], la[:, pr * 128:(pr + 1) * 128], onec[:], start=True, stop=True),
                             [la, onec], [pc])
                        S.op("act", lambda e: e.activation(out=dcol[:], in_=pc[:, 0:1], func=AF.Exp), [pc], [dcol])
                    pds = S.get("pds")
                    S.op("pe", lambda e: e.matmul(pds[:], kd[:, pr * 128:(pr + 1) * 128], vv[:, pr * 256:(pr + 1) * 256],
                                                  start=True, stop=True), [kd, vv], [pds])
                    S.op("dve", lambda e: e.scalar_tensor_tensor(out=Sst[pr][:], in0=Sst[pr][:], scalar=dcol[:, 0:1], in1=pds[:],
                                                                 op0=ALU.mult, op1=ALU.add), [Sst[pr], dcol, pds], [Sst[pr]])
                    S.op("pool", lambda e: e.tensor_copy(out=Sb[pr][:], in_=Sst[pr][:]), [Sst[pr]], [Sb[pr]])

            for d in range(2):
                for pr in range(2):
                    S.op("pool", lambda e: e.memset(Sst[pr][:], 0.0), [], [Sst[pr]])
                    S.op("pool", lambda e: e.memset(Sb[pr][:], 0.0), [], [Sb[pr]])
                own = lambda t: (la_d, la_d[t * 128:(t + 1) * 128, d * 256:(d + 1) * 256], k_gl, k_gl[t * 128:(t + 1) * 128, :],
                                 v_gl, v_gl[t * 128:(t + 1) * 128, :])
                pre = lambda t: (pf_la, pf_la[d, t * 128:(t + 1) * 128, :], pf_k, pf_k[d, t * 128:(t + 1) * 128, :],
                                 pf_v, pf_v[d, t * 128:(t + 1) * 128, :])
                order = (lambda r: list(r)) if d == 0 else (lambda r: list(r)[::-1])
                for t in order(range(32, 34)):
                    gla_tile(d, *own(t), True, t * 128)
                for t in order(range(NPF)):
                    gla_tile(d, *pre(t), False, 0)
                for t in order(range(32)):
                    gla_tile(d, *own(t), True, t * 128)
            for h in range(4):
                for (t0, n) in TT:
                    sq = S.get("bfa")
                    S.op("act", lambda e: e.activation(out=sq[:, 0:n], in_=og[h][:, t0:t0 + n], func=AF.Square), [og[h]], [sq])
                    ms = S.get("pex")
                    S.op("pe", lambda e: e.matmul(ms[:, 0:n], o128[:], sq[:, 0:n], start=True, stop=True), [o128, sq], [ms])
                    rstd = rstd_from(ms, n, "f32a")
                    g = S.get("gg")
                    load(S, g, g[:, 0:n], ggT, ggT[h * 128:(h + 1) * 128, t0:t0 + n])
                    y0 = S.get("f32a")
                    S.op("dve", lambda e: e.scalar_tensor_tensor(out=y0[:, 0:n], in0=og[h][:, t0:t0 + n], scalar=gv[:, 1:2],
                                                                 in1=rstd[:, 0:n], op0=ALU.mult, op1=ALU.mult), [og[h], gv, rstd], [y0])
                    y = S.get("bfo")
                    S.op("pool", lambda e: e.tensor_tensor(out=y[:, 0:n], in0=y0[:, 0:n], in1=g[:, 0:n], op=ALU.mult), [y0, g], [y])
                    store(S, ys[1], ys[1][h * 128:(h + 1) * 128, t0:t0 + n], y, y[:, 0:n])
            barrier(S)

        with contextlib.ExitStack() as es2:
            S.es = es2
            kTn = S.sbuf("kTn", [128, 4, NA_NTOK], BF16)
            load(S, kTn, kTn[:], kT_na, kT_na.t.rearrange("(c p) t -> p c t", p=128))
            Vn = S.sbuf("Vn", [128, NA_NTOK // 128, 512], BF16)
            load(S, Vn, Vn[:], v_na, v_na.t.rearrange("(t p) c -> p t c", p=128))
            nb = S.sbuf("nb", [128, 56 * 128], F32)
            load(S, nb, nb[:], nbias, nbias[:, :])
            nm = S.sbuf("nm", [128, 35 * 128], F32)
            load(S, nm, nm[:], nmask, nmask[:, :])
            S.pool("q", 2, [128, NTOK], BF16)
            S.pool("s1", 3, [128, 128], F32)
            S.pool("s2", 3, [128, 128], F32)
            S.pool("pT", 4, [128, 128], BF16)
            S.pool("rec", 2, [128, 128], F32)
            S.pool("yt", 3, [128, 128], BF16)
            S.pool("st", 3, [128, 128], F32, space="psum")
            S.pool("acc", 4, [128, 128], F32, space="psum")
            for c in range(4):
                q = S.get("q")
                load(S, q, q[:], qT_na, qT_na[c * 128:(c + 1) * 128, :])
                for qt in range(34):
                    tcol = qt * 128
                    if qt < 32:
                        units = [(qt + o, o) for o in range(7)] + [(38, None), (39, None)]
                        cls = {0: 0, 1: 1, 30: 3, 31: 4}.get(qt, 2)
                    else:
                        units = [(38, None), (39, None)]
                        cls = 2
                    yt = S.get("yt")
                    for hl in range(2):
                        h = 2 * c + hl
                        p0 = hl * 64
                        accO = S.get("acc")
                        accS = S.get("acc")
                        nsts = {}

                        def nqk(ui_):
                            kt_ = units[ui_][0]
                            st_ = S.get("st")
                            S.op("pe", lambda e: e.matmul(st_[:], kTn[p0:p0 + 64, c, kt_ * 128:(kt_ + 1) * 128],
                                                          q[p0:p0 + 64, tcol:tcol + 128], start=True, stop=True), [kTn, q], [st_])
                            nsts[ui_] = st_

                        NLA = 2
                        for u0 in range(min(NLA, len(units))):
                            nqk(u0)
                        for ui, (kt, o) in enumerate(units):
                            st = nsts.pop(ui)
                            if ui + NLA < len(units):
                                nqk(ui + NLA)
                            pT = S.get("pT")
                            if o is not None:
                                s1 = S.get("s1")
                                bo = (h * 7 + o) * 128
                                S.op("dve", lambda e: e.tensor_tensor(out=s1[:], in0=st[:], in1=nb[:, bo:bo + 128], op=ALU.add),
                                     [st, nb], [s1])
                                s2 = S.get("s2")
                                mo = (cls * 7 + o) * 128
                                S.op("pool", lambda e: e.tensor_tensor(out=s2[:], in0=s1[:], in1=nm[:, mo:mo + 128], op=ALU.add),
                                     [s1, nm], [s2])
                                S.op("act", lambda e: e.activation(out=pT[:], in_=s2[:], func=AF.Exp), [s2], [pT])
                            else:
                                S.op("act", lambda e: e.activation(out=pT[:], in_=st[:], func=AF.Exp), [st], [pT])
                            first, last = ui == 0, ui == len(units) - 1
                            S.op("pe", lambda e: e.matmul(accO[:], Vn[:, kt, c * 128:(c + 1) * 128], pT[:], start=first, stop=last),
                                 [Vn, pT], [accO])
                            S.op("pe", lambda e: e.matmul(accS[:], onesb[:], pT[:], start=first, stop=last), [onesb, pT], [accS])
                        rec = S.get("rec")
                        S.op("dve", lambda e: e.reciprocal(out=rec[p0:p0 + 64, :], in_=accS[p0:p0 + 64, :]), [accS], [rec])
                        S.op("dve", lambda e: e.tensor_tensor(out=yt[p0:p0 + 64, :], in0=accO[p0:p0 + 64, :], in1=rec[p0:p0 + 64, :],
                                                              op=ALU.mult), [accO, rec], [yt])
                    store(S, ys[2], ys[2][c * 128:(c + 1) * 128, tcol:tcol + 128], yt, yt[:])
            barrier(S)

        with contextlib.ExitStack() as es2:
            S.es = es2
            wbr = S.sbuf("wbr", [128, 12, D], BF16)
            for i in range(3):
                load(S, wbr, wbr[:, i * 4:(i + 1) * 4, :], wbr_b[i], wbr_b[i].t.rearrange("p (kc c) -> p kc c", kc=4))
            wo = S.sbuf("wo", [128, KC, D], BF16)
            load(S, wo, wo[:], wout_b, wout_b.t.rearrange("p (kc c) -> p kc c", kc=KC))
            S.pool("x", 1, [128, KC, 512], F32)
            S.pool("y", 1, [128, 12, 512], BF16)
            S.pool("g3", 2, [128, 3, 512], BF16)
            S.pool("m", 1, [128, KC, 512], BF16)
            S.pool("sq8", 1, [128, KC, 512], BF16)
            S.pool("h2", 1, [128, KC, 512], BF16)
            S.pool("a", 1, [128, 32, 512], BF16)
            S.pool("w1", 2, [128, KC, 512], BF16)
            S.pool("w2", 2, [128, 32, 128], BF16)
            S.pool("f32a", 4, [128, 512], F32)
            S.pool("f32b", 4, [128, 512], F32)
            S.pool("xo", 3, [128, 512], F32)
            S.pool("ps", 8, [128, 512], F32, space="psum")
            gv3 = gates.t.rearrange("(br fo p) t -> p br fo t", br=3, fo=8)
            xv = xT.t.rearrange("(kc p) t -> p kc t", p=128)
            for ti, (t0, n) in enumerate(TT):
                col = 0 if ti < 8 else 1
                mcol = lambda ch: modT[:, ch * 2 + col:ch * 2 + col + 1]
                x = S.get("x")
                load(S, x, x[:, :, 0:n], xT, xv[:, :, t0:t0 + n])
                y = S.get("y")
                for br in range(3):
                    load(S, y, y[:, br * 4:(br + 1) * 4, 0:n], ys[br], ys[br].t.rearrange("(kc p) t -> p kc t", p=128)[:, :, t0:t0 + n])
                m = S.get("m")
                for fo in range(8):
                    g3 = S.get("g3")
                    load(S, g3, g3[:, :, 0:n], gates, gv3[:, :, fo, t0:t0 + n])
                    tmps = []
                    for br in range(3):
                        ps = S.get("ps")
                        for kc in range(4):
                            S.op("pe", lambda e, kc=kc: e.matmul(ps[:, 0:n], wbr[:, br * 4 + kc, fo * 128:(fo + 1) * 128],
                                                                  y[:, br * 4 + kc, 0:n], start=(kc == 0), stop=(kc == 3)), [wbr, y], [ps])
                        tb = S.get("f32a")
                        S.op("dve", lambda e: e.tensor_tensor(out=tb[:, 0:n], in0=ps[:, 0:n], in1=g3[:, br, 0:n], op=ALU.mult),
                             [ps, g3], [tb])
                        tmps.append(tb)
                    s01 = S.get("f32b")
                    S.op("pool", lambda e: e.tensor_tensor(out=s01[:, 0:n], in0=tmps[0][:, 0:n], in1=tmps[1][:, 0:n], op=ALU.add),
                         [tmps[0], tmps[1]], [s01])
                    S.op("pool", lambda e: e.tensor_tensor(out=m[:, fo, 0:n], in0=s01[:, 0:n], in1=tmps[2][:, 0:n], op=ALU.add),
                         [s01, tmps[2]], [m])
                for fo in range(8):
                    ps = S.get("ps")
                    for kc in range(KC):
                        S.op("pe", lambda e, kc=kc: e.matmul(ps[:, 0:n], wo[:, kc, fo * 128:(fo + 1) * 128], m[:, kc, 0:n],
                                                              start=(kc == 0), stop=(kc == KC - 1)), [wo, m], [ps])
                    S.op("dve", lambda e: e.scalar_tensor_tensor(out=x[:, fo, 0:n], in0=ps[:, 0:n], scalar=mcol(16 + fo),
                                                                 in1=x[:, fo, 0:n], op0=ALU.mult, op1=ALU.add), [ps, modT, x], [x])
                sq = S.get("sq8")
                S.op("act", lambda e: e.activation(out=sq[:, :, 0:n], in_=x[:, :, 0:n], func=AF.Square), [x], [sq])
                ps = S.get("ps")
                for kc in range(KC):
                    S.op("pe", lambda e, kc=kc: e.matmul(ps[:, 0:n], o1024[:], sq[:, kc, 0:n], start=(kc == 0), stop=(kc == KC - 1)),
                         [o1024, sq], [ps])
                rstd = rstd_from(ps, n, "f32a")
                h2 = S.get("h2")
                for kc in range(KC):
                    tmp = S.get("f32b")
                    S.op("dve", lambda e, kc=kc: e.scalar_tensor_tensor(
                        out=tmp[:, 0:n], in0=x[:, kc, 0:n], scalar=A2[:, kc * 2 + col:kc * 2 + col + 1], in1=rstd[:, 0:n],
                        op0=ALU.mult, op1=ALU.mult), [x, A2, rstd], [tmp])
                    S.op("act", lambda e, kc=kc: e.activation(out=h2[:, kc, 0:n], in_=tmp[:, 0:n], func=AF.Identity,
                                                              bias=mcol(24 + kc), scale=1.0), [tmp, modT], [h2])
                a = S.get("a")
                for fb in range(8):
                    w1 = S.get("w1")
                    load(S, w1, w1[:], w1_b, w1_b.t[fb].rearrange("p (kc c) -> p kc c", kc=KC))
                    for j in range(4):
                        f = fb * 4 + j
                        ps = S.get("ps")
                        for kc in range(KC):
                            S.op("pe", lambda e, kc=kc: e.matmul(ps[:, 0:n], w1[:, kc, j * 128:(j + 1) * 128], h2[:, kc, 0:n],
                                                                  start=(kc == 0), stop=(kc == KC - 1)), [w1, h2], [ps])
                        r = S.get("f32a")
                        S.op("act", lambda e: e.activation(out=r[:, 0:n], in_=ps[:, 0:n], func=AF.Relu), [ps], [r])
                        S.op("pool", lambda e: e.tensor_tensor(out=a[:, f, 0:n], in0=r[:, 0:n], in1=r[:, 0:n], op=ALU.mult), [r], [a])
                for fo in range(8):
                    w2 = S.get("w2")
                    load(S, w2, w2[:], w2_b, w2_b.t[fo].rearrange("p (kc c) -> p kc c", kc=32))
                    ps = S.get("ps")
                    for f in range(32):
                        S.op("pe", lambda e, f=f: e.matmul(ps[:, 0:n], w2[:, f, :], a[:, f, 0:n], start=(f == 0), stop=(f == 31)),
                             [w2, a], [ps])
                    xt = S.get("xo")
                    S.op("dve", lambda e: e.scalar_tensor_tensor(out=xt[:, 0:n], in0=ps[:, 0:n], scalar=mcol(40 + fo),
                                                                 in1=x[:, fo, 0:n], op0=ALU.mult, op1=ALU.add), [ps, modT, x], [xt])
                    store(S, xo, xo[fo * 128:(fo + 1) * 128, t0:t0 + n], xt, xt[:, 0:n])
            S.finish()
            barrier(S)
        S.es = es
    return nc


def tri_consts():
    i = np.arange(128)
    sp, s = i[:, None], i[None, :]
    mats = [(sp > s), (sp <= s), (sp <= s), (sp < s), (sp >= s), (sp >= s)]
    return np.concatenate([m.astype(np.float32) for m in mats], axis=1)


def na_bias_table(rpb):
    ka = np.arange(128)
    a, kc = ka // 64, ka % 64
    bq, cq = ka // 64, ka % 64
    out = np.zeros((128, 8, 7, 128), np.float32)
    for o in range(7):
        rel_row = (-6 + 2 * o + a)[:, None] - bq[None, :] + 7
        rel_col = kc[:, None] - cq[None, :] + 15
        ok = (rel_row >= 0) & (rel_row <= 14) & (rel_col >= 0) & (rel_col <= 30)
        rr = np.clip(rel_row, 0, 14)
        rc = np.clip(rel_col, 0, 30)
        for h in range(8):
            out[:, h, o, :] = np.where(ok, rpb[h][rr, rc], 0.0)
    return out.reshape(128, -1)


def na_mask_table(qtr):
    ka = np.arange(128)
    a, kc = ka // 64, ka % 64
    bq, cq = ka // 64, ka % 64
    out = np.zeros((128, 5, 7, 128), np.float32)
    for cls, qt in enumerate((0, 1, 2, 30, 31)):
        Rq = qtr * 64 + 2 * qt
        R = Rq + bq
        rs = np.clip(R - 4, 0, 256 - 8)
        cs = np.clip(cq - 8, 0, GRID_W - 16)
        for o in range(7):
            kr = (Rq - 6 + 2 * o + a)[:, None]
            ok = (kr >= rs[None, :]) & (kr < rs[None, :] + 8) & (kc[:, None] >= cs[None, :]) & (kc[:, None] < cs[None, :] + 16)
            out[:, cls, o, :] = np.where(ok, 0.0, -30000.0)
    return out.reshape(128, -1)


def lb_inputs(l, core, xT_core, ra, inp):
    b, qtr = core // 4, core % 4
    grp = [ra[4 * b + j] for j in range(4)]
    me = ra[core]
    bf = ml_dtypes.bfloat16
    kT_da = np.concatenate([np.asarray(g["kT_da"])[:, :NLAT] for g in grp] + [np.asarray(me["kT_da"])[:, NLAT:]], axis=1)
    v_all = np.concatenate([np.asarray(g["v_da"])[:NLAT] for g in grp] + [np.asarray(me["v_da"])[NLAT:]], axis=0)
    v_da = np.ascontiguousarray(v_all.reshape(NKT, 128, 4, 128).transpose(2, 1, 0, 3).reshape(4, 128, NKT * 128))
    lam_init = 0.8 - 0.6 * math.exp(-0.3 * l)
    npf = NPF * 128
    pf_la = np.zeros((2, npf, 256), np.float32)
    pf_k = np.zeros((2, npf, 256), bf)
    pf_v = np.zeros((2, npf, 512), bf)
    nb_ = qtr
    if nb_:
        pf_la[0, :nb_ * NLAT] = np.concatenate([np.asarray(grp[j]["la"])[:NLAT, 0:256] for j in range(qtr)], 0)
        pf_k[0, :nb_ * NLAT] = np.concatenate([np.asarray(grp[j]["k_gl"])[:NLAT] for j in range(qtr)], 0)
        pf_v[0, :nb_ * NLAT] = np.concatenate([np.asarray(grp[j]["v_gl"])[:NLAT] for j in range(qtr)], 0)
    na_ = 3 - qtr
    if na_:
        pf_la[1, npf - na_ * NLAT:] = np.concatenate([np.asarray(grp[j]["la"])[:NLAT, 256:512] for j in range(qtr + 1, 4)], 0)
        pf_k[1, npf - na_ * NLAT:] = np.concatenate([np.asarray(grp[j]["k_gl"])[:NLAT] for j in range(qtr + 1, 4)], 0)
        pf_v[1, npf - na_ * NLAT:] = np.concatenate([np.asarray(grp[j]["v_gl"])[:NLAT] for j in range(qtr + 1, 4)], 0)
    kn_all = np.concatenate([np.asarray(g["kT_na"])[:, :NLAT] for g in grp], axis=1)
    vn_all = np.concatenate([np.asarray(g["v_na"])[:NLAT] for g in grp], axis=0)
    lo = (qtr * 64 - 6) * GRID_W
    hi = lo + NHALO
    kT_na = np.zeros((512, NA_NTOK), bf)
    v_na = np.zeros((NA_NTOK, 512), bf)
    s0, s1 = max(lo, 0), min(hi, SEQ)
    kT_na[:, s0 - lo:s1 - lo] = kn_all[:, s0:s1]
    v_na[s0 - lo:s1 - lo] = vn_all[s0:s1]
    kT_na[:, NHALO:] = np.asarray(me["kT_na"])[:, NLAT:]
    v_na[NHALO:] = np.asarray(me["v_na"])[NLAT:]
    return {
        "xT": xT_core, "modT": np.asarray(me["modT"]), "n2g": chunkT(inp["norm2_g"][l]),
        "qT_da": np.asarray(me["qT_da"]), "kT_da": np.ascontiguousarray(kT_da), "v_da": v_da,
        "lamtab": np.ascontiguousarray(np.tile(inp["da_lambda"][l].reshape(1, 256), (128, 1)).astype(np.float32)),
        "lconst": np.tile(np.array([[lam_init, 1.0 - lam_init]], np.float32), (128, 1)),
        "gvec": np.stack([inp["da_subln_g"][l], inp["gla_gn_g"][l]], 1).astype(np.float32),
        "qT_gl": np.asarray(me["qT_gl"]), "kT_gl": np.asarray(me["kT_gl"]), "k_gl": np.asarray(me["k_gl"]),
        "v_gl": np.asarray(me["v_gl"]), "la": np.asarray(me["la"]), "ggT": np.asarray(me["ggT"]),
        "pf_la": pf_la, "pf_k": pf_k, "pf_v": pf_v, "trim": tri_consts(),
        "qT_na": np.asarray(me["qT_na"]), "kT_na": kT_na, "v_na": v_na,
        "nbias": na_bias_table(np.asarray(inp["na_rpb"][l], np.float32)), "nmask": na_mask_table(qtr),
        "gates": np.asarray(me["gates"]),
        "w_br0": inp["w_br_da"][l], "w_br1": inp["w_br_gla"][l], "w_br2": inp["w_br_na"][l],
        "w_out": inp["w_out"][l], "w_ff1": inp["w_ff1"][l], "w_ff2": inp["w_ff2"][l],
    }


def run_lb(l, xTs, ra, inp):
    if "lb" not in _NC:
        _NC["lb"] = build_lb()
    in_maps = [lb_inputs(l, c, xTs[c], ra, inp) for c in range(NCORES)]
    res = run_bass_kernel_spmd(_NC["lb"], in_maps, core_ids=list(range(NCORES)))
    return res.results


def kernel(**inputs):
    inp = {k: np.asarray(v) for k, v in inputs.items()}
    xTs = make_xT(inp["x"], inp["ctx"])
    for l in range(2):
        ra = run_la(l, xTs, inp)
        rb = run_lb(l, xTs, ra, inp)
        xTs = [np.ascontiguousarray(np.asarray(rb[c]["xo"], dtype=np.float32)) for c in range(NCORES)]
    out = np.empty((2, SEQ, D), np.float32)
    for c in range(NCORES):
        b, qtr = c // 4, c % 4
        out[b, qtr * NLAT:(qtr + 1) * NLAT] = xTs[c][:, :NLAT].T
    return out
```

```python
import contextlib
import math
import numpy as np
import ml_dtypes
import concourse.bass as bass
import concourse.mybir as mybir
from concourse.bass_utils import run_bass_kernel_spmd

F32 = mybir.dt.float32
BF16 = mybir.dt.bfloat16
AF = mybir.ActivationFunctionType
ALU = mybir.AluOpType
AX = mybir.AxisListType

NCORES = 8
D = 1024
KC = 8
SEQ = 16384
NLAT = 4096
NCTX = 256
NTOK = NLAT + NCTX
GRID_W = 64
EPS = 1e-6
N_IN = 7712
TT = [(i * 512, 512) for i in range(8)] + [(4096, 256)]


class Tile:
    def __init__(self, t, name):
        self.t = t
        self.name = name
        self.w = {}
        self.r = {}
        self.sem = None
        self.semval = 0
        self.track = True

    def __getitem__(self, k):
        return self.t[k]


def _merge(dst, src):
    for k, v in src.items():
        if dst.get(k, 0) < v:
            dst[k] = v


class Sched:
    def __init__(self, nc, es):
        self.nc = nc
        self.es = es
        self.eng = {"pe": nc.tensor, "act": nc.scalar, "dve": nc.vector, "pool": nc.gpsimd, "sp": nc.sync}
        self.sem = {}
        self.cnt = {}
        for k in ("pe", "act", "dve", "pool"):
            self.sem[k] = es.enter_context(nc.semaphore("s_" + k))
            self.cnt[k] = 0
        self.seen = {k: {} for k in self.eng}
        self.pending = {}
        self.nsem = 0
        self.pools = {}

    def sbuf(self, name, shape, dtype):
        self.uid = getattr(self, "uid", 0) + 1
        return Tile(self.es.enter_context(self.nc.sbuf_tensor("sb%d_%s" % (self.uid, name), list(shape), dtype)),
                    "%s_%d" % (name, self.uid))

    def psum(self, name, shape, dtype=F32):
        self.uid = getattr(self, "uid", 0) + 1
        return Tile(self.es.enter_context(self.nc.psum_tensor("pp%d_%s" % (self.uid, name), list(shape), dtype)),
                    "%s_%d" % (name, self.uid))

    def dram(self, name, shape, dtype, kind):
        t = self.nc.dram_tensor(name, list(shape), dtype, kind=kind)
        tl = Tile(t.ap(), name)
        tl.track = False
        return tl

    def pool(self, tag, n, shape, dtype, space="sbuf"):
        mk = self.sbuf if space == "sbuf" else self.psum
        self.pools[tag] = [[mk("%s%d" % (tag, i), shape, dtype) for i in range(n)], 0]

    def get(self, tag):
        p = self.pools[tag]
        t = p[0][p[1] % len(p[0])]
        p[1] += 1
        return t

    def _wait(self, en, deps):
        seen = self.seen[en]
        for k, v in deps.items():
            if k == "pe" and en == "pe":
                continue
            if seen.get(k, 0) >= v:
                continue
            self.eng[en].wait_ge(self.sem[k], v)
            seen[k] = v

    def op(self, en, fn, reads=(), writes=()):
        deps = {}
        for t in reads:
            _merge(deps, t.w)
        for t in writes:
            _merge(deps, t.w)
            _merge(deps, t.r)
        self._wait(en, deps)
        ins = fn(self.eng[en])
        self.cnt[en] += 1
        ev = {en: self.cnt[en]}
        ins.then_inc(self.sem[en], 1)
        for t in reads:
            _merge(t.r, ev)
        for t in writes:
            t.w = dict(ev)
            t.r = {}
        return ins

    def dma(self, dst, dst_ap, src, src_ap, owner):
        if owner.sem is None:
            self.nsem += 1
            owner.sem = "d%d_%s" % (self.nsem, owner.name)
            self.sem[owner.sem] = self.es.enter_context(self.nc.semaphore(owner.sem))
        deps = {}
        if src.track:
            _merge(deps, src.w)
        if dst.track:
            _merge(deps, dst.w)
            _merge(deps, dst.r)
        self._wait("sp", deps)
        owner.semval += 16
        ev = {owner.sem: owner.semval}
        self.nc.sync.dma_start(out=dst_ap, in_=src_ap).then_inc(self.sem[owner.sem], 16)
        if src.track:
            _merge(src.r, ev)
        if dst.track:
            dst.w = dict(ev)
            dst.r = {}
        _merge(self.pending, ev)

    def finish(self):
        self._wait("sp", self.pending)


def load(S, dst, dst_ap, src, src_ap):
    S.dma(dst, dst_ap, src, src_ap, owner=dst)


def store(S, dst, dst_ap, src, src_ap):
    S.dma(dst, dst_ap, src, src_ap, owner=src)


C_DAQ, C_DAK, C_DAV = 0, 512, 1024
C_GQ, C_GK, C_GV, C_GG, C_GA = 1536, 1792, 2048, 2560, 3072
C_NQ, C_NK, C_NV = 3104, 3616, 4128
C_GATE = 4640

LA_OUT = [("qT_da", [512, NTOK], BF16), ("kT_da", [512, NTOK], BF16), ("v_da", [NTOK, 512], BF16),
          ("qT_gl", [256, NTOK], BF16), ("kT_gl", [256, NTOK], BF16), ("k_gl", [NTOK, 256], BF16),
          ("v_gl", [NTOK, 512], BF16), ("ggT", [512, NTOK], BF16), ("la", [NTOK, 512], F32),
          ("qT_na", [512, NTOK], BF16), ("kT_na", [512, NTOK], BF16), ("v_na", [NTOK, 512], BF16),
          ("gates", [3072, NTOK], BF16), ("modT", [128, 96], F32)]


def build_la():
    nc = bass.Bass("TRN2", target_bir_lowering=False)
    es = contextlib.ExitStack()
    with es:
        S = Sched(nc, es)
        di = lambda n, s, d=F32: S.dram(n, s, d, "ExternalInput")
        xT = di("xT", [D, NTOK])
        cT = di("cT", [128, KC * 2])
        w_mod = di("w_mod", [D, 6 * D])
        b_modT = di("b_modT", [128, 48])
        n1g = di("n1g", [128, KC])
        w_in = di("w_in", [D, N_IN])
        gcols = di("gcols", [128, 4])
        ropeC = di("ropeC", [128, NTOK])
        ropeS = di("ropeS", [128, NTOK])
        cmats = di("cmats", [128, 3 * 128])
        a2aug = di("a2aug", [33, 512])
        outs = {n: S.dram(n, s, d, "ExternalOutput") for n, s, d in LA_OUT}

        hT = S.sbuf("hT", [128, KC, NTOK], BF16)
        S.pool("stage", 2, [128, KC, 512], F32)
        S.pool("wb", 2, [128, KC, 512], BF16)
        S.pool("sq8", 1, [128, KC, 512], BF16)
        S.pool("rope", 4, [128, 512], F32)
        S.pool("f32a", 4, [128, 512], F32)
        S.pool("f32b", 4, [128, 512], F32)
        S.pool("bfa", 3, [128, 512], BF16)
        S.pool("bfo", 4, [128, 512], BF16)
        S.pool("ps", 7, [128, 512], F32, space="psum")
        cm32 = S.sbuf("cm32", [128, 384], F32)
        cmb = S.sbuf("cmb", [128, 384], BF16)
        cTs = S.sbuf("cTs", [128, KC * 2], F32)
        sil = S.sbuf("sil", [128, KC * 2], F32)
        bmod = S.sbuf("bmod", [128, 48], F32)
        n1gs = S.sbuf("n1gs", [128, KC], F32)
        gcs = S.sbuf("gcs", [128, 4], F32)
        modT = S.sbuf("modT", [128, 96], F32)
        A1 = S.sbuf("A1", [128, KC * 2], F32)
        a2s = S.sbuf("a2s", [33, 512], F32)
        gaT = S.sbuf("gaT", [33, 512], F32)
        ps_small = S.psum("ps_small", [128, 2], F32)
        cst = S.sbuf("cst", [128, 2], F32)
        S.op("pool", lambda e: e.memset(cst[:, 0:1], EPS), [], [cst])
        S.op("pool", lambda e: e.memset(cst[:, 1:2], 1.0), [], [cst])

        def rstd_from(ps, n, pool_tag):
            sq_ = S.get(pool_tag)
            S.op("act", lambda e: e.activation(out=sq_[:, 0:n], in_=ps[:, 0:n], func=AF.Sqrt, bias=cst[:, 0:1], scale=1.0),
                 [ps, cst], [sq_])
            r_ = S.get(pool_tag)
            S.op("dve", lambda e: e.reciprocal(out=r_[:, 0:n], in_=sq_[:, 0:n]), [sq_], [r_])
            return r_

        load(S, cm32, cm32[:], cmats, cmats[:, :])
        load(S, cTs, cTs[:], cT, cT[:, :])
        load(S, bmod, bmod[:], b_modT, b_modT[:, :])
        load(S, n1gs, n1gs[:], n1g, n1g[:, :])
        load(S, gcs, gcs[:], gcols, gcols[:, :])
        load(S, a2s, a2s[:], a2aug, a2aug[:, :])
        S.op("dve", lambda e: e.tensor_copy(out=cmb[:], in_=cm32[:]), [cm32], [cmb])
        Pm = cmb[:, 0:128]
        Bones = cmb[:, 128:256]
        Ones = cmb[:, 256:384]
        S.op("act", lambda e: e.activation(out=sil[:], in_=cTs[:], func=AF.Silu), [cTs], [sil])
        S.op("pool", lambda e: e.memset(gaT[32:33, :], 1.0), [], [gaT])

        wm_v = w_mod.t.rearrange("(kc p) f -> p kc f", p=128)
        for sb in range(12):
            st = S.get("stage")
            load(S, st, st[:], w_mod, wm_v[:, :, sb * 512:(sb + 1) * 512])
            for j in range(4):
                fo = sb * 4 + j
                for kc in range(KC):
                    S.op("pe", lambda e, kc=kc, j=j, st=st: e.matmul(
                        ps_small[:], st[:, kc, j * 128:(j + 1) * 128], sil[:, kc * 2:kc * 2 + 2],
                        start=(kc == 0), stop=(kc == KC - 1)), [st, sil], [ps_small])
                S.op("dve", lambda e, fo=fo: e.tensor_scalar(
                    out=modT[:, fo * 2:fo * 2 + 2], in0=ps_small[:], scalar1=bmod[:, fo:fo + 1], scalar2=None,
                    op0=ALU.add), [ps_small, bmod], [modT])
        store(S, outs["modT"], outs["modT"][:, :], modT, modT[:])
        for kc in range(KC):
            S.op("dve", lambda e, kc=kc: e.tensor_scalar(
                out=A1[:, kc * 2:kc * 2 + 2], in0=modT[:, (8 + kc) * 2:(8 + kc) * 2 + 2], scalar1=1.0,
                scalar2=n1gs[:, kc:kc + 1], op0=ALU.add, op1=ALU.mult), [modT, n1gs], [A1])

        xv = xT.t.rearrange("(kc p) t -> p kc t", p=128)
        for ti, (t0, n) in enumerate(TT):
            col = 0 if ti < 8 else 1
            st = S.get("stage")
            load(S, st, st[:, :, 0:n], xT, xv[:, :, t0:t0 + n])
            sq = S.get("sq8")
            S.op("act", lambda e: e.activation(out=sq[:, :, 0:n], in_=st[:, :, 0:n], func=AF.Square), [st], [sq])
            ps = S.get("ps")
            for kc in range(KC):
                S.op("pe", lambda e, kc=kc: e.matmul(ps[:, 0:n], Ones, sq[:, kc, 0:n], start=(kc == 0),
                                                      stop=(kc == KC - 1)), [cmb, sq], [ps])
            rstd = rstd_from(ps, n, "f32a")
            for kc in range(KC):
                tmp = S.get("f32b")
                S.op("dve", lambda e, kc=kc: e.scalar_tensor_tensor(
                    out=tmp[:, 0:n], in0=st[:, kc, 0:n], scalar=A1[:, kc * 2 + col:kc * 2 + col + 1],
                    in1=rstd[:, 0:n], op0=ALU.mult, op1=ALU.mult), [st, A1, rstd], [tmp])
                S.op("act", lambda e, kc=kc: e.activation(
                    out=hT[:, kc, t0:t0 + n], in_=tmp[:, 0:n], func=AF.Identity,
                    bias=modT[:, kc * 2 + col:kc * 2 + col + 1], scale=1.0), [tmp, modT], [hT])

        wv = w_in.t.rearrange("(kc p) c -> p kc c", p=128)

        def load_w(c0, ncol):
            st = S.get("stage")
            load(S, st, st[:, :, 0:ncol], w_in, wv[:, :, c0:c0 + ncol])
            wb = S.get("wb")
            for kc in range(KC):
                en = "pool" if kc % 2 == 0 else "dve"
                S.op(en, lambda e, kc=kc: e.tensor_copy(out=wb[:, kc, 0:ncol], in_=st[:, kc, 0:ncol]), [st], [wb])
            return wb

        def proj_fm(wb, j, t0, n, m=128):
            ps = S.get("ps")
            for kc in range(KC):
                S.op("pe", lambda e, kc=kc: e.matmul(ps[0:m, 0:n], wb[:, kc, j * 128:j * 128 + m], hT[:, kc, t0:t0 + n],
                                                      start=(kc == 0), stop=(kc == KC - 1)), [wb, hT], [ps])
            return ps

        def headnorm(ps, n, gi):
            zs = S.get("f32a")
            S.op("act", lambda e: e.activation(out=zs[:, 0:n], in_=ps[:, 0:n], func=AF.Copy), [ps], [zs])
            sq = S.get("bfa")
            S.op("act", lambda e: e.activation(out=sq[:, 0:n], in_=ps[:, 0:n], func=AF.Square), [ps], [sq])
            ps2 = S.get("ps")
            S.op("pe", lambda e: e.matmul(ps2[:, 0:n], Bones, sq[:, 0:n], start=True, stop=True), [cmb, sq], [ps2])
            rstd = rstd_from(ps2, n, "f32b")
            qh = S.get("bfa")
            S.op("dve", lambda e: e.scalar_tensor_tensor(out=qh[:, 0:n], in0=zs[:, 0:n], scalar=gcs[:, gi:gi + 1],
                                                         in1=rstd[:, 0:n], op0=ALU.mult, op1=ALU.mult),
                 [zs, gcs, rstd], [qh])
            return qh

        def fm_store(name, j, t0, n, src, m=128):
            o = outs[name]
            store(S, o, o[j * 128:j * 128 + m, t0:t0 + n], src, src[0:m, 0:n])

        def qk_block(c0, name, gi, rope):
            wb = load_w(c0, 512)
            for j in range(4):
                for (t0, n) in TT:
                    ps = proj_fm(wb, j, t0, n)
                    qh = headnorm(ps, n, gi)
                    if rope:
                        rc = S.get("rope")
                        load(S, rc, rc[:, 0:n], ropeC, ropeC[:, t0:t0 + n])
                        rs = S.get("rope")
                        load(S, rs, rs[:, 0:n], ropeS, ropeS[:, t0:t0 + n])
                        ps3 = S.get("ps")
                        S.op("pe", lambda e: e.matmul(ps3[:, 0:n], Pm, qh[:, 0:n], start=True, stop=True), [cmb, qh], [ps3])
                        t1 = S.get("f32a")
                        S.op("pool", lambda e: e.tensor_tensor(out=t1[:, 0:n], in0=qh[:, 0:n], in1=rc[:, 0:n], op=ALU.mult),
                             [qh, rc], [t1])
                        t2 = S.get("f32b")
                        S.op("dve", lambda e: e.tensor_tensor(out=t2[:, 0:n], in0=ps3[:, 0:n], in1=rs[:, 0:n], op=ALU.mult),
                             [ps3, rs], [t2])
                        ob = S.get("bfo")
                        S.op("pool", lambda e: e.tensor_tensor(out=ob[:, 0:n], in0=t1[:, 0:n], in1=t2[:, 0:n], op=ALU.add),
                             [t1, t2], [ob])
                    else:
                        ob = qh
                    fm_store(name, j, t0, n, ob)

        def act_block(c0, ncol, name, func, scale=1.0, jbase=0):
            wb = load_w(c0, ncol)
            for j in range(ncol // 128):
                for (t0, n) in TT:
                    ps = proj_fm(wb, j, t0, n)
                    ob = S.get("bfo")
                    S.op("act", lambda e: e.activation(out=ob[:, 0:n], in_=ps[:, 0:n], func=func, scale=scale), [ps], [ob])
                    fm_store(name, jbase + j, t0, n, ob)
            return wb

        def tm_block(wb, cofs, ncol, name):
            o = outs[name]
            for s in range(NTOK // 128):
                ps = S.get("ps")
                for kc in range(KC):
                    S.op("pe", lambda e, kc=kc: e.matmul(ps[:, 0:ncol], hT[:, kc, s * 128:(s + 1) * 128],
                                                          wb[:, kc, cofs:cofs + ncol], start=(kc == 0), stop=(kc == KC - 1)),
                         [wb, hT], [ps])
                ob = S.get("bfo")
                if s % 2 == 0:
                    S.op("act", lambda e: e.activation(out=ob[:, 0:ncol], in_=ps[:, 0:ncol], func=AF.Copy), [ps], [ob])
                else:
                    S.op("dve", lambda e: e.tensor_copy(out=ob[:, 0:ncol], in_=ps[:, 0:ncol]), [ps], [ob])
                store(S, o, o[s * 128:(s + 1) * 128, 0:ncol], ob, ob[:, 0:ncol])

        qk_block(C_DAQ, "qT_da", 0, True)
        qk_block(C_DAK, "kT_da", 1, True)
        tm_block(load_w(C_DAV, 512), 0, 512, "v_da")
        wb = load_w(C_GQ, 512)
        for j in range(4):
            for (t0, n) in TT:
                ps = proj_fm(wb, j, t0, n)
                ob = S.get("bfo")
                S.op("act", lambda e: e.activation(out=ob[:, 0:n], in_=ps[:, 0:n], func=AF.Copy,
                                                   scale=(0.125 if j < 2 else 1.0)), [ps], [ob])
                fm_store("qT_gl" if j < 2 else "kT_gl", j % 2, t0, n, ob)
        tm_block(wb, 256, 256, "k_gl")
        tm_block(load_w(C_GV, 512), 0, 512, "v_gl")
        act_block(C_GG, 512, "ggT", AF.Silu)
        wb = load_w(C_GA, 32)
        lao = outs["la"]
        for (t0, n) in TT:
            ps = proj_fm(wb, 0, t0, n, m=32)
            S.op("act", lambda e: e.activation(out=gaT[0:32, 0:n], in_=ps[0:32, 0:n], func=AF.Copy), [ps], [gaT])
            for s in range(n // 128):
                ps2 = S.get("ps")
                S.op("pe", lambda e, s=s: e.matmul(ps2[:, :], gaT[0:33, s * 128:(s + 1) * 128], a2s[0:33, :],
                                                    start=True, stop=True), [gaT, a2s], [ps2])
                ex = S.get("f32a")
                S.op("act", lambda e: e.activation(out=ex[:, :], in_=ps2[:, :], func=AF.Exp, scale=-1.0), [ps2], [ex])
                ln = S.get("f32b")
                S.op("act", lambda e: e.activation(out=ln[:, :], in_=ex[:, :], func=AF.Ln, bias=cst[:, 1:2], scale=1.0), [ex, cst], [ln])
                lo = S.get("f32a")
                S.op("dve", lambda e: e.tensor_scalar(out=lo[:, :], in0=ln[:, :], scalar1=-1.0 / 16.0, scalar2=None,
                                                      op0=ALU.mult), [ln], [lo])
                store(S, lao, lao[t0 + s * 128:t0 + (s + 1) * 128, :], lo, lo[:, :])
        qk_block(C_NQ, "qT_na", 2, False)
        qk_block(C_NK, "kT_na", 3, False)
        tm_block(load_w(C_NV, 512), 0, 512, "v_na")
        for g in range(6):
            act_block(C_GATE + g * 512, 512, "gates", AF.Sigmoid, jbase=g * 4)
        S.finish()
    return nc


def rope_tables():
    pos = np.arange(NLAT)
    return pos


def la_consts():
    Pm = np.zeros((128, 128), np.float32)
    for m in range(128):
        w = (m % 64) % 32
        if w < 16:
            Pm[m + 16, m] = -1.0
        else:
            Pm[m - 16, m] = 1.0
    Bones = np.zeros((128, 128), np.float32)
    Bones[:64, :64] = 1.0 / 64
    Bones[64:, 64:] = 1.0 / 64
    Ones = np.full((128, 128), 1.0 / 1024, np.float32)
    return np.concatenate([Pm, Bones, Ones], axis=1)


def rope_cs(qtr):
    tpos = qtr * NLAT + np.arange(NLAT)
    row = (tpos // GRID_W).astype(np.float32)
    colp = (tpos % GRID_W).astype(np.float32)
    freqs = (10000.0 ** (-np.arange(16, dtype=np.float32) / 16)).astype(np.float32)
    C = np.ones((128, NTOK), np.float32)
    Sn = np.zeros((128, NTOK), np.float32)
    for p in range(128):
        u = p % 64
        i = (u % 32) % 16
        ang = (row if u < 32 else colp) * freqs[i]
        C[p, :NLAT] = np.cos(ang.astype(np.float32))
        Sn[p, :NLAT] = np.sin(ang.astype(np.float32))
    return C, Sn


def chunkT(v):
    v = np.asarray(v, np.float32)
    if v.ndim == 1:
        return np.ascontiguousarray(v.reshape(-1, 128).T)
    return np.ascontiguousarray(v.reshape(v.shape[0], -1, 128).transpose(2, 1, 0).reshape(128, -1))


def la_inputs(l, core, xT_core, inp):
    b, qtr = core // 4, core % 4
    cc = np.stack([inp["c"][b], inp["c_ctx"]], 0)
    C, Sn = rope_cs(qtr)
    gc = np.stack([np.tile(inp["da_qn_g"][l], 2) * 0.125, np.tile(inp["da_kn_g"][l], 2),
                   np.tile(inp["na_qn_g"][l], 2) * 0.125, np.tile(inp["na_kn_g"][l], 2)], 1).astype(np.float32)
    a2aug = np.zeros((33, 512), np.float32)
    a2aug[0:16, 0:256] = inp["gla_a2"][l, 0]
    a2aug[16:32, 256:512] = inp["gla_a2"][l, 1]
    a2aug[32, 0:256] = inp["gla_a_b"][l, 0]
    a2aug[32, 256:512] = inp["gla_a_b"][l, 1]
    return {"xT": xT_core, "cT": chunkT(cc), "w_mod": inp["w_mod"][l], "b_modT": chunkT(inp["b_mod"][l]),
            "n1g": chunkT(inp["norm1_g"][l]), "w_in": inp["w_in"][l], "gcols": gc, "ropeC": C, "ropeS": Sn,
            "cmats": la_consts(), "a2aug": a2aug}


_NC = {}


def run_la(l, xTs, inp):
    if "la" not in _NC:
        _NC["la"] = build_la()
    in_maps = [la_inputs(l, c, xTs[c], inp) for c in range(NCORES)]
    res = run_bass_kernel_spmd(_NC["la"], in_maps, core_ids=list(range(NCORES)))
    return res.results


def make_xT(x, ctx):
    xs = []
    for c in range(NCORES):
        b, qtr = c // 4, c % 4
        xs.append(np.ascontiguousarray(
            np.concatenate([x[b, qtr * NLAT:(qtr + 1) * NLAT], ctx[b]], 0).T.astype(np.float32)))
    return xs


NKT = 130
NPF = 96
NHALO = 38 * 128
NA_NTOK = NHALO + NCTX


def barrier(S):
    allv = dict(S.pending)
    for k in ("pe", "act", "dve", "pool"):
        if S.cnt[k]:
            allv[k] = S.cnt[k]
    for en in ("pe", "act", "dve", "pool", "sp"):
        d = {k: v for k, v in allv.items() if k != en or en == "sp"}
        seen = S.seen[en]
        for k, v in d.items():
            if seen.get(k, 0) < v:
                S.eng[en].wait_ge(S.sem[k], v)
                seen[k] = v


def build_lb():
    nc = bass.Bass("TRN2", target_bir_lowering=False)
    es = contextlib.ExitStack()
    with es:
        S = Sched(nc, es)
        di = lambda n, s, d=F32: S.dram(n, s, d, "ExternalInput")
        xT = di("xT", [D, NTOK])
        modT_d = di("modT", [128, 96])
        n2g = di("n2g", [128, KC])
        qT_da = di("qT_da", [512, NTOK], BF16)
        kT_da = di("kT_da", [512, NKT * 128], BF16)
        v_da = di("v_da", [4, 128, NKT * 128], BF16)
        lamtab = di("lamtab", [128, 256])
        lconst = di("lconst", [128, 2])
        gvec = di("gvec", [128, 2])
        qT_gl = di("qT_gl", [256, NTOK], BF16)
        kT_gl = di("kT_gl", [256, NTOK], BF16)
        k_gl = di("k_gl", [NTOK, 256], BF16)
        v_gl = di("v_gl", [NTOK, 512], BF16)
        la_d = di("la", [NTOK, 512])
        ggT = di("ggT", [512, NTOK], BF16)
        pf_la = di("pf_la", [2, NPF * 128, 256])
        pf_k = di("pf_k", [2, NPF * 128, 256], BF16)
        pf_v = di("pf_v", [2, NPF * 128, 512], BF16)
        trim = di("trim", [128, 6 * 128])
        qT_na = di("qT_na", [512, NTOK], BF16)
        kT_na = di("kT_na", [512, NA_NTOK], BF16)
        v_na = di("v_na", [NA_NTOK, 512], BF16)
        nbias = di("nbias", [128, 8 * 7 * 128])
        nmask = di("nmask", [128, 5 * 7 * 128])
        gates = di("gates", [3072, NTOK], BF16)
        w_br = [di("w_br%d" % i, [512, D]) for i in range(3)]
        w_out = di("w_out", [D, D])
        w_ff1 = di("w_ff1", [D, 4 * D])
        w_ff2 = di("w_ff2", [4 * D, D])
        xo = S.dram("xo", [D, NTOK], F32, "ExternalOutput")
        ys = [S.dram("y%d" % i, [512, NTOK], BF16, "Internal") for i in range(3)]
        wbr_b = [S.dram("wbrb%d" % i, [128, 4 * D], BF16, "Internal") for i in range(3)]
        wout_b = S.dram("woutb", [128, KC * D], BF16, "Internal")
        w1_b = S.dram("w1b", [8, 128, KC * 512], BF16, "Internal")
        w2_b = S.dram("w2b", [8, 128, 32 * 128], BF16, "Internal")

        cst = S.sbuf("cst", [128, 2], F32)
        S.op("pool", lambda e: e.memset(cst[:, 0:1], EPS), [], [cst])
        S.op("pool", lambda e: e.memset(cst[:, 1:2], 1.0), [], [cst])
        onesb = S.sbuf("onesb", [128, 128], BF16)
        S.op("pool", lambda e: e.memset(onesb[:], 1.0), [], [onesb])
        ones32 = S.sbuf("ones32", [128, 128], F32)
        S.op("pool", lambda e: e.memset(ones32[:], 1.0), [], [ones32])
        o128 = S.sbuf("o128", [128, 128], BF16)
        S.op("pool", lambda e: e.memset(o128[:], 1.0 / 128), [], [o128])
        o1024 = S.sbuf("o1024", [128, 128], BF16)
        S.op("pool", lambda e: e.memset(o1024[:], 1.0 / 1024), [], [o1024])
        modT = S.sbuf("modT", [128, 96], F32)
        load(S, modT, modT[:], modT_d, modT_d[:, :])
        n2gs = S.sbuf("n2gs", [128, KC], F32)
        load(S, n2gs, n2gs[:], n2g, n2g[:, :])
        lt = S.sbuf("lt", [128, 256], F32)
        load(S, lt, lt[:], lamtab, lamtab[:, :])
        lcs = S.sbuf("lcs", [128, 2], F32)
        load(S, lcs, lcs[:], lconst, lconst[:, :])
        gv = S.sbuf("gv", [128, 2], F32)
        load(S, gv, gv[:], gvec, gvec[:, :])
        A2 = S.sbuf("A2", [128, KC * 2], F32)
        for kc in range(KC):
            S.op("dve", lambda e, kc=kc: e.tensor_scalar(
                out=A2[:, kc * 2:kc * 2 + 2], in0=modT[:, (32 + kc) * 2:(32 + kc) * 2 + 2], scalar1=1.0,
                scalar2=n2gs[:, kc:kc + 1], op0=ALU.add, op1=ALU.mult), [modT, n2gs], [A2])
        lw = S.sbuf("lw", [128, 128], F32)
        lsum = S.sbuf("lsum", [128, 8], F32)
        S.op("dve", lambda e: e.tensor_tensor(out=lw[:, 0:64], in0=lt[:, 0:64], in1=lt[:, 64:128], op=ALU.mult), [lt], [lw])
        S.op("dve", lambda e: e.tensor_tensor(out=lw[:, 64:128], in0=lt[:, 128:192], in1=lt[:, 192:256], op=ALU.mult), [lt, lw], [lw])
        S.op("dve", lambda e: e.reduce_sum(out=lsum[:, 0:1], in_=lw[:, 0:64], axis=AX.X), [lw], [lsum])
        S.op("dve", lambda e: e.reduce_sum(out=lsum[:, 1:2], in_=lw[:, 64:128], axis=AX.X), [lw, lsum], [lsum])
        S.op("act", lambda e: e.activation(out=lsum[:, 2:4], in_=lsum[:, 0:2], func=AF.Exp), [lsum], [lsum])
        S.op("dve", lambda e: e.tensor_tensor(out=lsum[:, 4:5], in0=lsum[:, 3:4], in1=lsum[:, 2:3], op=ALU.subtract), [lsum], [lsum])
        S.op("dve", lambda e: e.tensor_tensor(out=lsum[:, 5:6], in0=lsum[:, 4:5], in1=lcs[:, 0:1], op=ALU.subtract), [lsum, lcs], [lsum])
        S.op("dve", lambda e: e.tensor_tensor(out=lsum[:, 6:7], in0=gv[:, 0:1], in1=lcs[:, 1:2], op=ALU.mult), [lsum, gv, lcs], [lsum])
        neglam = lsum[:, 5:6]
        gsub = lsum[:, 6:7]

        def rstd_from(ps, n, pool_tag):
            sq_ = S.get(pool_tag)
            S.op("act", lambda e: e.activation(out=sq_[:, 0:n], in_=ps[:, 0:n], func=AF.Sqrt, bias=cst[:, 0:1], scale=1.0),
                 [ps, cst], [sq_])
            r_ = S.get(pool_tag)
            S.op("dve", lambda e: e.reciprocal(out=r_[:, 0:n], in_=sq_[:, 0:n]), [sq_], [r_])
            return r_

        with contextlib.ExitStack() as es2:
            S.es = es2
            S.pool("stage", 2, [128, KC, 512], F32)
            S.pool("wb", 2, [128, KC, 512], BF16)

            def cast_block(src_t, src_ap, nk, dsts):
                st = S.get("stage")
                load(S, st, st[:, 0:nk, :], src_t, src_ap)
                wb = S.get("wb")
                for kc in range(nk):
                    en = "pool" if kc % 2 == 0 else "dve"
                    S.op(en, lambda e, kc=kc: e.tensor_copy(out=wb[:, kc, :], in_=st[:, kc, :]), [st], [wb])
                for (dt_, dap, sap) in dsts(wb):
                    store(S, dt_, dap, wb, sap)

            for i in range(3):
                v = w_br[i].t.rearrange("(kc p) c -> p kc c", p=128)
                dv = wbr_b[i].t.rearrange("p (kc c) -> p kc c", kc=4)
                for hb in range(2):
                    cast_block(w_br[i], v[:, :, hb * 512:(hb + 1) * 512], 4,
                               lambda wb, dv=dv, hb=hb, i=i: [(wbr_b[i], dv[:, :, hb * 512:(hb + 1) * 512], wb[:, 0:4, :])])
            v = w_out.t.rearrange("(kc p) c -> p kc c", p=128)
            dv = wout_b.t.rearrange("p (kc c) -> p kc c", kc=KC)
            for hb in range(2):
                cast_block(w_out, v[:, :, hb * 512:(hb + 1) * 512], KC,
                           lambda wb, dv=dv, hb=hb: [(wout_b, dv[:, :, hb * 512:(hb + 1) * 512], wb[:, :, :])])
            v = w_ff1.t.rearrange("(kc p) c -> p kc c", p=128)
            for fb in range(8):
                dv = w1_b.t[fb].rearrange("p (kc c) -> p kc c", kc=KC)
                cast_block(w_ff1, v[:, :, fb * 512:(fb + 1) * 512], KC,
                           lambda wb, dv=dv: [(w1_b, dv, wb[:, :, :])])
            v = w_ff2.t.rearrange("(kc p) c -> p kc c", p=128)
            for kg in range(4):
                for hb in range(2):
                    def dsts(wb, kg=kg, hb=hb):
                        r = []
                        for j in range(4):
                            fo = hb * 4 + j
                            dv = w2_b.t[fo].rearrange("p (kc c) -> p kc c", kc=32)
                            r.append((w2_b, dv[:, kg * 8:(kg + 1) * 8, :], wb[:, :, j * 128:(j + 1) * 128]))
                        return r
                    cast_block(w_ff2, v[:, kg * 8:(kg + 1) * 8, hb * 512:(hb + 1) * 512], KC, dsts)
            barrier(S)

        with contextlib.ExitStack() as es2:
            S.es = es2
            S.pool("kT", 2, [128, NKT * 128], BF16)
            S.pool("V", 2, [128, NKT * 128], BF16)
            S.pool("q", 2, [128, NTOK], BF16)
            S.pool("pT", 6, [128, 512], BF16)
            S.pool("f32a", 4, [128, 512], F32)
            S.pool("f32b", 4, [128, 512], F32)
            S.pool("osub", 4, [128, 512], F32)
            S.pool("saccd", 4, [128, 512], F32)
            S.pool("saccp", 4, [128, 512], F32)
            S.pool("bfa", 2, [128, 512], BF16)
            S.pool("bfo", 3, [128, 512], BF16)
            S.pool("st", 4, [128, 512], F32, space="psum")
            S.pool("acc", 4, [128, 512], F32, space="psum")
            for h in range(4):
                kT = S.get("kT")
                load(S, kT, kT[:], kT_da, kT_da[h * 128:(h + 1) * 128, :])
                V = S.get("V")
                load(S, V, V[:], v_da, v_da[h])
                q = S.get("q")
                load(S, q, q[:], qT_da, qT_da[h * 128:(h + 1) * 128, :])
                for ti, (t0, n) in enumerate(TT):
                    keys = list(range(NKT)) if ti < 8 else [128, 129]
                    accO = [S.get("acc"), S.get("acc")]
                    accS0 = S.get("acc")
                    sacc_d = [S.get("saccd"), S.get("saccd")]
                    sacc_p = [S.get("saccp"), S.get("saccp")]
                    sts = {}

                    def qk(j):
                        pair = []
                        for sub in range(2):
                            p0 = sub * 64
                            st = S.get("st")
                            S.op("pe", lambda e: e.matmul(st[:, 0:n], kT[p0:p0 + 64, j * 128:(j + 1) * 128],
                                                          q[p0:p0 + 64, t0:t0 + n], start=True, stop=True), [kT, q], [st])
                            pair.append(st)
                        sts[j] = pair

                    qk(keys[0])
                    for idx, j in enumerate(keys):
                        pair = sts.pop(j)
                        pTs = []
                        for sub in range(2):
                            pT = S.get("pT")
                            S.op("act", lambda e: e.activation(out=pT[:, 0:n], in_=pair[sub][:, 0:n], func=AF.Exp), [pair[sub]], [pT])
                            pTs.append(pT)
                        if idx + 1 < len(keys):
                            qk(keys[idx + 1])
                        first, last = idx == 0, idx == len(keys) - 1
                        for sub in range(2):
                            pT = pTs[sub]
                            S.op("pe", lambda e: e.matmul(accO[sub][:, 0:n], V[:, j * 128:(j + 1) * 128], pT[:, 0:n],
                                                          start=first, stop=last), [V, pT], [accO[sub]])
                            if sub == 0:
                                S.op("pe", lambda e: e.matmul(accS0[:, 0:n], onesb[:], pT[:, 0:n], start=first, stop=last),
                                     [onesb, pT], [accS0])
                                continue
                            en_, sa_ = ("dve", sacc_d[sub]) if idx % 2 == 0 else ("pool", sacc_p[sub])
                            if idx < 2:
                                S.op(en_, lambda e: e.tensor_copy(out=sa_[:, 0:n], in_=pT[:, 0:n]), [pT], [sa_])
                            else:
                                S.op(en_, lambda e: e.tensor_tensor(out=sa_[:, 0:n], in0=sa_[:, 0:n], in1=pT[:, 0:n], op=ALU.add),
                                     [sa_, pT], [sa_])
                    osub = []
                    for sub in range(2):
                        if sub == 0:
                            accS = accS0
                        else:
                            accS = S.get("acc")
                            S.op("pe", lambda e: e.matmul(accS[:, 0:n], ones32[:], sacc_d[sub][:, 0:n], start=True, stop=False),
                                 [ones32, sacc_d[sub]], [accS])
                            S.op("pe", lambda e: e.matmul(accS[:, 0:n], ones32[:], sacc_p[sub][:, 0:n], start=False, stop=True),
                                 [ones32, sacc_p[sub]], [accS])
                        rec = S.get("f32a")
                        S.op("dve", lambda e: e.reciprocal(out=rec[:, 0:n], in_=accS[:, 0:n]), [accS], [rec])
                        os_ = S.get("osub")
                        S.op("dve", lambda e: e.tensor_tensor(out=os_[:, 0:n], in0=accO[sub][:, 0:n], in1=rec[:, 0:n], op=ALU.mult),
                             [accO[sub], rec], [os_])
                        osub.append(os_)
                    o = S.get("f32b")
                    S.op("dve", lambda e: e.scalar_tensor_tensor(out=o[:, 0:n], in0=osub[1][:, 0:n], scalar=neglam,
                                                                 in1=osub[0][:, 0:n], op0=ALU.mult, op1=ALU.add),
                         [osub[0], osub[1], lsum], [o])
                    sq = S.get("bfa")
                    S.op("act", lambda e: e.activation(out=sq[:, 0:n], in_=o[:, 0:n], func=AF.Square), [o], [sq])
                    ms = S.get("acc")
                    S.op("pe", lambda e: e.matmul(ms[:, 0:n], o128[:], sq[:, 0:n], start=True, stop=True), [o128, sq], [ms])
                    rstd = rstd_from(ms, n, "f32a")
                    y = S.get("bfo")
                    S.op("dve", lambda e: e.scalar_tensor_tensor(out=y[:, 0:n], in0=o[:, 0:n], scalar=gsub, in1=rstd[:, 0:n],
                                                                 op0=ALU.mult, op1=ALU.mult), [o, lsum, rstd], [y])
                    store(S, ys[0], ys[0][h * 128:(h + 1) * 128, t0:t0 + n], y, y[:, 0:n])
            barrier(S)

        with contextlib.ExitStack() as es2:
            S.es = es2
            tri32 = S.sbuf("tri32", [128, 768], F32)
            load(S, tri32, tri32[:], trim, trim[:, :])
            onec = S.sbuf("onec", [128, 1], F32)
            S.op("pool", lambda e: e.memset(onec[:], 1.0), [], [onec])
            og = [S.sbuf("og%d" % h, [128, NTOK], F32) for h in range(4)]
            Sst = [S.sbuf("Sst%d" % p, [128, 256], F32) for p in range(2)]
            Sb = [S.sbuf("Sb%d" % p, [128, 256], BF16) for p in range(2)]
            S.pool("la", 3, [128, 256], F32)
            S.pool("k", 3, [128, 256], BF16)
            S.pool("v", 3, [128, 512], BF16)
            S.pool("qk", 3, [128, 4, 128], BF16)
            S.pool("ekd", 2, [128, 256], F32)
            S.pool("kd", 2, [128, 256], BF16)
            S.pool("eqk", 4, [128, 128], F32)
            S.pool("qeke", 4, [128, 128], BF16)
            S.pool("aTm", 3, [128, 128], BF16)
            S.pool("dcol", 4, [128, 1], F32)
            S.pool("f32a", 4, [128, 512], F32)
            S.pool("bfa", 2, [128, 512], BF16)
            S.pool("bfo", 3, [128, 512], BF16)
            S.pool("gg", 2, [128, 512], BF16)
            S.pool("pex", 1, [128, 512], F32, space="psum")
            S.pool("pcum", 2, [128, 128], F32, space="psum")
            S.pool("paT", 2, [128, 128], F32, space="psum")
            S.pool("po", 2, [128, 128], F32, space="psum")
            S.pool("pds", 1, [128, 256], F32, space="psum")

            def gla_tile(d, la_t, la_ap, k_t, k_ap, v_t, v_ap, full, tcol):
                b = d * 384
                Tstr, Tincl, Mask = tri32[:, b:b + 128], tri32[:, b + 128:b + 256], tri32[:, b + 256:b + 384]
                la = S.get("la")
                load(S, la, la[:], la_t, la_ap)
                kk = S.get("k")
                load(S, kk, kk[:], k_t, k_ap)
                vv = S.get("v")
                load(S, vv, vv[:], v_t, v_ap)
                pex = S.get("pex")
                S.op("pe", lambda e: e.matmul(pex[:, 0:256], Tstr, la[:], start=True, stop=True), [tri32, la], [pex])
                ekd = S.get("ekd")
                S.op("act", lambda e: e.activation(out=ekd[:], in_=pex[:, 0:256], func=AF.Exp), [pex], [ekd])
                kd = S.get("kd")
                S.op("dve", lambda e: e.tensor_tensor(out=kd[:], in0=kk[:], in1=ekd[:], op=ALU.mult), [kk, ekd], [kd])
                if full:
                    qk = S.get("qk")
                    load(S, qk, qk[:, 0:2, :], qT_gl, qT_gl.t.rearrange("(pr p) t -> p pr t", p=128)[:, :, tcol:tcol + 128])
                    load(S, qk, qk[:, 2:4, :], kT_gl, kT_gl.t.rearrange("(pr p) t -> p pr t", p=128)[:, :, tcol:tcol + 128])
                for pr in range(2):
                    dcol = S.get("dcol")
                    if full:
                        pc = S.get("pcum")
                        S.op("pe", lambda e: e.matmul(pc[:], la[:, pr * 128:(pr + 1) * 128], Tincl, start=True, stop=True),
                             [la, tri32], [pc])
                        eq = S.get("eqk")
                        S.op("act", lambda e: e.activation(out=eq[:], in_=pc[:], func=AF.Exp), [pc], [eq])
                        ek = S.get("eqk")
                        S.op("act", lambda e: e.activation(out=ek[:], in_=pc[:], func=AF.Exp, scale=-1.0), [pc], [ek])
                        qe = S.get("qeke")
                        S.op("dve", lambda e: e.tensor_tensor(out=qe[:], in0=qk[:, pr, :], in1=eq[:], op=ALU.mult), [qk, eq], [qe])
                        ke = S.get("qeke")
                        S.op("pool", lambda e: e.tensor_tensor(out=ke[:], in0=qk[:, 2 + pr, :], in1=ek[:], op=ALU.mult), [qk, ek], [ke])
                        lastc = 127 if d == 0 else 0
                        S.op("act", lambda e: e.activation(out=dcol[:], in_=eq[:, lastc:lastc + 1], func=AF.Copy), [eq], [dcol])
                        for hl in range(2):
                            h = pr * 2 + hl
                            p0 = hl * 64
                            pa = S.get("paT")
                            S.op("pe", lambda e: e.matmul(pa[:], ke[p0:p0 + 64, :], qe[p0:p0 + 64, :], start=True, stop=True),
                                 [ke, qe], [pa])
                            am = S.get("aTm")
                            S.op("dve", lambda e: e.tensor_tensor(out=am[:], in0=pa[:], in1=Mask, op=ALU.mult), [pa, tri32], [am])
                            po = S.get("po")
                            S.op("pe", lambda e: e.matmul(po[:], vv[:, h * 128:(h + 1) * 128], am[:], start=True, stop=False),
                                 [vv, am], [po])
                            S.op("pe", lambda e: e.matmul(po[:], Sb[pr][p0:p0 + 64, hl * 128:(hl + 1) * 128], qe[p0:p0 + 64, :],
                                                          start=False, stop=True), [Sb[pr], qe], [po])
                            if d == 0:
                                S.op("act", lambda e: e.activation(out=og[h][:, tcol:tcol + 128], in_=po[:], func=AF.Copy),
                                     [po], [og[h]])
                            else:
                                S.op("dve", lambda e: e.tensor_tensor(out=og[h][:, tcol:tcol + 128], in0=po[:],
                                                                      in1=og[h][:, tcol:tcol + 128], op=ALU.add), [po, og[h]], [og[h]])
                    else:
                        pc = S.get("pcum")
                        S.op("pe", lambda e: e.matmul(pc[:, 0:1], la[:, pr * 128:(pr + 1) * 128], onec[:], start=True, stop=True),
                             [la, onec], [pc])
                        S.op("act", lambda e: e.activation(out=dcol[:], in_=pc[:, 0:1], func=AF.Exp), [pc], [dcol])
                    pds = S.get("pds")
                    S.op("pe", lambda e: e.matmul(pds[:], kd[:, pr * 128:(pr + 1) * 128], vv[:, pr * 256:(pr + 1) * 256],
                                                  start=True, stop=True), [kd, vv], [pds])
                    S.op("dve", lambda e: e.scalar_tensor_tensor(out=Sst[pr][:], in0=Sst[pr][:], scalar=dcol[:, 0:1], in1=pds[:],
                                                                 op0=ALU.mult, op1=ALU.add), [Sst[pr], dcol, pds], [Sst[pr]])
                    S.op("pool", lambda e: e.tensor_copy(out=Sb[pr][:], in_=Sst[pr][:]), [Sst[pr]], [Sb[pr]])

            for d in range(2):
                for pr in range(2):
                    S.op("pool", lambda e: e.memset(Sst[pr][:], 0.0), [], [Sst[pr]])
                    S.op("pool", lambda e: e.memset(Sb[pr][:], 0.0), [], [Sb[pr]])
                own = lambda t: (la_d, la_d[t * 128:(t + 1) * 128, d * 256:(d + 1) * 256], k_gl, k_gl[t * 128:(t + 1) * 128, :],
                                 v_gl, v_gl[t * 128:(t + 1) * 128, :])
                pre = lambda t: (pf_la, pf_la[d, t * 128:(t + 1) * 128, :], pf_k, pf_k[d, t * 128:(t + 1) * 128, :],
                                 pf_v, pf_v[d, t * 128:(t + 1) * 128, :])
                order = (lambda r: list(r)) if d == 0 else (lambda r: list(r)[::-1])
                for t in order(range(32, 34)):
                    gla_tile(d, *own(t), True, t * 128)
                for t in order(range(NPF)):
                    gla_tile(d, *pre(t), False, 0)
                for t in order(range(32)):
                    gla_tile(d, *own(t), True, t * 128)
            for h in range(4):
                for (t0, n) in TT:
                    sq = S.get("bfa")
                    S.op("act", lambda e: e.activation(out=sq[:, 0:n], in_=og[h][:, t0:t0 + n], func=AF.Square), [og[h]], [sq])
                    ms = S.get("pex")
                    S.op("pe", lambda e: e.matmul(ms[:, 0:n], o128[:], sq[:, 0:n], start=True, stop=True), [o128, sq], [ms])
                    rstd = rstd_from(ms, n, "f32a")
                    g = S.get("gg")
                    load(S, g, g[:, 0:n], ggT, ggT[h * 128:(h + 1) * 128, t0:t0 + n])
                    y0 = S.get("f32a")
                    S.op("dve", lambda e: e.scalar_tensor_tensor(out=y0[:, 0:n], in0=og[h][:, t0:t0 + n], scalar=gv[:, 1:2],
                                                                 in1=rstd[:, 0:n], op0=ALU.mult, op1=ALU.mult), [og[h], gv, rstd], [y0])
                    y = S.get("bfo")
                    S.op("pool", lambda e: e.tensor_tensor(out=y[:, 0:n], in0=y0[:, 0:n], in1=g[:, 0:n], op=ALU.mult), [y0, g], [y])
                    store(S, ys[1], ys[1][h * 128:(h + 1) * 128, t0:t0 + n], y, y[:, 0:n])
            barrier(S)

        with contextlib.ExitStack() as es2:
            S.es = es2
            kTn = S.sbuf("kTn", [128, 4, NA_NTOK], BF16)
            load(S, kTn, kTn[:], kT_na, kT_na.t.rearrange("(c p) t -> p c t", p=128))
            Vn = S.sbuf("Vn", [128, NA_NTOK // 128, 512], BF16)
            load(S, Vn, Vn[:], v_na, v_na.t.rearrange("(t p) c -> p t c", p=128))
            nb = S.sbuf("nb", [128, 56 * 128], F32)
            load(S, nb, nb[:], nbias, nbias[:, :])
            nm = S.sbuf("nm", [128, 35 * 128], F32)
            load(S, nm, nm[:], nmask, nmask[:, :])
            S.pool("q", 2, [128, NTOK], BF16)
            S.pool("s1", 3, [128, 128], F32)
            S.pool("s2", 3, [128, 128], F32)
            S.pool("pT", 4, [128, 128], BF16)
            S.pool("rec", 2, [128, 128], F32)
            S.pool("yt", 3, [128, 128], BF16)
            S.pool("st", 4, [128, 128], F32, space="psum")
            S.pool("acc", 4, [128, 128], F32, space="psum")
            for c in range(4):
                q = S.get("q")
                load(S, q, q[:], qT_na, qT_na[c * 128:(c + 1) * 128, :])
                for qt in range(34):
                    tcol = qt * 128
                    if qt < 32:
                        units = [(qt + o, o) for o in range(7)] + [(38, None), (39, None)]
                        cls = {0: 0, 1: 1, 30: 3, 31: 4}.get(qt, 2)
                    else:
                        units = [(38, None), (39, None)]
                        cls = 2
                    yt = S.get("yt")
                    for hl in range(2):
                        h = 2 * c + hl
                        p0 = hl * 64
                        accO = S.get("acc")
                        accS = S.get("acc")
                        nsts = {}

                        def nqk(ui_):
                            kt_ = units[ui_][0]
                            st_ = S.get("st")
                            S.op("pe", lambda e: e.matmul(st_[:], kTn[p0:p0 + 64, c, kt_ * 128:(kt_ + 1) * 128],
                                                          q[p0:p0 + 64, tcol:tcol + 128], start=True, stop=True), [kTn, q], [st_])
                            nsts[ui_] = st_

                        NLA = 3
                        for u0 in range(min(NLA, len(units))):
                            nqk(u0)
                        for ui, (kt, o) in enumerate(units):
                            st = nsts.pop(ui)
                            if ui + NLA < len(units):
                                nqk(ui + NLA)
                            pT = S.get("pT")
                            if o is not None:
                                s1 = S.get("s1")
                                bo = (h * 7 + o) * 128
                                S.op("dve", lambda e: e.tensor_tensor(out=s1[:], in0=st[:], in1=nb[:, bo:bo + 128], op=ALU.add),
                                     [st, nb], [s1])
                                s2 = S.get("s2")
                                mo = (cls * 7 + o) * 128
                                S.op("pool", lambda e: e.tensor_tensor(out=s2[:], in0=s1[:], in1=nm[:, mo:mo + 128], op=ALU.add),
                                     [s1, nm], [s2])
                                S.op("act", lambda e: e.activation(out=pT[:], in_=s2[:], func=AF.Exp), [s2], [pT])
                            else:
                                S.op("act", lambda e: e.activation(out=pT[:], in_=st[:], func=AF.Exp), [st], [pT])
                            first, last = ui == 0, ui == len(units) - 1
                            S.op("pe", lambda e: e.matmul(accO[:], Vn[:, kt, c * 128:(c + 1) * 128], pT[:], start=first, stop=last),
                                 [Vn, pT], [accO])
                            S.op("pe", lambda e: e.matmul(accS[:], onesb[:], pT[:], start=first, stop=last), [onesb, pT], [accS])
                        rec = S.get("rec")
                        S.op("dve", lambda e: e.reciprocal(out=rec[p0:p0 + 64, :], in_=accS[p0:p0 + 64, :]), [accS], [rec])
                        S.op("dve", lambda e: e.tensor_tensor(out=yt[p0:p0 + 64, :], in0=accO[p0:p0 + 64, :], in1=rec[p0:p0 + 64, :],
                                                              op=ALU.mult), [accO, rec], [yt])
                    store(S, ys[2], ys[2][c * 128:(c + 1) * 128, tcol:tcol + 128], yt, yt[:])
            barrier(S)

        with contextlib.ExitStack() as es2:
            S.es = es2
            wbr = S.sbuf("wbr", [128, 12, D], BF16)
            for i in range(3):
                load(S, wbr, wbr[:, i * 4:(i + 1) * 4, :], wbr_b[i], wbr_b[i].t.rearrange("p (kc c) -> p kc c", kc=4))
            wo = S.sbuf("wo", [128, KC, D], BF16)
            load(S, wo, wo[:], wout_b, wout_b.t.rearrange("p (kc c) -> p kc c", kc=KC))
            S.pool("x", 1, [128, KC, 512], F32)
            S.pool("y", 1, [128, 12, 512], BF16)
            S.pool("g3", 2, [128, 3, 512], BF16)
            S.pool("m", 1, [128, KC, 512], BF16)
            S.pool("sq8", 1, [128, KC, 512], BF16)
            S.pool("h2", 1, [128, KC, 512], BF16)
            S.pool("a", 1, [128, 32, 512], BF16)
            S.pool("w1", 2, [128, KC, 512], BF16)
            S.pool("w2", 2, [128, 32, 128], BF16)
            S.pool("f32a", 4, [128, 512], F32)
            S.pool("f32b", 4, [128, 512], F32)
            S.pool("xo", 3, [128, 512], F32)
            S.pool("ps", 8, [128, 512], F32, space="psum")
            gv3 = gates.t.rearrange("(br fo p) t -> p br fo t", br=3, fo=8)
            xv = xT.t.rearrange("(kc p) t -> p kc t", p=128)
            for ti, (t0, n) in enumerate(TT):
                col = 0 if ti < 8 else 1
                mcol = lambda ch: modT[:, ch * 2 + col:ch * 2 + col + 1]
                x = S.get("x")
                load(S, x, x[:, :, 0:n], xT, xv[:, :, t0:t0 + n])
                y = S.get("y")
                for br in range(3):
                    load(S, y, y[:, br * 4:(br + 1) * 4, 0:n], ys[br], ys[br].t.rearrange("(kc p) t -> p kc t", p=128)[:, :, t0:t0 + n])
                m = S.get("m")
                for fo in range(8):
                    g3 = S.get("g3")
                    load(S, g3, g3[:, :, 0:n], gates, gv3[:, :, fo, t0:t0 + n])
                    tmps = []
                    for br in range(3):
                        ps = S.get("ps")
                        for kc in range(4):
                            S.op("pe", lambda e, kc=kc: e.matmul(ps[:, 0:n], wbr[:, br * 4 + kc, fo * 128:(fo + 1) * 128],
                                                                  y[:, br * 4 + kc, 0:n], start=(kc == 0), stop=(kc == 3)), [wbr, y], [ps])
                        tb = S.get("f32a")
                        S.op("dve", lambda e: e.tensor_tensor(out=tb[:, 0:n], in0=ps[:, 0:n], in1=g3[:, br, 0:n], op=ALU.mult),
                             [ps, g3], [tb])
                        tmps.append(tb)
                    s01 = S.get("f32b")
                    S.op("pool", lambda e: e.tensor_tensor(out=s01[:, 0:n], in0=tmps[0][:, 0:n], in1=tmps[1][:, 0:n], op=ALU.add),
                         [tmps[0], tmps[1]], [s01])
                    S.op("pool", lambda e: e.tensor_tensor(out=m[:, fo, 0:n], in0=s01[:, 0:n], in1=tmps[2][:, 0:n], op=ALU.add),
                         [s01, tmps[2]], [m])
                for fo in range(8):
                    ps = S.get("ps")
                    for kc in range(KC):
                        S.op("pe", lambda e, kc=kc: e.matmul(ps[:, 0:n], wo[:, kc, fo * 128:(fo + 1) * 128], m[:, kc, 0:n],
                                                              start=(kc == 0), stop=(kc == KC - 1)), [wo, m], [ps])
                    S.op("dve", lambda e: e.scalar_tensor_tensor(out=x[:, fo, 0:n], in0=ps[:, 0:n], scalar=mcol(16 + fo),
                                                                 in1=x[:, fo, 0:n], op0=ALU.mult, op1=ALU.add), [ps, modT, x], [x])
                sq = S.get("sq8")
                S.op("act", lambda e: e.activation(out=sq[:, :, 0:n], in_=x[:, :, 0:n], func=AF.Square), [x], [sq])
                ps = S.get("ps")
                for kc in range(KC):
                    S.op("pe", lambda e, kc=kc: e.matmul(ps[:, 0:n], o1024[:], sq[:, kc, 0:n], start=(kc == 0), stop=(kc == KC - 1)),
                         [o1024, sq], [ps])
                rstd = rstd_from(ps, n, "f32a")
                h2 = S.get("h2")
                for kc in range(KC):
                    tmp = S.get("f32b")
                    S.op("dve", lambda e, kc=kc: e.scalar_tensor_tensor(
                        out=tmp[:, 0:n], in0=x[:, kc, 0:n], scalar=A2[:, kc * 2 + col:kc * 2 + col + 1], in1=rstd[:, 0:n],
                        op0=ALU.mult, op1=ALU.mult), [x, A2, rstd], [tmp])
                    S.op("act", lambda e, kc=kc: e.activation(out=h2[:, kc, 0:n], in_=tmp[:, 0:n], func=AF.Identity,
                                                              bias=mcol(24 + kc), scale=1.0), [tmp, modT], [h2])
                a = S.get("a")
                for fb in range(8):
                    w1 = S.get("w1")
                    load(S, w1, w1[:], w1_b, w1_b.t[fb].rearrange("p (kc c) -> p kc c", kc=KC))
                    for j in range(4):
                        f = fb * 4 + j
                        ps = S.get("ps")
                        for kc in range(KC):
                            S.op("pe", lambda e, kc=kc: e.matmul(ps[:, 0:n], w1[:, kc, j * 128:(j + 1) * 128], h2[:, kc, 0:n],
                                                                  start=(kc == 0), stop=(kc == KC - 1)), [w1, h2], [ps])
                        r = S.get("f32a")
                        S.op("act", lambda e: e.activation(out=r[:, 0:n], in_=ps[:, 0:n], func=AF.Relu), [ps], [r])
                        S.op("pool", lambda e: e.tensor_tensor(out=a[:, f, 0:n], in0=r[:, 0:n], in1=r[:, 0:n], op=ALU.mult), [r], [a])
                for fo in range(8):
                    w2 = S.get("w2")
                    load(S, w2, w2[:], w2_b, w2_b.t[fo].rearrange("p (kc c) -> p kc c", kc=32))
                    ps = S.get("ps")
                    for f in range(32):
                        S.op("pe", lambda e, f=f: e.matmul(ps[:, 0:n], w2[:, f, :], a[:, f, 0:n], start=(f == 0), stop=(f == 31)),
                             [w2, a], [ps])
                    xt = S.get("xo")
                    S.op("dve", lambda e: e.scalar_tensor_tensor(out=xt[:, 0:n], in0=ps[:, 0:n], scalar=mcol(40 + fo),
                                                                 in1=x[:, fo, 0:n], op0=ALU.mult, op1=ALU.add), [ps, modT, x], [xt])
                    store(S, xo, xo[fo * 128:(fo + 1) * 128, t0:t0 + n], xt, xt[:, 0:n])
            S.finish()
            barrier(S)
        S.es = es
    return nc


def tri_consts():
    i = np.arange(128)
    sp, s = i[:, None], i[None, :]
    mats = [(sp > s), (sp <= s), (sp <= s), (sp < s), (sp >= s), (sp >= s)]
    return np.concatenate([m.astype(np.float32) for m in mats], axis=1)


def na_bias_table(rpb):
    ka = np.arange(128)
    a, kc = ka // 64, ka % 64
    bq, cq = ka // 64, ka % 64
    out = np.zeros((128, 8, 7, 128), np.float32)
    for o in range(7):
        rel_row = (-6 + 2 * o + a)[:, None] - bq[None, :] + 7
        rel_col = kc[:, None] - cq[None, :] + 15
        ok = (rel_row >= 0) & (rel_row <= 14) & (rel_col >= 0) & (rel_col <= 30)
        rr = np.clip(rel_row, 0, 14)
        rc = np.clip(rel_col, 0, 30)
        for h in range(8):
            out[:, h, o, :] = np.where(ok, rpb[h][rr, rc], 0.0)
    return out.reshape(128, -1)


def na_mask_table(qtr):
    ka = np.arange(128)
    a, kc = ka // 64, ka % 64
    bq, cq = ka // 64, ka % 64
    out = np.zeros((128, 5, 7, 128), np.float32)
    for cls, qt in enumerate((0, 1, 2, 30, 31)):
        Rq = qtr * 64 + 2 * qt
        R = Rq + bq
        rs = np.clip(R - 4, 0, 256 - 8)
        cs = np.clip(cq - 8, 0, GRID_W - 16)
        for o in range(7):
            kr = (Rq - 6 + 2 * o + a)[:, None]
            ok = (kr >= rs[None, :]) & (kr < rs[None, :] + 8) & (kc[:, None] >= cs[None, :]) & (kc[:, None] < cs[None, :] + 16)
            out[:, cls, o, :] = np.where(ok, 0.0, -30000.0)
    return out.reshape(128, -1)


def lb_inputs(l, core, xT_core, ra, inp):
    b, qtr = core // 4, core % 4
    grp = [ra[4 * b + j] for j in range(4)]
    me = ra[core]
    bf = ml_dtypes.bfloat16
    kT_da = np.concatenate([np.asarray(g["kT_da"])[:, :NLAT] for g in grp] + [np.asarray(me["kT_da"])[:, NLAT:]], axis=1)
    v_all = np.concatenate([np.asarray(g["v_da"])[:NLAT] for g in grp] + [np.asarray(me["v_da"])[NLAT:]], axis=0)
    v_da = np.ascontiguousarray(v_all.reshape(NKT, 128, 4, 128).transpose(2, 1, 0, 3).reshape(4, 128, NKT * 128))
    lam_init = 0.8 - 0.6 * math.exp(-0.3 * l)
    npf = NPF * 128
    pf_la = np.zeros((2, npf, 256), np.float32)
    pf_k = np.zeros((2, npf, 256), bf)
    pf_v = np.zeros((2, npf, 512), bf)
    nb_ = qtr
    if nb_:
        pf_la[0, :nb_ * NLAT] = np.concatenate([np.asarray(grp[j]["la"])[:NLAT, 0:256] for j in range(qtr)], 0)
        pf_k[0, :nb_ * NLAT] = np.concatenate([np.asarray(grp[j]["k_gl"])[:NLAT] for j in range(qtr)], 0)
        pf_v[0, :nb_ * NLAT] = np.concatenate([np.asarray(grp[j]["v_gl"])[:NLAT] for j in range(qtr)], 0)
    na_ = 3 - qtr
    if na_:
        pf_la[1, npf - na_ * NLAT:] = np.concatenate([np.asarray(grp[j]["la"])[:NLAT, 256:512] for j in range(qtr + 1, 4)], 0)
        pf_k[1, npf - na_ * NLAT:] = np.concatenate([np.asarray(grp[j]["k_gl"])[:NLAT] for j in range(qtr + 1, 4)], 0)
        pf_v[1, npf - na_ * NLAT:] = np.concatenate([np.asarray(grp[j]["v_gl"])[:NLAT] for j in range(qtr + 1, 4)], 0)
    kn_all = np.concatenate([np.asarray(g["kT_na"])[:, :NLAT] for g in grp], axis=1)
    vn_all = np.concatenate([np.asarray(g["v_na"])[:NLAT] for g in grp], axis=0)
    lo = (qtr * 64 - 6) * GRID_W
    hi = lo + NHALO
    kT_na = np.zeros((512, NA_NTOK), bf)
    v_na = np.zeros((NA_NTOK, 512), bf)
    s0, s1 = max(lo, 0), min(hi, SEQ)
    kT_na[:, s0 - lo:s1 - lo] = kn_all[:, s0:s1]
    v_na[s0 - lo:s1 - lo] = vn_all[s0:s1]
    kT_na[:, NHALO:] = np.asarray(me["kT_na"])[:, NLAT:]
    v_na[NHALO:] = np.asarray(me["v_na"])[NLAT:]
    return {
        "xT": xT_core, "modT": np.asarray(me["modT"]), "n2g": chunkT(inp["norm2_g"][l]),
        "qT_da": np.asarray(me["qT_da"]), "kT_da": np.ascontiguousarray(kT_da), "v_da": v_da,
        "lamtab": np.ascontiguousarray(np.tile(inp["da_lambda"][l].reshape(1, 256), (128, 1)).astype(np.float32)),
        "lconst": np.tile(np.array([[lam_init, 1.0 - lam_init]], np.float32), (128, 1)),
        "gvec": np.stack([inp["da_subln_g"][l], inp["gla_gn_g"][l]], 1).astype(np.float32),
        "qT_gl": np.asarray(me["qT_gl"]), "kT_gl": np.asarray(me["kT_gl"]), "k_gl": np.asarray(me["k_gl"]),
        "v_gl": np.asarray(me["v_gl"]), "la": np.asarray(me["la"]), "ggT": np.asarray(me["ggT"]),
        "pf_la": pf_la, "pf_k": pf_k, "pf_v": pf_v, "trim": tri_consts(),
        "qT_na": np.asarray(me["qT_na"]), "kT_na": kT_na, "v_na": v_na,
        "nbias": na_bias_table(np.asarray(inp["na_rpb"][l], np.float32)), "nmask": na_mask_table(qtr),
        "gates": np.asarray(me["gates"]),
        "w_br0": inp["w_br_da"][l], "w_br1": inp["w_br_gla"][l], "w_br2": inp["w_br_na"][l],
        "w_out": inp["w_out"][l], "w_ff1": inp["w_ff1"][l], "w_ff2": inp["w_ff2"][l],
    }


def run_lb(l, xTs, ra, inp):
    if "lb" not in _NC:
        _NC["lb"] = build_lb()
    in_maps = [lb_inputs(l, c, xTs[c], ra, inp) for c in range(NCORES)]
    res = run_bass_kernel_spmd(_NC["lb"], in_maps, core_ids=list(range(NCORES)))
    return res.results


def kernel(**inputs):
    inp = {k: np.asarray(v) for k, v in inputs.items()}
    xTs = make_xT(inp["x"], inp["ctx"])
    for l in range(2):
        ra = run_la(l, xTs, inp)
        rb = run_lb(l, xTs, ra, inp)
        xTs = [np.ascontiguousarray(np.asarray(rb[c]["xo"], dtype=np.float32)) for c in range(NCORES)]
    out = np.empty((2, SEQ, D), np.float32)
    for c in range(NCORES):
        b, qtr = c // 4, c % 4
        out[b, qtr * NLAT:(qtr + 1) * NLAT] = xTs[c][:, :NLAT].T
    return out
```
